# Optimizing a Trainium2 kernel written in Bass

```python
import jax, jax.numpy as jnp
from jax import lax
import numpy as np

D_MODEL = 1024
BATCH = 16
SEQ = 2048
DEPTH = 4

N_MIXERS = 2
N_Q_HEADS = 16
N_KV_HEADS = 4
HEAD_DIM = D_MODEL // N_Q_HEADS
GQA_GROUP = N_Q_HEADS // N_KV_HEADS
ATTN_WIDTH = N_Q_HEADS * HEAD_DIM
KV_WIDTH = N_KV_HEADS * HEAD_DIM
WINDOW = 128
BLOCK = 128
ROPE_THETA = 10000.0
CONV_WIDTH = 3
CONV_DIM = D_MODEL
A_IN_COLS = ATTN_WIDTH + 2 * KV_WIDTH + ATTN_WIDTH
B_IN_COLS = 4 * CONV_DIM
N_ATTN_LAYERS = (DEPTH + 1) // 2
N_CONV_LAYERS = DEPTH // 2
EPS = 1e-6
MASK_VALUE = -1e30

kernel_name = "hybrid_swa_sink_shortconv_encoder"


def rmsnorm(x, g):
    xf = x.astype(jnp.float32)
    y = xf * lax.rsqrt(jnp.mean(xf * xf, axis=-1, keepdims=True) + EPS)
    return (y * g.astype(jnp.float32)).astype(x.dtype)


def rope_tables(seq_len):
    inv_freq = ROPE_THETA ** (-jnp.arange(0, HEAD_DIM, 2, dtype=jnp.float32) / HEAD_DIM)
    ang = jnp.arange(seq_len, dtype=jnp.float32)[:, None] * inv_freq[None, :]
    return jnp.cos(ang)[:, None, :], jnp.sin(ang)[:, None, :]


def apply_rope(x, cos, sin):
    x1, x2 = jnp.split(x, 2, axis=-1)
    cos = cos.astype(x.dtype)
    sin = sin.astype(x.dtype)
    return jnp.concatenate([x1 * cos - x2 * sin, x2 * cos + x1 * sin], axis=-1)


def banded_gqa_sink_attention(q, k, v, sink):
    b, s, _, _ = q.shape
    nblk = s // BLOCK
    qb = q.reshape(b, nblk, BLOCK, N_KV_HEADS, GQA_GROUP, HEAD_DIM)

    def band(t):
        tp = jnp.pad(t, ((0, 0), (BLOCK, BLOCK), (0, 0), (0, 0)))
        tp = tp.reshape(b, nblk + 2, BLOCK, N_KV_HEADS, HEAD_DIM)
        return jnp.concatenate([tp[:, :-2], tp[:, 1:-1], tp[:, 2:]], axis=2)

    kb, vb = band(k), band(v)
    scores = jnp.einsum('bnqkgd,bnckd->bnkgqc', qb, kb,
                        preferred_element_type=jnp.float32) * (HEAD_DIM ** -0.5)
    qi = jnp.arange(BLOCK)[:, None]
    ci = jnp.arange(3 * BLOCK)[None, :]
    rel = ci - BLOCK - qi
    kpos = jnp.arange(nblk)[:, None, None] * BLOCK - BLOCK + ci[None]
    valid = (jnp.abs(rel) <= WINDOW)[None] & (kpos >= 0) & (kpos < s)
    scores = jnp.where(valid[None, :, None, None], scores, MASK_VALUE)

    sink_f = sink.astype(jnp.float32).reshape(1, 1, N_KV_HEADS, GQA_GROUP, 1)
    m = jnp.maximum(jnp.max(scores, axis=-1), sink_f)
    p = jnp.exp(scores - m[..., None])
    denom = jnp.sum(p, axis=-1) + jnp.exp(sink_f - m)
    out = jnp.einsum('bnkgqc,bnckd->bnqkgd', p.astype(v.dtype), vb,
                     preferred_element_type=jnp.float32)
    out = out / jnp.moveaxis(denom, -1, 2)[..., None]
    return out.reshape(b, s, ATTN_WIDTH).astype(q.dtype)


def attention_mixer(h, w_in, q_g, k_g, sink, w_out, cos, sin):
    b, s, _ = h.shape
    proj = h @ w_in
    q, k, v, gate = jnp.split(
        proj, [ATTN_WIDTH, ATTN_WIDTH + KV_WIDTH, ATTN_WIDTH + 2 * KV_WIDTH], axis=-1)
    q = apply_rope(rmsnorm(q.reshape(b, s, N_Q_HEADS, HEAD_DIM), q_g), cos, sin)
    k = apply_rope(rmsnorm(k.reshape(b, s, N_KV_HEADS, HEAD_DIM), k_g), cos, sin)
    v = v.reshape(b, s, N_KV_HEADS, HEAD_DIM)
    o = banded_gqa_sink_attention(q, k, v, sink)
    return (o * jax.nn.silu(gate)) @ w_out


def short_conv_mixer(h, w_in, conv_w, w_out):
    s = h.shape[1]
    bg, cg, u, gate = jnp.split(h @ w_in, 4, axis=-1)
    z = cg * u
    pad = CONV_WIDTH // 2
    zp = jnp.pad(z, ((0, 0), (pad, pad), (0, 0)))
    conv = sum(conv_w[j] * zp[:, j:j + s] for j in range(CONV_WIDTH))
    y = bg * conv
    return (y * jax.nn.silu(gate)) @ w_out


def setup_inputs(seed: int = 0) -> dict:
    key = jax.random.key(seed)
    ks = jax.random.split(key, 12)
    f32 = jnp.float32
    x = jax.random.normal(ks[0], (BATCH, SEQ, D_MODEL), f32)
    norm_g = 1.0 + 0.02 * jax.random.normal(ks[1], (DEPTH, D_MODEL), f32)
    a_w_in = jax.random.normal(ks[2], (N_ATTN_LAYERS, D_MODEL, A_IN_COLS), f32) * D_MODEL ** -0.5
    a_q_norm = 1.0 + 0.02 * jax.random.normal(ks[3], (N_ATTN_LAYERS, HEAD_DIM), f32)
    a_k_norm = 1.0 + 0.02 * jax.random.normal(ks[4], (N_ATTN_LAYERS, HEAD_DIM), f32)
    a_sink = 0.5 * jax.random.normal(ks[5], (N_ATTN_LAYERS, N_Q_HEADS), f32)
    a_w_out = jax.random.normal(ks[6], (N_ATTN_LAYERS, ATTN_WIDTH, D_MODEL), f32) * ATTN_WIDTH ** -0.5
    b_w_in = jax.random.normal(ks[7], (N_CONV_LAYERS, D_MODEL, B_IN_COLS), f32) * D_MODEL ** -0.5
    b_conv = jax.random.normal(ks[8], (N_CONV_LAYERS, CONV_WIDTH, CONV_DIM), f32) * CONV_WIDTH ** -0.5
    b_w_out = jax.random.normal(ks[9], (N_CONV_LAYERS, CONV_DIM, D_MODEL), f32) * CONV_DIM ** -0.5
    return {"x": x, "norm_g": norm_g, "a_w_in": a_w_in, "a_q_norm": a_q_norm,
            "a_k_norm": a_k_norm, "a_sink": a_sink, "a_w_out": a_w_out,
            "b_w_in": b_w_in, "b_conv": b_conv, "b_w_out": b_w_out}


def reference(x, norm_g, a_w_in, a_q_norm, a_k_norm, a_sink, a_w_out, b_w_in, b_conv, b_w_out):
    cos, sin = rope_tables(x.shape[1])
    for i in range(DEPTH):
        h = rmsnorm(x, norm_g[i])
        slot = i // N_MIXERS
        if i % N_MIXERS == 0:
            y = attention_mixer(h, a_w_in[slot], a_q_norm[slot], a_k_norm[slot],
                                a_sink[slot], a_w_out[slot], cos, sin)
        else:
            y = short_conv_mixer(h, b_w_in[slot], b_conv[slot], b_w_out[slot])
        x = x + y
    return x
```

```python
import contextlib
import numpy as np
import concourse.bass as bass
import concourse.mybir as mybir
from concourse.bass_utils import run_bass_kernel_spmd

F32 = mybir.dt.float32
BF16 = mybir.dt.bfloat16
ALU = mybir.AluOpType
AF = mybir.ActivationFunctionType

D = 1024
S = 2048
BATCH = 16
NCORES = 8
SEQ_PER_CORE = BATCH // NCORES
DEPTH = 4
NT = 4
TW = 512
KC = 8
NBLK = 16
EPS = 1e-6
NS = 6
NP = 4
NDS = 24

CV_NORM = 0
CV_QG = 32
CV_KG = 34
CV_CONV = 36
CV_SINK = 84
CV_EPS = 100
NCV = 101
CB_ONES = 0
CB_BO = 128
CB_RT = 256
CB_MLO = 384
CB_MHI = 896
NCB = 1408

A_CH = 30
B_CH = 40


def _layer_chunk_base(l):
    base = 0
    for i in range(l):
        base += A_CH if i % 2 == 0 else B_CH
    return base


NCH = _layer_chunk_base(DEPTH)


class Tok:
    __slots__ = ("sem", "val", "eng", "key")

    def __init__(self, sem, val, eng, key):
        self.sem, self.val, self.eng, self.key = sem, val, eng, key


class Trk:
    def __init__(self, nc, es):
        self.nc = nc
        self.eng = {"pe": nc.tensor, "act": nc.scalar, "dve": nc.vector,
                    "pool": nc.gpsimd, "sp": nc.sync}
        self.sem = {e: es.enter_context(nc.semaphore("s_" + e))
                    for e in ("pe", "act", "dve", "pool")}
        self.cnt = {e: 0 for e in self.sem}
        self.dsem = [es.enter_context(nc.semaphore("s_dma%d" % i)) for i in range(NDS)]
        self.dcnt = [0] * NDS
        self.dnext = 0
        self.waited = {e: {} for e in self.eng}
        self.lw = {}
        self.rd = {}
        self.nwaits = 0

    def _wait(self, e, tok):
        if tok.eng == "pe" and e == "pe":
            return
        w = self.waited[e]
        if w.get(tok.key, 0) >= tok.val:
            return
        w[tok.key] = tok.val
        self.eng[e].wait_ge(tok.sem, tok.val)
        self.nwaits += 1

    def _deps(self, e, reads, writes):
        for r in reads:
            t = self.lw.get(r)
            if t is not None:
                self._wait(e, t)
        for w in writes:
            t = self.lw.get(w)
            if t is not None:
                self._wait(e, t)
            for t in self.rd.get(w, {}).values():
                self._wait(e, t)

    def _commit(self, tok, reads, writes):
        for r in reads:
            d = self.rd.setdefault(r, {})
            o = d.get(tok.key)
            if o is None or o.val < tok.val:
                d[tok.key] = tok
        for w in writes:
            self.lw[w] = tok
            self.rd[w] = {}

    def grp(self, e, instrs, reads=(), writes=()):
        self._deps(e, reads, writes)
        eng = self.eng[e]
        ins = None
        for name, kw in instrs:
            ins = getattr(eng, name)(**kw)
        self.cnt[e] += 1
        ins.then_inc(self.sem[e], 1)
        tok = Tok(self.sem[e], self.cnt[e], e, e)
        self._commit(tok, reads, writes)
        return tok

    def op(self, e, name, kw, reads=(), writes=()):
        return self.grp(e, [(name, kw)], reads, writes)

    def dma(self, e, kw, reads=(), writes=()):
        i = self.dnext
        self.dnext = (i + 1) % NDS
        key = ("d", i)
        if self.dcnt[i] > 0:
            self._wait(e, Tok(self.dsem[i], 16 * self.dcnt[i], "dma", key))
        self._deps(e, reads, writes)
        ins = self.eng[e].dma_start(**kw)
        self.dcnt[i] += 1
        ins.then_inc(self.dsem[i], 16)
        tok = Tok(self.dsem[i], 16 * self.dcnt[i], "dma", key)
        self._commit(tok, reads, writes)
        return tok

    def finish(self, e="sp"):
        for x in self.sem:
            if self.cnt[x] > 0:
                self._wait(e, Tok(self.sem[x], self.cnt[x], x, x))
        for i in range(NDS):
            if self.dcnt[i] > 0:
                self._wait(e, Tok(self.dsem[i], 16 * self.dcnt[i], "dma", ("d", i)))


def build_program(layers, nseq=SEQ_PER_CORE, debug=False):
    nc = bass.Bass("TRN2", target_bir_lowering=False)
    xin = nc.dram_tensor("xin", [nseq, D, S], F32, kind="ExternalInput").ap()
    wts = nc.dram_tensor("wts", [NCH, 128, KC, 128], F32, kind="ExternalInput").ap()
    cvec = nc.dram_tensor("cvec", [128, NCV], F32, kind="ExternalInput").ap()
    cbf = nc.dram_tensor("cbf", [128, NCB], F32, kind="ExternalInput").ap()
    rope = nc.dram_tensor("rope", [128, 2, S], F32, kind="ExternalInput").ap()
    yout = nc.dram_tensor("yout", [nseq, D, S], F32, kind="ExternalOutput").ap()
    if debug:
        dbg_h = nc.dram_tensor("dbg_h", [128, KC, S], BF16, kind="ExternalOutput").ap()
        dbg_og = nc.dram_tensor("dbg_og", [128, KC, S], BF16, kind="ExternalOutput").ap()

    es = contextlib.ExitStack()
    with es:
        def sb(name, shape, dt):
            return es.enter_context(nc.sbuf_tensor(name, shape, dt))

        xT = sb("xT", [128, KC, S], F32)
        hT = sb("hT", [128, KC, S], BF16)
        ogT = sb("ogT", [128, KC, S], BF16)
        ring = [sb("ring%d" % i, [128, KC, 128], BF16) for i in range(NS)]
        cs = sb("cs", [128, 2, S], BF16)
        cv = sb("cv", [128, NCV], F32)
        esink = sb("esink", [128, 16], F32)
        cb = sb("cb", [128, NCB], BF16)
        sqn = [sb("sqn%d" % i, [128, TW], BF16) for i in range(2)]
        nl = sb("nl", [128, TW], F32)
        nr = sb("nr", [128, TW], F32)
        SCR_BYTES = 48 * 1024
        scr = sb("scr", [128, SCR_BYTES // 2], BF16)
        ps = [es.enter_context(nc.psum_tensor("ps%d" % i, [128, TW], F32)) for i in range(8)]

        T = Trk(nc, es)

        class Carve:
            def __init__(self):
                self.off = 0

            def take(self, nbytes, dt):
                nbytes = (nbytes + 31) // 32 * 32
                o = self.off
                self.off += nbytes
                assert self.off <= SCR_BYTES, (self.off, SCR_BYTES)
                v = scr[:, o // 2:(o + nbytes) // 2]
                if dt == F32:
                    v = v.bitcast(F32)
                return v

        ca = Carve()
        qr = ca.take(2 * S * 2, BF16).rearrange("p (c t) -> p c t", c=2)
        kr = ca.take(S * 2, BF16)
        Vt = ca.take(NBLK * 256 * 2, BF16)
        Pt = [ca.take(4 * 384 * 2, BF16).rearrange("p (h q) -> p h q", h=4) for _ in range(NP)]
        sq = [ca.take(TW * 2, BF16) for _ in range(2)]
        qg = [ca.take(TW * 2, BF16) for _ in range(2)]
        lt = ca.take(TW * 4, F32)
        rstd = ca.take(TW * 4, F32)
        t1 = ca.take(TW * 4, F32)
        t2 = ca.take(TW * 4, F32)
        d1 = ca.take(256 * 4, F32)
        rdn = ca.take(256 * 4, F32)
        onr = ca.take(256 * 4, F32)
        A_KEYS = ([("q", c, j) for c in range(2) for j in range(NT)] + [("k", j) for j in range(NT)]
                  + [("v", i) for i in range(8)] + [("p", i) for i in range(NP)]
                  + [("sq", i) for i in range(2)] + [("qg", i) for i in range(2)]
                  + [("lt",), ("rstd",), ("t1",), ("t2",), ("d1",), ("rdn",), ("onr",)])
        cbv = Carve()
        ZW = S + 2
        zb = [cbv.take(ZW * 4, F32) for _ in range(2)]
        u_sb = cbv.take(TW * 4, F32)
        sgb = cbv.take(TW * 4, F32)
        bsb = [cbv.take(TW * 4, F32) for _ in range(2)]
        a0 = cbv.take(TW * 4, F32)
        a1 = cbv.take(TW * 4, F32)
        B_KEYS = ([("z", i, j) for i in range(2) for j in range(NT)] + [("zpad", i) for i in range(2)]
                  + [("u",), ("sg",), ("bs", 0), ("bs", 1), ("a0",), ("a1",)])

        def retire(keys_old, keys_new):
            toks = {}
            for k in keys_old:
                t = T.lw.pop(k, None)
                cands = list(T.rd.pop(k, {}).values())
                if t is not None:
                    cands.append(t)
                for t in cands:
                    o = toks.get(t.key)
                    if o is None or o.val < t.val:
                        toks[t.key] = t
            for k in keys_new:
                T.lw.pop(k, None)
                T.rd[k] = dict(toks)

        state = {"bank": 0, "wuse": 0, "wissued": 0}

        def nb():
            b = state["bank"]
            state["bank"] = (b + 1) % 8
            return b

        wplan = []

        def plan_layer(l):
            base = _layer_chunk_base(l)
            if l % 2 == 0:
                order = list(range(0, 8)) + [8, 9]
                for g in range(4):
                    order += [10 + 2 * g, 10 + 2 * g + 1, 18 + g]
                for j in range(NT):
                    order += [22 + co for co in range(8)]
            else:
                order = []
                for c in range(8):
                    order += [8 + c, 16 + c, 24 + c, c]
                for j in range(NT):
                    order += [32 + co for co in range(8)]
            return [base + o for o in order]

        for s_ in range(nseq):
            for l in layers:
                wplan.extend(plan_layer(l))

        def wprefetch(upto):
            upto = min(upto, len(wplan) - 1)
            while state["wissued"] <= upto:
                i = state["wissued"]
                slot = i % NS
                T.dma("pool", dict(out=ring[slot][:, :, :], in_=wts[wplan[i]]),
                      reads=[], writes=[("w", slot)])
                state["wissued"] += 1

        def wuse():
            i = state["wuse"]
            state["wuse"] += 1
            wprefetch(i)
            return i % NS

        def wrel():
            state["wrel"] = state.get("wrel", 0) + 1
            wprefetch(state["wrel"] + NS - 1)

        def ts(j):
            return slice(j * TW, (j + 1) * TW)

        T.dma("sp", dict(out=cv[:, :], in_=cvec[:, :]), writes=[("cv",)])
        T.dma("pool", dict(out=cb[:, :], in_=cbf[:, :]), writes=[("cb",)])
        T.dma("pool", dict(out=cs[:, :, :], in_=rope[:, :, :]), writes=[("cs",)])
        T.op("act", "activation", dict(out=esink[:, :], in_=cv[:, CV_SINK:CV_SINK + 16], func=AF.Exp),
             reads=[("cv",)], writes=[("esink",)])
        ones = cb[:, CB_ONES:CB_ONES + 128]
        bo = cb[:, CB_BO:CB_BO + 128]
        rt = cb[:, CB_RT:CB_RT + 128]
        mlo = cb[:, CB_MLO:CB_MLO + 512].rearrange("p (h q) -> p h q", h=4)
        mhi = cb[:, CB_MHI:CB_MHI + 512].rearrange("p (h q) -> p h q", h=4)
        epsc = cv[:, CV_EPS:CV_EPS + 1]

        def ogkeys(c, j):
            return [("og", c, n) for n in range(4 * j, 4 * j + 4)]

        def emit_norm(l, j):
            b = nb()
            for k in range(KC):
                T.op("act", "activation", dict(out=sqn[k % 2][:, :], in_=xT[:, k, ts(j)], func=AF.Square),
                     reads=[("x", k, j)], writes=[("sqn", k % 2)])
                T.op("pe", "matmul", dict(out=ps[b][:, :], lhsT=ones, rhs=sqn[k % 2][:, :],
                                          start=(k == 0), stop=(k == KC - 1)),
                     reads=[("sqn", k % 2), ("cb",)], writes=[("ps", b)])
            T.op("act", "activation", dict(out=nl[:, :], in_=ps[b][:, :], func=AF.Ln, scale=1.0 / D, bias=epsc),
                 reads=[("cv",)], writes=[("ps", b), ("nl",)])
            T.op("act", "activation", dict(out=nr[:, :], in_=nl[:, :], func=AF.Exp, scale=-0.5),
                 reads=[("nl",)], writes=[("nr",)])
            for k in range(KC):
                T.op("dve", "scalar_tensor_tensor",
                     dict(out=hT[:, k, ts(j)], in0=xT[:, k, ts(j)], scalar=cv[:, CV_NORM + l * 8 + k:CV_NORM + l * 8 + k + 1],
                          in1=nr[:, :], op0=ALU.mult, op1=ALU.mult),
                     reads=[("x", k, j), ("nr",), ("cv",)], writes=[("h", k, j)])

        def emit_proj(slot, j, b, src, srckeys):
            T.grp("pe", [("matmul", dict(out=ps[b][:, :], lhsT=ring[slot][:, k, :], rhs=src[:, k, ts(j)],
                                         start=(k == 0), stop=(k == KC - 1))) for k in range(KC)],
                  reads=[("w", slot)] + srckeys, writes=[("ps", b)])

        def hkeys(j):
            return [("h", k, j) for k in range(KC)]

        def emit_outproj(l, j):
            srck = [key for k in range(KC) for key in ogkeys(k, j)]
            for co in range(KC):
                slot = wuse()
                b = nb()
                emit_proj(slot, j, b, ogT, srck)
                wrel()
                T.op("dve", "tensor_tensor", dict(out=xT[:, co, ts(j)], in0=ps[b][:, :], in1=xT[:, co, ts(j)], op=ALU.add),
                     reads=[], writes=[("ps", b), ("x", co, j)])

        def emit_qk_stageA(item):
            kind, slot, j, i = item["kind"], item["slot"], item["j"], item["i"]
            b = nb()
            item["b"] = b
            emit_proj(slot, j, b, hT, hkeys(j))
            if j == NT - 1:
                wrel()
            gcol = item["gcol"]
            T.op("act", "activation", dict(out=sq[i % 2][:, :], in_=ps[b][:, :], func=AF.Square),
                 writes=[("ps", b), ("sq", i % 2)])
            T.op("act", "activation", dict(out=qg[i % 2][:, :], in_=ps[b][:, :], func=AF.Copy, scale=gcol),
                 reads=[("cv",)], writes=[("ps", b), ("qg", i % 2)])

        def emit_qk_stageB(item):
            j, i, b = item["j"], item["i"], item["b"]
            gcol = item["gcol"]
            b2 = nb()
            T.op("pe", "matmul", dict(out=ps[b2][:, :], lhsT=bo, rhs=sq[i % 2][:, :], start=True, stop=True),
                 reads=[("sq", i % 2), ("cb",)], writes=[("ps", b2)])
            b3 = nb()
            T.op("pe", "matmul", dict(out=ps[b3][:, :], lhsT=rt, rhs=qg[i % 2][:, :], start=True, stop=True),
                 reads=[("qg", i % 2), ("cb",)], writes=[("ps", b3)])
            T.op("act", "activation", dict(out=lt[:, :], in_=ps[b2][:, :], func=AF.Ln, scale=1.0 / 64, bias=epsc),
                 reads=[("cv",)], writes=[("ps", b2), ("lt",)])
            T.op("act", "activation", dict(out=rstd[:, :], in_=lt[:, :], func=AF.Exp, scale=-0.5),
                 reads=[("lt",)], writes=[("rstd",)])
            T.op("dve", "scalar_tensor_tensor",
                 dict(out=t1[:, :], in0=ps[b][:, :], scalar=gcol, in1=cs[:, 0, ts(j)], op0=ALU.mult, op1=ALU.mult),
                 reads=[("cv",), ("cs",)], writes=[("ps", b), ("t1",)])
            T.op("dve", "tensor_tensor", dict(out=t2[:, :], in0=ps[b3][:, :], in1=cs[:, 1, ts(j)], op=ALU.mult),
                 reads=[("cs",)], writes=[("ps", b3), ("t2",)])
            T.op("pool", "tensor_tensor", dict(out=t1[:, :], in0=t1[:, :], in1=t2[:, :], op=ALU.add),
                 reads=[("t2",)], writes=[("t1",)])
            T.op("dve", "tensor_tensor", dict(out=item["dst"], in0=t1[:, :], in1=rstd[:, :], op=ALU.mult),
                 reads=[("t1",), ("rstd",)], writes=[item["dkey"]])

        def emit_S(a, g, m):
            slot = m % NP
            lo, hi = max(m - 1, 0), min(m + 1, NBLK - 1)
            qs, qe = lo * 128, (hi + 1) * 128
            off = (lo - (m - 1)) * 128
            w = qe - qs
            qkeys_j = sorted(set([qs // TW, (qe - 1) // TW]))
            for hh in range(4):
                cc, par = hh // 2, hh % 2
                rows = slice(par * 64, (par + 1) * 64)
                b = nb()
                T.op("pe", "matmul", dict(out=ps[b][:, off:off + w], lhsT=kr[rows, m * 128:(m + 1) * 128],
                                          rhs=qr[rows, cc, qs:qe], start=True, stop=True),
                     reads=[("k", m // 4)] + [("q", cc, jj) for jj in qkeys_j], writes=[("ps", b)])
                T.op("act", "activation", dict(out=Pt[slot][:, hh, off:off + w], in_=ps[b][:, off:off + w],
                                               func=AF.Exp, scale=0.125),
                     writes=[("ps", b), ("p", slot)])
            if m >= 1:
                T.op("pool", "tensor_tensor", dict(out=Pt[slot][:, :, 0:128], in0=Pt[slot][:, :, 0:128], in1=mlo, op=ALU.mult),
                     reads=[("cb",)], writes=[("p", slot)])
            if m <= NBLK - 2:
                T.op("pool", "tensor_tensor", dict(out=Pt[slot][:, :, 256:384], in0=Pt[slot][:, :, 256:384], in1=mhi, op=ALU.mult),
                     reads=[("cb",)], writes=[("p", slot)])

        def emit_PV(a, g, n):
            b = nb()
            mms = [mm for mm in (n - 1, n, n + 1) if 0 <= mm < NBLK]
            instrs = []
            for par in range(2):
                rows = slice(par * 64, (par + 1) * 64)
                for kind in range(2):
                    outv = ps[b][rows, kind * 256:(kind + 1) * 256].rearrange("p (a q) -> p a q", a=2)
                    for idx, mm in enumerate(mms):
                        cbk = n - mm + 1
                        rhs = Pt[mm % NP][:, par::2, cbk * 128:(cbk + 1) * 128]
                        lhsT = Vt[:, mm * 256 + g * 64: mm * 256 + g * 64 + 64] if kind == 0 else cb[:, CB_ONES:CB_ONES + 64]
                        instrs.append(("matmul", dict(out=outv, lhsT=lhsT, rhs=rhs, start=(idx == 0),
                                                      stop=(idx == len(mms) - 1), tile_position=(0, par * 64))))
            T.grp("pe", instrs, reads=[("p", mm % NP) for mm in mms] + [("v", mm // 2) for mm in mms] + [("cb",)],
                  writes=[("ps", b)])
            sc = CV_SINK - CV_SINK + (a * 4 + g) * 2
            for jj in range(2):
                T.op("dve", "tensor_scalar",
                     dict(out=d1[:, jj * 128:(jj + 1) * 128], in0=ps[b][:, 256 + jj * 128:256 + (jj + 1) * 128],
                          scalar1=esink[:, sc + jj:sc + jj + 1], scalar2=None, op0=ALU.add),
                     reads=[("esink",)], writes=[("ps", b), ("d1",)])
            T.op("dve", "reciprocal", dict(out=rdn[:, :], in_=d1[:, :]), reads=[("d1",)], writes=[("rdn",)])
            T.op("dve", "tensor_tensor", dict(out=onr[:, :], in0=ps[b][:, 0:256], in1=rdn[:, :], op=ALU.mult),
                 reads=[("rdn",)], writes=[("ps", b), ("onr",)])
            ogv = ogT[:, 2 * g:2 * g + 2, n * 128:(n + 1) * 128]
            T.op("pool", "tensor_tensor", dict(out=ogv, in0=onr[:, :].rearrange("p (a q) -> p a q", a=2), in1=ogv, op=ALU.mult),
                 reads=[("onr",)], writes=[("og", 2 * g, n), ("og", 2 * g + 1, n)])

        def emit_attn_layer(l):
            a = l // 2
            for c in range(KC):
                slot = wuse()
                for j in range(NT):
                    b = nb()
                    emit_proj(slot, j, b, hT, hkeys(j))
                    if j == NT - 1:
                        wrel()
                    T.op("act", "activation", dict(out=ogT[:, c, ts(j)], in_=ps[b][:, :], func=AF.Silu),
                         writes=[("ps", b)] + ogkeys(c, j))
            vs = [wuse(), wuse()]
            for tb2 in range(8):
                b = nb()
                instrs = []
                for t in range(2):
                    tb = 2 * tb2 + t
                    for vc in range(2):
                        for k in range(KC):
                            instrs.append(("matmul", dict(out=ps[b][:, t * 256 + vc * 128:t * 256 + (vc + 1) * 128],
                                                          lhsT=hT[:, k, tb * 128:(tb + 1) * 128], rhs=ring[vs[vc]][:, k, :],
                                                          start=(k == 0), stop=(k == KC - 1))))
                T.grp("pe", instrs, reads=[("w", vs[0]), ("w", vs[1])] + hkeys(tb2 // 2), writes=[("ps", b)])
                if tb2 == 7:
                    wrel()
                    wrel()
                T.op("act", "activation", dict(out=Vt[:, tb2 * 512:(tb2 + 1) * 512], in_=ps[b][:, :], func=AF.Copy),
                     writes=[("ps", b), ("v", tb2)])
            for g in range(4):
                items = []
                for kind, cc in (("q", 0), ("q", 1), ("k", 0)):
                    slot = wuse()
                    for j in range(NT):
                        if kind == "q":
                            dst, dkey = qr[:, cc, ts(j)], ("q", cc, j)
                            gcol = cv[:, CV_QG + a:CV_QG + a + 1]
                        else:
                            dst, dkey = kr[:, ts(j)], ("k", j)
                            gcol = cv[:, CV_KG + a:CV_KG + a + 1]
                        items.append(dict(kind=kind, slot=slot, j=j, i=len(items), dst=dst, dkey=dkey, gcol=gcol))
                for i in range(len(items) + 1):
                    if i < len(items):
                        emit_qk_stageA(items[i])
                    if i >= 1:
                        emit_qk_stageB(items[i - 1])
                for step in range(NBLK + 2):
                    if step < NBLK:
                        emit_S(a, g, step)
                    if step >= 2:
                        emit_PV(a, g, step - 2)

        def emit_conv_layer(l):
            bi = l // 2
            for c in range(KC):
                zi = c % 2
                z = zb[zi]
                slots = [wuse() for _ in range(4)]
                wc = CV_CONV + (bi * 8 + c) * 3

                def conv(j):
                    rk = [("z", zi, jj) for jj in (j - 1, j, j + 1) if 0 <= jj < NT] + [("zpad", zi)]
                    T.op("act", "activation", dict(out=a0[:, :], in_=z[:, j * TW:j * TW + TW], func=AF.Copy, scale=cv[:, wc:wc + 1]),
                         reads=rk + [("cv",)], writes=[("a0",)])
                    T.op("dve", "scalar_tensor_tensor",
                         dict(out=a1[:, :], in0=z[:, 1 + j * TW:1 + j * TW + TW], scalar=cv[:, wc + 1:wc + 2], in1=a0[:, :],
                              op0=ALU.mult, op1=ALU.add),
                         reads=rk + [("a0",), ("cv",)], writes=[("a1",)])
                    T.op("dve", "scalar_tensor_tensor",
                         dict(out=a0[:, :], in0=z[:, 2 + j * TW:2 + j * TW + TW], scalar=cv[:, wc + 2:wc + 3], in1=a1[:, :],
                              op0=ALU.mult, op1=ALU.add),
                         reads=rk + [("a1",), ("cv",)], writes=[("a0",)])
                    T.op("pool", "tensor_tensor", dict(out=ogT[:, c, ts(j)], in0=a0[:, :], in1=bsb[j % 2][:, :], op=ALU.mult),
                         reads=[("a0",), ("bs", j % 2)], writes=ogkeys(c, j))

                for j in range(NT):
                    bcg, bu, bgt, bbg = nb(), nb(), nb(), nb()
                    for si, bb in enumerate((bcg, bu, bgt, bbg)):
                        emit_proj(slots[si], j, bb, hT, hkeys(j))
                        if j == NT - 1:
                            wrel()
                    T.op("act", "activation", dict(out=u_sb[:, :], in_=ps[bu][:, :], func=AF.Copy),
                         writes=[("ps", bu), ("u",)])
                    T.op("act", "activation", dict(out=sgb[:, :], in_=ps[bgt][:, :], func=AF.Silu),
                         writes=[("ps", bgt), ("sg",)])
                    T.op("dve", "tensor_tensor", dict(out=z[:, 1 + j * TW:1 + (j + 1) * TW], in0=ps[bcg][:, :], in1=u_sb[:, :], op=ALU.mult),
                         reads=[("u",)], writes=[("ps", bcg), ("z", zi, j)])
                    T.op("dve", "tensor_tensor", dict(out=bsb[j % 2][:, :], in0=ps[bbg][:, :], in1=sgb[:, :], op=ALU.mult),
                         reads=[("sg",)], writes=[("ps", bbg), ("bs", j % 2)])
                    if j >= 1:
                        conv(j - 1)
                conv(NT - 1)

        cur_scratch = None

        def set_scratch(kind):
            nonlocal cur_scratch
            if cur_scratch == kind:
                return
            old = A_KEYS if cur_scratch == "A" else (B_KEYS if cur_scratch == "B" else [])
            new = A_KEYS if kind == "A" else B_KEYS
            retire(old, new)
            cur_scratch = kind
            if kind == "B":
                for i in range(2):
                    T.op("pool", "memset", dict(ap=zb[i][:, 0:1], constant=0.0), writes=[("zpad", i)])
                    T.op("pool", "memset", dict(ap=zb[i][:, ZW - 1:ZW], constant=0.0), writes=[("zpad", i)])

        for s_ in range(nseq):
            for j in range(NT):
                for k in range(KC):
                    T.dma("sp", dict(out=xT[:, k, ts(j)], in_=xin[s_, k * 128:(k + 1) * 128, ts(j)]),
                          writes=[("x", k, j)])
            for li, l in enumerate(layers):
                if li == 0:
                    for j in range(NT):
                        emit_norm(l, j)
                set_scratch("A" if l % 2 == 0 else "B")
                if l % 2 == 0:
                    emit_attn_layer(l)
                else:
                    emit_conv_layer(l)
                nxt = layers[li + 1] if li + 1 < len(layers) else None
                for step in range(NT + 1):
                    if step < NT:
                        emit_outproj(l, step)
                    if step >= 1:
                        j = step - 1
                        if nxt is not None:
                            emit_norm(nxt, j)
                        else:
                            for k in range(KC):
                                T.dma("sp", dict(out=yout[s_, k * 128:(k + 1) * 128, ts(j)], in_=xT[:, k, ts(j)]),
                                      reads=[("x", k, j)])
        if debug:
            T.dma("sp", dict(out=dbg_h[:, :, :], in_=hT[:, :, :]), reads=[("h", k, j) for k in range(KC) for j in range(NT)])
            T.dma("sp", dict(out=dbg_og[:, :, :], in_=ogT[:, :, :]), reads=[("og", k, n) for k in range(KC) for n in range(NBLK)])
        T.finish("sp")
        T.finish("pool")
        T.finish("act")
        T.finish("dve")
        T.finish("pe")
    return nc


def _chunk(wcols):
    return np.ascontiguousarray(wcols.reshape(KC, 128, 128).transpose(1, 0, 2))


def _prep_weights(a_w_in, a_w_out, b_w_in, b_w_out):
    out = np.empty((NCH, 128, KC, 128), np.float32)
    for l in range(DEPTH):
        base = _layer_chunk_base(l)
        s = l // 2
        if l % 2 == 0:
            w = a_w_in[s]
            for c in range(8):
                out[base + c] = _chunk(w[:, 1536 + c * 128:1536 + (c + 1) * 128])
            for vc in range(2):
                out[base + 8 + vc] = _chunk(w[:, 1280 + vc * 128:1280 + (vc + 1) * 128])
            for c in range(8):
                out[base + 10 + c] = _chunk(w[:, c * 128:(c + 1) * 128])
            for g in range(4):
                kg = w[:, 1024 + g * 64:1024 + (g + 1) * 64]
                out[base + 18 + g] = _chunk(np.concatenate([kg, kg], axis=1))
            for co in range(8):
                out[base + 22 + co] = _chunk(a_w_out[s][:, co * 128:(co + 1) * 128])
        else:
            w = b_w_in[s]
            for c in range(32):
                out[base + c] = _chunk(w[:, c * 128:(c + 1) * 128])
            for co in range(8):
                out[base + 32 + co] = _chunk(b_w_out[s][:, co * 128:(co + 1) * 128])
    return out


def _prep_consts(norm_g, a_q_norm, a_k_norm, a_sink, b_conv):
    cvec = np.zeros((128, NCV), np.float32)
    p = np.arange(128)
    for l in range(DEPTH):
        for k in range(KC):
            cvec[:, CV_NORM + l * 8 + k] = norm_g[l, k * 128:(k + 1) * 128]
    for a in range(2):
        cvec[:, CV_QG + a] = a_q_norm[a][p % 64]
        cvec[:, CV_KG + a] = a_k_norm[a][p % 64]
        for g in range(4):
            for j in range(2):
                cvec[:, CV_SINK + (a * 4 + g) * 2 + j] = a_sink[a][4 * g + 2 * j + (p >= 64)]
    for b in range(2):
        for c in range(KC):
            for t in range(3):
                cvec[:, CV_CONV + (b * 8 + c) * 3 + t] = b_conv[b, t, c * 128:(c + 1) * 128]
    cvec[:, CV_EPS] = EPS

    cbf = np.zeros((128, NCB), np.float32)
    cbf[:, CB_ONES:CB_ONES + 128] = 1.0
    cbf[:, CB_BO:CB_BO + 128] = (p[:, None] // 64 == p[None, :] // 64)
    rtm = np.zeros((128, 128), np.float32)
    for i in range(128):
        if i % 64 < 32:
            rtm[i + 32, i] = -1.0
        else:
            rtm[i - 32, i] = 1.0
    cbf[:, CB_RT:CB_RT + 128] = rtm
    mlo = (p[:, None] <= p[None, :]).astype(np.float32)
    mhi = (p[None, :] <= p[:, None]).astype(np.float32)
    cbf[:, CB_MLO:CB_MLO + 512] = np.tile(mlo, (1, 4))
    cbf[:, CB_MHI:CB_MHI + 512] = np.tile(mhi, (1, 4))

    inv_freq = (10000.0 ** (-np.arange(0, 64, 2, dtype=np.float32) / 64)).astype(np.float32)
    ang = np.arange(S, dtype=np.float32)[:, None] * inv_freq[None, :]
    cosT = np.cos(ang).astype(np.float32).T
    sinT = np.sin(ang).astype(np.float32).T
    rope = np.empty((128, 2, S), np.float32)
    rope[:, 0, :] = cosT[p % 32]
    rope[:, 1, :] = sinT[p % 32]
    return cvec, cbf, rope


def _run(inputs, layers, seqs=None, ncores=NCORES, debug=False):
    x = np.asarray(inputs["x"], np.float32)
    wts = _prep_weights(np.asarray(inputs["a_w_in"], np.float32), np.asarray(inputs["a_w_out"], np.float32),
                        np.asarray(inputs["b_w_in"], np.float32), np.asarray(inputs["b_w_out"], np.float32))
    cvec, cbf, rope = _prep_consts(np.asarray(inputs["norm_g"], np.float32), np.asarray(inputs["a_q_norm"], np.float32),
                                   np.asarray(inputs["a_k_norm"], np.float32), np.asarray(inputs["a_sink"], np.float32),
                                   np.asarray(inputs["b_conv"], np.float32))
    nseq = SEQ_PER_CORE if seqs is None else seqs
    xT = np.ascontiguousarray(x.transpose(0, 2, 1))
    import time as _t
    _t0 = _t.time()
    nc = build_program(list(layers), nseq=nseq, debug=debug)
    print("[kernel] build %.1fs" % (_t.time() - _t0), flush=True)
    in_maps = []
    for c in range(ncores):
        in_maps.append({"xin": np.ascontiguousarray(xT[c * nseq:(c + 1) * nseq]), "wts": wts, "cvec": cvec, "cbf": cbf, "rope": rope})
    res = run_bass_kernel_spmd(nc, in_maps, core_ids=list(range(ncores)))
    if debug:
        return res
    outT = np.concatenate([r["yout"] for r in res.results], axis=0)
    return np.ascontiguousarray(outT.transpose(0, 2, 1)).astype(np.float32)


def kernel(x, norm_g, a_w_in, a_q_norm, a_k_norm, a_sink, a_w_out, b_w_in, b_conv, b_w_out):
    inputs = dict(x=x, norm_g=norm_g, a_w_in=a_w_in, a_q_norm=a_q_norm, a_k_norm=a_k_norm, a_sink=a_sink,
                  a_w_out=a_w_out, b_w_in=b_w_in, b_conv=b_conv, b_w_out=b_w_out)
    return _run(inputs, layers=list(range(DEPTH)))
```

```python
import contextlib
import numpy as np
import concourse.bass as bass
import concourse.mybir as mybir
from concourse.bass_utils import run_bass_kernel_spmd

F32 = mybir.dt.float32
BF16 = mybir.dt.bfloat16
ALU = mybir.AluOpType
AF = mybir.ActivationFunctionType

D = 1024
S = 2048
BATCH = 16
NCORES = 8
SEQ_PER_CORE = BATCH // NCORES
DEPTH = 4
NT = 4
TW = 512
KC = 8
NBLK = 16
EPS = 1e-6
NS = 6
NP = 4
NDS = 24

CV_NORM = 0
CV_QG = 32
CV_KG = 34
CV_CONV = 36
CV_SINK = 84
CV_EPS = 100
NCV = 101
CB_ONES = 0
CB_BO = 128
CB_RT = 256
CB_MLO = 384
CB_MHI = 896
NCB = 1408

A_CH = 30
B_CH = 40


def _layer_chunk_base(l):
    base = 0
    for i in range(l):
        base += A_CH if i % 2 == 0 else B_CH
    return base


NCH = _layer_chunk_base(DEPTH)


class Tok:
    __slots__ = ("sem", "val", "eng", "key")

    def __init__(self, sem, val, eng, key):
        self.sem, self.val, self.eng, self.key = sem, val, eng, key


class Trk:
    def __init__(self, nc, es):
        self.nc = nc
        self.eng = {"pe": nc.tensor, "act": nc.scalar, "dve": nc.vector,
                    "pool": nc.gpsimd, "sp": nc.sync}
        self.sem = {e: es.enter_context(nc.semaphore("s_" + e))
                    for e in ("pe", "act", "dve", "pool")}
        self.cnt = {e: 0 for e in self.sem}
        self.dsem = [es.enter_context(nc.semaphore("s_dma%d" % i)) for i in range(NDS)]
        self.dcnt = [0] * NDS
        self.dnext = 0
        self.waited = {e: {} for e in self.eng}
        self.lw = {}
        self.rd = {}
        self.nwaits = 0
        self.phase = ""
        self.annotate = False

    def _wait(self, e, tok):
        if tok.eng == "pe" and e == "pe":
            return
        w = self.waited[e]
        if w.get(tok.key, 0) >= tok.val:
            return
        w[tok.key] = tok.val
        self.eng[e].wait_ge(tok.sem, tok.val)
        self.nwaits += 1

    def _deps(self, e, reads, writes):
        for r in reads:
            t = self.lw.get(r)
            if t is not None:
                self._wait(e, t)
        for w in writes:
            t = self.lw.get(w)
            if t is not None:
                self._wait(e, t)
            for t in self.rd.get(w, {}).values():
                self._wait(e, t)

    def _commit(self, tok, reads, writes):
        for r in reads:
            d = self.rd.setdefault(r, {})
            o = d.get(tok.key)
            if o is None or o.val < tok.val:
                d[tok.key] = tok
        for w in writes:
            self.lw[w] = tok
            self.rd[w] = {}

    def grp(self, e, instrs, reads=(), writes=()):
        self._deps(e, reads, writes)
        eng = self.eng[e]
        ins = None
        for name, kw in instrs:
            ins = getattr(eng, name)(**kw)
            if self.annotate:
                ins.annotate(self.phase)
        self.cnt[e] += 1
        ins.then_inc(self.sem[e], 1)
        tok = Tok(self.sem[e], self.cnt[e], e, e)
        self._commit(tok, reads, writes)
        return tok

    def op(self, e, name, kw, reads=(), writes=()):
        return self.grp(e, [(name, kw)], reads, writes)

    def dma(self, e, kw, reads=(), writes=()):
        i = self.dnext
        self.dnext = (i + 1) % NDS
        key = ("d", i)
        if self.dcnt[i] > 0:
            self._wait(e, Tok(self.dsem[i], 16 * self.dcnt[i], "dma", key))
        self._deps(e, reads, writes)
        ins = self.eng[e].dma_start(**kw)
        if self.annotate:
            ins.annotate(self.phase + "/dma")
        self.dcnt[i] += 1
        ins.then_inc(self.dsem[i], 16)
        tok = Tok(self.dsem[i], 16 * self.dcnt[i], "dma", key)
        self._commit(tok, reads, writes)
        return tok

    def finish(self, e="sp"):
        for x in self.sem:
            if self.cnt[x] > 0:
                self._wait(e, Tok(self.sem[x], self.cnt[x], x, x))
        for i in range(NDS):
            if self.dcnt[i] > 0:
                self._wait(e, Tok(self.dsem[i], 16 * self.dcnt[i], "dma", ("d", i)))


def build_program(layers, nseq=SEQ_PER_CORE, debug=False, annotate=False):
    nc = bass.Bass("TRN2", target_bir_lowering=False)
    xin = nc.dram_tensor("xin", [nseq, D, S], F32, kind="ExternalInput").ap()
    wts = nc.dram_tensor("wts", [NCH, 128, KC, 128], F32, kind="ExternalInput").ap()
    cvec = nc.dram_tensor("cvec", [128, NCV], F32, kind="ExternalInput").ap()
    cbf = nc.dram_tensor("cbf", [128, NCB], F32, kind="ExternalInput").ap()
    rope = nc.dram_tensor("rope", [128, 2, S], F32, kind="ExternalInput").ap()
    yout = nc.dram_tensor("yout", [nseq, D, S], F32, kind="ExternalOutput").ap()
    if debug:
        dbg_h = nc.dram_tensor("dbg_h", [128, KC, S], BF16, kind="ExternalOutput").ap()
        dbg_og = nc.dram_tensor("dbg_og", [128, KC, S], BF16, kind="ExternalOutput").ap()

    es = contextlib.ExitStack()
    with es:
        def sb(name, shape, dt):
            return es.enter_context(nc.sbuf_tensor(name, shape, dt))

        xT = sb("xT", [128, KC, S], F32)
        hT = sb("hT", [128, KC, S], BF16)
        ogT = sb("ogT", [128, KC, S], BF16)
        ring = [sb("ring%d" % i, [128, KC, 128], BF16) for i in range(NS)]
        cs = sb("cs", [128, 2, S], BF16)
        cv = sb("cv", [128, NCV], F32)
        esink = sb("esink", [128, 16], F32)
        cb = sb("cb", [128, NCB], BF16)
        SCR_BYTES = 56 * 1024
        scr = sb("scr", [128, SCR_BYTES // 2], BF16)
        ps = [es.enter_context(nc.psum_tensor("ps%d" % i, [128, TW], F32)) for i in range(8)]

        T = Trk(nc, es)
        T.annotate = annotate

        class Carve:
            def __init__(self):
                self.off = 0

            def take(self, nbytes, dt):
                nbytes = (nbytes + 31) // 32 * 32
                o = self.off
                self.off += nbytes
                assert self.off <= SCR_BYTES, (self.off, SCR_BYTES)
                v = scr[:, o // 2:(o + nbytes) // 2]
                if dt == F32:
                    v = v.bitcast(F32)
                return v

        ca = Carve()
        qr = [ca.take(2 * S * 2, BF16).rearrange("p (c t) -> p c t", c=2) for _ in range(2)]
        kr = [ca.take(S * 2, BF16) for _ in range(2)]
        Vt = ca.take(NBLK * 256 * 2, BF16)
        Pt = [ca.take(4 * 384 * 2, BF16).rearrange("p (h q) -> p h q", h=4) for _ in range(NP)]
        sq = [ca.take(TW * 2, BF16) for _ in range(2)]
        qg = [ca.take(TW * 2, BF16) for _ in range(2)]
        rstd = ca.take(TW * 4, F32)
        t1 = ca.take(TW * 4, F32)
        t2 = ca.take(TW * 4, F32)
        d1 = ca.take(256 * 4, F32)
        rdn = ca.take(256 * 4, F32)
        A_KEYS = ([("q", bq, c, j) for bq in range(2) for c in range(2) for j in range(NT)]
                  + [("k", bq, j) for bq in range(2) for j in range(NT)]
                  + [("v", i) for i in range(8)] + [("p", i) for i in range(NP)]
                  + [("sq", i) for i in range(2)] + [("qg", i) for i in range(2)]
                  + [("rstd",), ("t1",), ("t2",), ("d1",), ("rdn",)])
        cbv = Carve()
        ZW = S + 2
        zb = [cbv.take(ZW * 4, F32) for _ in range(2)]
        u_sb = cbv.take(TW * 4, F32)
        sgb = cbv.take(TW * 4, F32)
        bsb = [cbv.take(TW * 4, F32) for _ in range(2)]
        a0 = cbv.take(TW * 4, F32)
        a1 = cbv.take(TW * 4, F32)
        sq_b = [cbv.take(TW * 2, BF16) for _ in range(2)]
        rstd_b = cbv.take(TW * 4, F32)
        B_KEYS = ([("z", i, j) for i in range(2) for j in range(NT)] + [("zpad", i) for i in range(2)]
                  + [("u",), ("sg",), ("bs", 0), ("bs", 1), ("a0",), ("a1",), ("sq", 0), ("sq", 1), ("rstd",)])
        ntmp = {"sq": sq, "rs": rstd}

        def retire(keys_old, keys_new):
            toks = {}
            for k in keys_old:
                t = T.lw.pop(k, None)
                cands = list(T.rd.pop(k, {}).values())
                if t is not None:
                    cands.append(t)
                for t in cands:
                    o = toks.get(t.key)
                    if o is None or o.val < t.val:
                        toks[t.key] = t
            for k in keys_new:
                T.lw.pop(k, None)
                T.rd[k] = dict(toks)

        state = {"bank": 0, "wuse": 0, "wissued": 0}

        def nb(stream=None):
            if stream == "qk":
                b = state.get("bank_qk", 0)
                state["bank_qk"] = (b + 1) % 4
                return b
            if stream == "at":
                b = state.get("bank_at", 0)
                state["bank_at"] = (b + 1) % 4
                return 4 + b
            b = state["bank"]
            state["bank"] = (b + 1) % 8
            return b

        wplan = []

        def plan_layer(l):
            base = _layer_chunk_base(l)
            if l % 2 == 0:
                order = list(range(0, 8)) + [8, 9]
                for g in range(4):
                    order += [10 + 2 * g, 10 + 2 * g + 1, 18 + g]
                for j in range(NT):
                    order += [22 + co for co in range(8)]
            else:
                order = []
                for c in range(8):
                    order += [8 + c, 16 + c, 24 + c, c]
                for j in range(NT):
                    order += [32 + co for co in range(8)]
            return [base + o for o in order]

        for s_ in range(nseq):
            for l in layers:
                wplan.extend(plan_layer(l))

        def wprefetch(upto):
            upto = min(upto, len(wplan) - 1)
            while state["wissued"] <= upto:
                i = state["wissued"]
                slot = i % NS
                T.dma("pool", dict(out=ring[slot][:, :, :], in_=wts[wplan[i]]),
                      reads=[], writes=[("w", slot)])
                state["wissued"] += 1

        def wuse():
            i = state["wuse"]
            state["wuse"] += 1
            wprefetch(i)
            return i % NS

        def wrel():
            state["wrel"] = state.get("wrel", 0) + 1
            wprefetch(state["wrel"] + NS - 1)

        def ts(j):
            return slice(j * TW, (j + 1) * TW)

        T.dma("sp", dict(out=cv[:, :], in_=cvec[:, :]), writes=[("cv",)])
        T.dma("pool", dict(out=cb[:, :], in_=cbf[:, :]), writes=[("cb",)])
        T.dma("pool", dict(out=cs[:, :, :], in_=rope[:, :, :]), writes=[("cs",)])
        T.op("act", "activation", dict(out=esink[:, :], in_=cv[:, CV_SINK:CV_SINK + 16], func=AF.Exp),
             reads=[("cv",)], writes=[("esink",)])
        ones = cb[:, CB_ONES:CB_ONES + 128]
        bo = cb[:, CB_BO:CB_BO + 128]
        rt = cb[:, CB_RT:CB_RT + 128]
        mlo = cb[:, CB_MLO:CB_MLO + 512].rearrange("p (h q) -> p h q", h=4)
        mhi = cb[:, CB_MHI:CB_MHI + 512].rearrange("p (h q) -> p h q", h=4)
        epsc = cv[:, CV_EPS:CV_EPS + 1]

        def ogkeys(c, j):
            return [("og", c, n) for n in range(4 * j, 4 * j + 4)]

        def emit_norm(l, j):
            T.phase = "norm"
            b = nb()
            nsq, nrs = ntmp["sq"], ntmp["rs"]
            for k in range(KC):
                T.op("act", "activation", dict(out=nsq[k % 2][:, :], in_=xT[:, k, ts(j)], func=AF.Square),
                     reads=[("x", k, j)], writes=[("sq", k % 2)])
                T.op("pe", "matmul", dict(out=ps[b][:, :], lhsT=ones, rhs=nsq[k % 2][:, :],
                                          start=(k == 0), stop=(k == KC - 1)),
                     reads=[("sq", k % 2), ("cb",)], writes=[("ps", b)])
            T.op("act", "activation", dict(out=nrs[:, :], in_=ps[b][:, :], func=AF.Ln, scale=1.0 / D, bias=epsc),
                 reads=[("cv",)], writes=[("ps", b), ("rstd",)])
            T.op("act", "activation", dict(out=nrs[:, :], in_=nrs[:, :], func=AF.Exp, scale=-0.5),
                 writes=[("rstd",)])
            for k in range(KC):
                T.op("dve", "scalar_tensor_tensor",
                     dict(out=hT[:, k, ts(j)], in0=xT[:, k, ts(j)], scalar=cv[:, CV_NORM + l * 8 + k:CV_NORM + l * 8 + k + 1],
                          in1=nrs[:, :], op0=ALU.mult, op1=ALU.mult),
                     reads=[("x", k, j), ("rstd",), ("cv",)], writes=[("h", k, j)])

        def emit_proj(slot, j, b, src, srckeys):
            T.grp("pe", [("matmul", dict(out=ps[b][:, :], lhsT=ring[slot][:, k, :], rhs=src[:, k, ts(j)],
                                         start=(k == 0), stop=(k == KC - 1))) for k in range(KC)],
                  reads=[("w", slot)] + srckeys, writes=[("ps", b)])

        def hkeys(j):
            return [("h", k, j) for k in range(KC)]

        def emit_outproj(l, j):
            T.phase = "outproj"
            srck = [key for k in range(KC) for key in ogkeys(k, j)]
            for co in range(KC):
                slot = wuse()
                b = nb()
                emit_proj(slot, j, b, ogT, srck)
                wrel()
                T.op("dve", "tensor_tensor", dict(out=xT[:, co, ts(j)], in0=ps[b][:, :], in1=xT[:, co, ts(j)], op=ALU.add),
                     reads=[], writes=[("ps", b), ("x", co, j)])

        def emit_qk_stageA(item):
            T.phase = "qkA"
            slot, j, i = item["slot"], item["j"], item["i"]
            if slot is None:
                slot = item["slotref"][0] = wuse() if item["slotref"][0] is None else item["slotref"][0]
            b = i % 2
            item["b"] = b
            emit_proj(slot, j, b, hT, hkeys(j))
            if j == NT - 1:
                wrel()
            gcol = item["gcol"]
            T.op("act", "activation", dict(out=sq[i % 2][:, :], in_=ps[b][:, :], func=AF.Square),
                 writes=[("ps", b), ("sq", i % 2)])
            T.op("act", "activation", dict(out=qg[i % 2][:, :], in_=ps[b][:, :], func=AF.Copy, scale=gcol),
                 reads=[("cv",)], writes=[("ps", b), ("qg", i % 2)])

        def emit_qk_stageB(item):
            T.phase = "qkB"
            j, i, b = item["j"], item["i"], item["b"]
            gcol = item["gcol"]
            b2 = 2
            T.op("pe", "matmul", dict(out=ps[b2][:, :], lhsT=bo, rhs=sq[i % 2][:, :], start=True, stop=True),
                 reads=[("sq", i % 2), ("cb",)], writes=[("ps", b2)])
            b3 = 3
            T.op("pe", "matmul", dict(out=ps[b3][:, :], lhsT=rt, rhs=qg[i % 2][:, :], start=True, stop=True),
                 reads=[("qg", i % 2), ("cb",)], writes=[("ps", b3)])
            T.op("act", "activation", dict(out=rstd[:, :], in_=ps[b2][:, :], func=AF.Ln, scale=1.0 / 64, bias=epsc),
                 reads=[("cv",)], writes=[("ps", b2), ("rstd",)])
            T.op("act", "activation", dict(out=rstd[:, :], in_=rstd[:, :], func=AF.Exp, scale=-0.5),
                 writes=[("rstd",)])
            T.op("dve", "scalar_tensor_tensor",
                 dict(out=t1[:, :], in0=ps[b][:, :], scalar=gcol, in1=cs[:, 0, ts(j)], op0=ALU.mult, op1=ALU.mult),
                 reads=[("cv",), ("cs",)], writes=[("ps", b), ("t1",)])
            T.op("dve", "tensor_tensor", dict(out=t2[:, :], in0=ps[b3][:, :], in1=cs[:, 1, ts(j)], op=ALU.mult),
                 reads=[("cs",)], writes=[("ps", b3), ("t2",)])
            T.op("pool", "tensor_tensor", dict(out=t1[:, :], in0=t1[:, :], in1=t2[:, :], op=ALU.add),
                 reads=[("t2",)], writes=[("t1",)])
            T.op("dve", "tensor_tensor", dict(out=item["dst"], in0=t1[:, :], in1=rstd[:, :], op=ALU.mult),
                 reads=[("t1",), ("rstd",)], writes=[item["dkey"]])

        def emit_S(a, g, m):
            T.phase = "S"
            bq = g % 2
            slot = m % NP
            lo, hi = max(m - 1, 0), min(m + 1, NBLK - 1)
            qs, qe = lo * 128, (hi + 1) * 128
            off = (lo - (m - 1)) * 128
            w = qe - qs
            qkeys_j = sorted(set([qs // TW, (qe - 1) // TW]))
            for hh in range(4):
                cc, par = hh // 2, hh % 2
                rows = slice(par * 64, (par + 1) * 64)
                b = nb("at")
                T.op("pe", "matmul", dict(out=ps[b][:, off:off + w], lhsT=kr[bq][rows, m * 128:(m + 1) * 128],
                                          rhs=qr[bq][rows, cc, qs:qe], start=True, stop=True),
                     reads=[("k", bq, m // 4)] + [("q", bq, cc, jj) for jj in qkeys_j], writes=[("ps", b)])
                T.op("act", "activation", dict(out=Pt[slot][:, hh, off:off + w], in_=ps[b][:, off:off + w],
                                               func=AF.Exp, scale=0.125),
                     writes=[("ps", b), ("p", slot)])
            if m >= 1:
                T.op("dve", "tensor_tensor", dict(out=Pt[slot][:, :, 0:128], in0=Pt[slot][:, :, 0:128], in1=mlo, op=ALU.mult),
                     reads=[("cb",)], writes=[("p", slot)])
            if m <= NBLK - 2:
                T.op("dve", "tensor_tensor", dict(out=Pt[slot][:, :, 256:384], in0=Pt[slot][:, :, 256:384], in1=mhi, op=ALU.mult),
                     reads=[("cb",)], writes=[("p", slot)])

        def emit_PV(a, g, n):
            T.phase = "PV"
            b = nb("at")
            mms = [mm for mm in (n - 1, n, n + 1) if 0 <= mm < NBLK]
            instrs = []
            for kind in range(2):
                for idx, mm in enumerate(mms):
                    cbk = n - mm + 1
                    for par in range(2):
                        rows = slice(par * 64, (par + 1) * 64)
                        outv = ps[b][rows, kind * 256:(kind + 1) * 256].rearrange("p (a q) -> p a q", a=2)
                        rhs = Pt[mm % NP][:, par::2, cbk * 128:(cbk + 1) * 128]
                        lhsT = Vt[:, mm * 256 + g * 64: mm * 256 + g * 64 + 64] if kind == 0 else cb[:, CB_ONES:CB_ONES + 64]
                        instrs.append(("matmul", dict(out=outv, lhsT=lhsT, rhs=rhs, start=(idx == 0),
                                                      stop=(idx == len(mms) - 1), tile_position=(0, par * 64))))
            T.grp("pe", instrs, reads=[("p", mm % NP) for mm in mms] + [("v", mm // 2) for mm in mms] + [("cb",)],
                  writes=[("ps", b)])
            sc = (a * 4 + g) * 2
            for jj in range(2):
                T.op("act", "activation",
                     dict(out=rdn[:, jj * 128:(jj + 1) * 128], in_=ps[b][:, 256 + jj * 128:256 + (jj + 1) * 128],
                          func=AF.Ln, bias=esink[:, sc + jj:sc + jj + 1]),
                     reads=[("esink",)], writes=[("ps", b), ("rdn",)])
            T.op("act", "activation", dict(out=rdn[:, :], in_=rdn[:, :], func=AF.Exp, scale=-1.0), writes=[("rdn",)])
            T.op("dve", "tensor_tensor", dict(out=d1[:, :], in0=ps[b][:, 0:256], in1=rdn[:, :], op=ALU.mult),
                 reads=[("rdn",)], writes=[("ps", b), ("d1",)])
            ogv = ogT[:, 2 * g:2 * g + 2, n * 128:(n + 1) * 128]
            T.op("pool", "tensor_tensor", dict(out=ogv, in0=d1[:, :].rearrange("p (a q) -> p a q", a=2), in1=ogv, op=ALU.mult),
                 reads=[("d1",)], writes=[("og", 2 * g, n), ("og", 2 * g + 1, n)])

        def interleave(la, lb):
            ia = ib = 0
            while ia < len(la) or ib < len(lb):
                fa = ia / len(la) if la else 2.0
                fb = ib / len(lb) if lb else 2.0
                if ib >= len(lb) or (ia < len(la) and fa <= fb):
                    la[ia]()
                    ia += 1
                else:
                    lb[ib]()
                    ib += 1

        def emit_attn_layer(l):
            a = l // 2

            def proj_work(g):
                th = []
                bq = g % 2

                def gate_tile(c, j, ref):
                    def f():
                        T.phase = "gate"
                        if ref[0] is None:
                            ref[0] = wuse()
                        b = nb()
                        emit_proj(ref[0], j, b, hT, hkeys(j))
                        if j == NT - 1:
                            wrel()
                        T.op("act", "activation", dict(out=ogT[:, c, ts(j)], in_=ps[b][:, :], func=AF.Silu),
                             writes=[("ps", b)] + ogkeys(c, j))
                    return f
                for c in (range(KC) if g == 0 else ()):
                    ref = [None]
                    for j in range(NT):
                        th.append(gate_tile(c, j, ref))
                if g == 0:
                    vref = [None, None]

                    def v_tile(tb2):
                        def f():
                            T.phase = "V"
                            if vref[0] is None:
                                vref[0] = wuse()
                                vref[1] = wuse()
                            b = nb()
                            instrs = []
                            for t in range(2):
                                tb = 2 * tb2 + t
                                for vc in range(2):
                                    for k in range(KC):
                                        instrs.append(("matmul", dict(out=ps[b][:, t * 256 + vc * 128:t * 256 + (vc + 1) * 128],
                                                                      lhsT=hT[:, k, tb * 128:(tb + 1) * 128], rhs=ring[vref[vc]][:, k, :],
                                                                      start=(k == 0), stop=(k == KC - 1))))
                            T.grp("pe", instrs, reads=[("w", vref[0]), ("w", vref[1])] + hkeys(tb2 // 2), writes=[("ps", b)])
                            if tb2 == 7:
                                wrel()
                                wrel()
                            T.op("act", "activation", dict(out=Vt[:, tb2 * 512:(tb2 + 1) * 512], in_=ps[b][:, :], func=AF.Copy),
                                 writes=[("ps", b), ("v", tb2)])
                        return f
                    for tb2 in range(8):
                        th.append(v_tile(tb2))
                items = []
                for kind, cc in (("q", 0), ("q", 1), ("k", 0)):
                    ref = [None]
                    for j in range(NT):
                        if kind == "q":
                            dst, dkey = qr[bq][:, cc, ts(j)], ("q", bq, cc, j)
                            gcol = cv[:, CV_QG + a:CV_QG + a + 1]
                        else:
                            dst, dkey = kr[bq][:, ts(j)], ("k", bq, j)
                            gcol = cv[:, CV_KG + a:CV_KG + a + 1]
                        items.append(dict(kind=kind, slot=None, slotref=ref, j=j, i=len(items), dst=dst, dkey=dkey, gcol=gcol))

                def qk_step(i):
                    def f():
                        if i < len(items):
                            emit_qk_stageA(items[i])
                        if i >= 1:
                            emit_qk_stageB(items[i - 1])
                    return f
                for i in range(len(items) + 1):
                    th.append(qk_step(i))
                return th

            def attn_work(g):
                th = []

                def step_f(step):
                    def f():
                        if step < NBLK:
                            emit_S(a, g, step)
                        if step >= 2:
                            emit_PV(a, g, step - 2)
                    return f
                for step in range(NBLK + 2):
                    th.append(step_f(step))
                return th

            interleave(proj_work(0), [])
            for g in range(4):
                interleave(attn_work(g), proj_work(g + 1) if g + 1 < 4 else [])

        def emit_conv_layer(l):
            bi = l // 2
            T.phase = "conv"
            for c in range(KC):
                zi = c % 2
                z = zb[zi]
                slots = [wuse() for _ in range(4)]
                wc = CV_CONV + (bi * 8 + c) * 3

                def conv(j):
                    rk = [("z", zi, jj) for jj in (j - 1, j, j + 1) if 0 <= jj < NT] + [("zpad", zi)]
                    T.op("act", "activation", dict(out=a0[:, :], in_=z[:, j * TW:j * TW + TW], func=AF.Copy, scale=cv[:, wc:wc + 1]),
                         reads=rk + [("cv",)], writes=[("a0",)])
                    T.op("dve", "scalar_tensor_tensor",
                         dict(out=a1[:, :], in0=z[:, 1 + j * TW:1 + j * TW + TW], scalar=cv[:, wc + 1:wc + 2], in1=a0[:, :],
                              op0=ALU.mult, op1=ALU.add),
                         reads=rk + [("a0",), ("cv",)], writes=[("a1",)])
                    T.op("dve", "scalar_tensor_tensor",
                         dict(out=a0[:, :], in0=z[:, 2 + j * TW:2 + j * TW + TW], scalar=cv[:, wc + 2:wc + 3], in1=a1[:, :],
                              op0=ALU.mult, op1=ALU.add),
                         reads=rk + [("a1",), ("cv",)], writes=[("a0",)])
                    T.op("pool", "tensor_tensor", dict(out=ogT[:, c, ts(j)], in0=a0[:, :], in1=bsb[j % 2][:, :], op=ALU.mult),
                         reads=[("a0",), ("bs", j % 2)], writes=ogkeys(c, j))

                for j in range(NT):
                    bcg, bu, bgt, bbg = nb(), nb(), nb(), nb()
                    for si, bb in enumerate((bcg, bu, bgt, bbg)):
                        emit_proj(slots[si], j, bb, hT, hkeys(j))
                        if j == NT - 1:
                            wrel()
                    T.op("act", "activation", dict(out=u_sb[:, :], in_=ps[bu][:, :], func=AF.Copy),
                         writes=[("ps", bu), ("u",)])
                    T.op("act", "activation", dict(out=sgb[:, :], in_=ps[bgt][:, :], func=AF.Silu),
                         writes=[("ps", bgt), ("sg",)])
                    T.op("dve", "tensor_tensor", dict(out=z[:, 1 + j * TW:1 + (j + 1) * TW], in0=ps[bcg][:, :], in1=u_sb[:, :], op=ALU.mult),
                         reads=[("u",)], writes=[("ps", bcg), ("z", zi, j)])
                    T.op("dve", "tensor_tensor", dict(out=bsb[j % 2][:, :], in0=ps[bbg][:, :], in1=sgb[:, :], op=ALU.mult),
                         reads=[("sg",)], writes=[("ps", bbg), ("bs", j % 2)])
                    if j >= 1:
                        conv(j - 1)
                conv(NT - 1)

        cur_scratch = None

        def set_scratch(kind):
            nonlocal cur_scratch
            if cur_scratch == kind:
                return
            old = A_KEYS if cur_scratch == "A" else (B_KEYS if cur_scratch == "B" else [])
            new = A_KEYS if kind == "A" else B_KEYS
            retire(old, new)
            cur_scratch = kind
            ntmp["sq"], ntmp["rs"] = (sq, rstd) if kind == "A" else (sq_b, rstd_b)
            if kind == "B":
                for i in range(2):
                    T.op("pool", "memset", dict(ap=zb[i][:, 0:1], constant=0.0), writes=[("zpad", i)])
                    T.op("pool", "memset", dict(ap=zb[i][:, ZW - 1:ZW], constant=0.0), writes=[("zpad", i)])

        for s_ in range(nseq):
            for j in range(NT):
                for k in range(KC):
                    T.dma("sp", dict(out=xT[:, k, ts(j)], in_=xin[s_, k * 128:(k + 1) * 128, ts(j)]),
                          writes=[("x", k, j)])
            for li, l in enumerate(layers):
                if li == 0:
                    if cur_scratch is None:
                        set_scratch("A" if l % 2 == 0 else "B")
                    for j in range(NT):
                        emit_norm(l, j)
                set_scratch("A" if l % 2 == 0 else "B")
                if l % 2 == 0:
                    emit_attn_layer(l)
                else:
                    emit_conv_layer(l)
                nxt = layers[li + 1] if li + 1 < len(layers) else None
                for step in range(NT + 1):
                    if step < NT:
                        emit_outproj(l, step)
                    if step >= 1:
                        j = step - 1
                        if nxt is not None:
                            emit_norm(nxt, j)
                        else:
                            for k in range(KC):
                                T.dma("sp", dict(out=yout[s_, k * 128:(k + 1) * 128, ts(j)], in_=xT[:, k, ts(j)]),
                                      reads=[("x", k, j)])
        if debug:
            T.dma("sp", dict(out=dbg_h[:, :, :], in_=hT[:, :, :]), reads=[("h", k, j) for k in range(KC) for j in range(NT)])
            T.dma("sp", dict(out=dbg_og[:, :, :], in_=ogT[:, :, :]), reads=[("og", k, n) for k in range(KC) for n in range(NBLK)])
        T.finish("sp")
        T.finish("pool")
        T.finish("act")
        T.finish("dve")
        T.finish("pe")
    return nc


def _chunk(wcols):
    return np.ascontiguousarray(wcols.reshape(KC, 128, 128).transpose(1, 0, 2))


def _prep_weights(a_w_in, a_w_out, b_w_in, b_w_out):
    out = np.empty((NCH, 128, KC, 128), np.float32)
    for l in range(DEPTH):
        base = _layer_chunk_base(l)
        s = l // 2
        if l % 2 == 0:
            w = a_w_in[s]
            for c in range(8):
                out[base + c] = _chunk(w[:, 1536 + c * 128:1536 + (c + 1) * 128])
            for vc in range(2):
                out[base + 8 + vc] = _chunk(w[:, 1280 + vc * 128:1280 + (vc + 1) * 128])
            for c in range(8):
                out[base + 10 + c] = _chunk(w[:, c * 128:(c + 1) * 128])
            for g in range(4):
                kg = w[:, 1024 + g * 64:1024 + (g + 1) * 64]
                out[base + 18 + g] = _chunk(np.concatenate([kg, kg], axis=1))
            for co in range(8):
                out[base + 22 + co] = _chunk(a_w_out[s][:, co * 128:(co + 1) * 128])
        else:
            w = b_w_in[s]
            for c in range(32):
                out[base + c] = _chunk(w[:, c * 128:(c + 1) * 128])
            for co in range(8):
                out[base + 32 + co] = _chunk(b_w_out[s][:, co * 128:(co + 1) * 128])
    return out


def _prep_consts(norm_g, a_q_norm, a_k_norm, a_sink, b_conv):
    cvec = np.zeros((128, NCV), np.float32)
    p = np.arange(128)
    for l in range(DEPTH):
        for k in range(KC):
            cvec[:, CV_NORM + l * 8 + k] = norm_g[l, k * 128:(k + 1) * 128]
    for a in range(2):
        cvec[:, CV_QG + a] = a_q_norm[a][p % 64]
        cvec[:, CV_KG + a] = a_k_norm[a][p % 64]
        for g in range(4):
            for j in range(2):
                cvec[:, CV_SINK + (a * 4 + g) * 2 + j] = a_sink[a][4 * g + 2 * j + (p >= 64)]
    for b in range(2):
        for c in range(KC):
            for t in range(3):
                cvec[:, CV_CONV + (b * 8 + c) * 3 + t] = b_conv[b, t, c * 128:(c + 1) * 128]
    cvec[:, CV_EPS] = EPS

    cbf = np.zeros((128, NCB), np.float32)
    cbf[:, CB_ONES:CB_ONES + 128] = 1.0
    cbf[:, CB_BO:CB_BO + 128] = (p[:, None] // 64 == p[None, :] // 64)
    rtm = np.zeros((128, 128), np.float32)
    for i in range(128):
        if i % 64 < 32:
            rtm[i + 32, i] = -1.0
        else:
            rtm[i - 32, i] = 1.0
    cbf[:, CB_RT:CB_RT + 128] = rtm
    mlo = (p[:, None] <= p[None, :]).astype(np.float32)
    mhi = (p[None, :] <= p[:, None]).astype(np.float32)
    cbf[:, CB_MLO:CB_MLO + 512] = np.tile(mlo, (1, 4))
    cbf[:, CB_MHI:CB_MHI + 512] = np.tile(mhi, (1, 4))

    inv_freq = (10000.0 ** (-np.arange(0, 64, 2, dtype=np.float32) / 64)).astype(np.float32)
    ang = np.arange(S, dtype=np.float32)[:, None] * inv_freq[None, :]
    cosT = np.cos(ang).astype(np.float32).T
    sinT = np.sin(ang).astype(np.float32).T
    rope = np.empty((128, 2, S), np.float32)
    rope[:, 0, :] = cosT[p % 32]
    rope[:, 1, :] = sinT[p % 32]
    return cvec, cbf, rope


def _run(inputs, layers, seqs=None, ncores=NCORES, debug=False):
    x = np.asarray(inputs["x"], np.float32)
    wts = _prep_weights(np.asarray(inputs["a_w_in"], np.float32), np.asarray(inputs["a_w_out"], np.float32),
                        np.asarray(inputs["b_w_in"], np.float32), np.asarray(inputs["b_w_out"], np.float32))
    cvec, cbf, rope = _prep_consts(np.asarray(inputs["norm_g"], np.float32), np.asarray(inputs["a_q_norm"], np.float32),
                                   np.asarray(inputs["a_k_norm"], np.float32), np.asarray(inputs["a_sink"], np.float32),
                                   np.asarray(inputs["b_conv"], np.float32))
    nseq = SEQ_PER_CORE if seqs is None else seqs
    xT = np.ascontiguousarray(x.transpose(0, 2, 1))
    import time as _t
    _t0 = _t.time()
    nc = build_program(list(layers), nseq=nseq, debug=debug)
    print("[kernel] build %.1fs" % (_t.time() - _t0), flush=True)
    in_maps = []
    for c in range(ncores):
        in_maps.append({"xin": np.ascontiguousarray(xT[c * nseq:(c + 1) * nseq]), "wts": wts, "cvec": cvec, "cbf": cbf, "rope": rope})
    res = run_bass_kernel_spmd(nc, in_maps, core_ids=list(range(ncores)))
    if debug:
        return res
    outT = np.concatenate([r["yout"] for r in res.results], axis=0)
    return np.ascontiguousarray(outT.transpose(0, 2, 1)).astype(np.float32)


def kernel(x, norm_g, a_w_in, a_q_norm, a_k_norm, a_sink, a_w_out, b_w_in, b_conv, b_w_out):
    inputs = dict(x=x, norm_g=norm_g, a_w_in=a_w_in, a_q_norm=a_q_norm, a_k_norm=a_k_norm, a_sink=a_sink,
                  a_w_out=a_w_out, b_w_in=b_w_in, b_conv=b_conv, b_w_out=b_w_out)
    return _run(inputs, layers=list(range(DEPTH)))
```

```python
import contextlib
import numpy as np
import concourse.bass as bass
import concourse.mybir as mybir
from concourse.bass_utils import run_bass_kernel_spmd

F32 = mybir.dt.float32
BF16 = mybir.dt.bfloat16
ALU = mybir.AluOpType
AF = mybir.ActivationFunctionType

D = 1024
S = 2048
BATCH = 16
NCORES = 8
SEQ_PER_CORE = BATCH // NCORES
DEPTH = 4
NT = 4
TW = 512
KC = 8
NBLK = 16
EPS = 1e-6
NS = 6
NP = 4
NDS = 24

CV_NORM = 0
CV_QG = 32
CV_KG = 34
CV_CONV = 36
CV_SINK = 84
CV_EPS = 100
CV_QGP = 101
CV_KGP = 103
NCV = 105
CB_ONES = 0
CB_BO = 128
CB_RT = 256
CB_MLO = 384
CB_MHI = 896
NCB = 1408

A_CH = 30
B_CH = 40


def _layer_chunk_base(l):
    base = 0
    for i in range(l):
        base += A_CH if i % 2 == 0 else B_CH
    return base


NCH = _layer_chunk_base(DEPTH)


class Tok:
    __slots__ = ("sem", "val", "eng", "key")

    def __init__(self, sem, val, eng, key):
        self.sem, self.val, self.eng, self.key = sem, val, eng, key


class Trk:
    def __init__(self, nc, es):
        self.nc = nc
        self.eng = {"pe": nc.tensor, "act": nc.scalar, "dve": nc.vector,
                    "pool": nc.gpsimd, "sp": nc.sync}
        self.sem = {e: es.enter_context(nc.semaphore("s_" + e))
                    for e in ("pe", "act", "dve", "pool")}
        self.cnt = {e: 0 for e in self.sem}
        self.dsem = [es.enter_context(nc.semaphore("s_dma%d" % i)) for i in range(NDS)]
        self.dcnt = [0] * NDS
        self.dnext = {"pool": 0, "sp": 0}
        self.waited = {e: {} for e in self.eng}
        self.lw = {}
        self.rd = {}
        self.nwaits = 0
        self.phase = ""
        self.annotate = False

    def _wait(self, e, tok):
        if tok.eng == "pe" and e == "pe":
            return
        w = self.waited[e]
        if w.get(tok.key, 0) >= tok.val:
            return
        w[tok.key] = tok.val
        self.eng[e].wait_ge(tok.sem, tok.val)
        self.nwaits += 1

    def _deps(self, e, reads, writes):
        for r in reads:
            t = self.lw.get(r)
            if t is not None:
                self._wait(e, t)
        for w in writes:
            t = self.lw.get(w)
            if t is not None:
                self._wait(e, t)
            for t in self.rd.get(w, {}).values():
                self._wait(e, t)

    def _commit(self, tok, reads, writes):
        for r in reads:
            d = self.rd.setdefault(r, {})
            o = d.get(tok.key)
            if o is None or o.val < tok.val:
                d[tok.key] = tok
        for w in writes:
            self.lw[w] = tok
            self.rd[w] = {}

    def grp(self, e, instrs, reads=(), writes=()):
        self._deps(e, reads, writes)
        eng = self.eng[e]
        ins = None
        for name, kw in instrs:
            ins = getattr(eng, name)(**kw)
            if self.annotate:
                ins.annotate(self.phase)
        self.cnt[e] += 1
        ins.then_inc(self.sem[e], 1)
        tok = Tok(self.sem[e], self.cnt[e], e, e)
        self._commit(tok, reads, writes)
        return tok

    def op(self, e, name, kw, reads=(), writes=()):
        return self.grp(e, [(name, kw)], reads, writes)

    def dma(self, e, kw, reads=(), writes=()):
        half = NDS // 2
        j = self.dnext[e]
        self.dnext[e] = (j + 1) % half
        i = j + (half if e == "pool" else 0)
        key = ("d", i)
        if self.dcnt[i] > 0:
            self._wait(e, Tok(self.dsem[i], 16 * self.dcnt[i], "dma", key))
        self._deps(e, reads, writes)
        ins = self.eng[e].dma_start(**kw)
        if self.annotate:
            ins.annotate(self.phase + "/dma")
        self.dcnt[i] += 1
        ins.then_inc(self.dsem[i], 16)
        tok = Tok(self.dsem[i], 16 * self.dcnt[i], "dma", key)
        self._commit(tok, reads, writes)
        return tok

    def finish(self, e="sp"):
        for x in self.sem:
            if self.cnt[x] > 0:
                self._wait(e, Tok(self.sem[x], self.cnt[x], x, x))
        for i in range(NDS):
            if self.dcnt[i] > 0:
                self._wait(e, Tok(self.dsem[i], 16 * self.dcnt[i], "dma", ("d", i)))


def build_program(layers, nseq=SEQ_PER_CORE, debug=False, annotate=False):
    nc = bass.Bass("TRN2", target_bir_lowering=False)
    xin = nc.dram_tensor("xin", [nseq, D, S], F32, kind="ExternalInput").ap()
    wts = nc.dram_tensor("wts", [NCH, 128, KC, 128], F32, kind="ExternalInput").ap()
    cvec = nc.dram_tensor("cvec", [128, NCV], F32, kind="ExternalInput").ap()
    cbf = nc.dram_tensor("cbf", [128, NCB], F32, kind="ExternalInput").ap()
    rope = nc.dram_tensor("rope", [128, 2, S], F32, kind="ExternalInput").ap()
    yout = nc.dram_tensor("yout", [nseq, D, S], F32, kind="ExternalOutput").ap()
    if debug:
        dbg_h = nc.dram_tensor("dbg_h", [128, KC, S], BF16, kind="ExternalOutput").ap()
        dbg_og = nc.dram_tensor("dbg_og", [128, KC, S], BF16, kind="ExternalOutput").ap()

    es = contextlib.ExitStack()
    with es:
        def sb(name, shape, dt):
            return es.enter_context(nc.sbuf_tensor(name, shape, dt))

        xT = sb("xT", [128, KC, S], F32)
        hT = sb("hT", [128, KC, S], BF16)
        ogT = sb("ogT", [128, KC, S], BF16)
        ring = [sb("ring%d" % i, [128, KC, 128], BF16) for i in range(NS)]
        cs = sb("cs", [128, 2, S], BF16)
        cv = sb("cv", [128, NCV], F32)
        esink = sb("esink", [128, 16], F32)
        cb = sb("cb", [128, NCB], BF16)
        SCR_BYTES = 56 * 1024
        scr = sb("scr", [128, SCR_BYTES // 2], BF16)
        ps = [es.enter_context(nc.psum_tensor("ps%d" % i, [128, TW], F32)) for i in range(8)]

        T = Trk(nc, es)
        T.annotate = annotate

        class Carve:
            def __init__(self):
                self.off = 0

            def take(self, nbytes, dt):
                nbytes = (nbytes + 31) // 32 * 32
                o = self.off
                self.off += nbytes
                assert self.off <= SCR_BYTES, (self.off, SCR_BYTES)
                v = scr[:, o // 2:(o + nbytes) // 2]
                if dt == F32:
                    v = v.bitcast(F32)
                return v

        ca = Carve()
        qr = [ca.take(2 * S * 2, BF16).rearrange("p (c t) -> p c t", c=2) for _ in range(2)]
        kr = [ca.take(S * 2, BF16) for _ in range(2)]
        Vt = ca.take(NBLK * 256 * 2, BF16)
        Pt = [ca.take(4 * 384 * 2, BF16).rearrange("p (h q) -> p h q", h=4) for _ in range(NP)]
        sq = [ca.take(TW * 2, BF16) for _ in range(2)]
        qg = [ca.take(TW * 2, BF16) for _ in range(2)]
        rstd = ca.take(TW * 4, F32)
        t1 = ca.take(TW * 4, F32)
        t2 = ca.take(TW * 4, F32)
        d1 = ca.take(256 * 4, F32)
        rdn = ca.take(256 * 4, F32)
        A_KEYS = ([("q", bq, c, j) for bq in range(2) for c in range(2) for j in range(NT)]
                  + [("k", bq, j) for bq in range(2) for j in range(NT)]
                  + [("v", i) for i in range(8)] + [("p", i) for i in range(NP)]
                  + [("sq", i) for i in range(2)] + [("qg", i) for i in range(2)]
                  + [("rstd",), ("t1",), ("t2",), ("d1",), ("rdn",)])
        cbv = Carve()
        ZW = S + 2
        zb = [cbv.take(ZW * 4, F32) for _ in range(2)]
        u_sb = cbv.take(TW * 4, F32)
        sgb = cbv.take(TW * 4, F32)
        bsb = [cbv.take(TW * 4, F32) for _ in range(2)]
        a0 = cbv.take(TW * 4, F32)
        a1 = cbv.take(TW * 4, F32)
        sq_b = [cbv.take(TW * 2, BF16) for _ in range(2)]
        rstd_b = cbv.take(TW * 4, F32)
        B_KEYS = ([("z", i, j) for i in range(2) for j in range(NT)] + [("zpad", i) for i in range(2)]
                  + [("u",), ("sg",), ("bs", 0), ("bs", 1), ("a0",), ("a1",), ("sq", 0), ("sq", 1), ("rstd",)])
        ntmp = {"sq": sq, "rs": rstd}

        def retire(keys_old, keys_new):
            toks = {}
            for k in keys_old:
                t = T.lw.pop(k, None)
                cands = list(T.rd.pop(k, {}).values())
                if t is not None:
                    cands.append(t)
                for t in cands:
                    o = toks.get(t.key)
                    if o is None or o.val < t.val:
                        toks[t.key] = t
            for k in keys_new:
                T.lw.pop(k, None)
                T.rd[k] = dict(toks)

        state = {"bank": 0, "wuse": 0, "wissued": 0}

        def nb(stream=None):
            if stream == "qk":
                b = state.get("bank_qk", 0)
                state["bank_qk"] = (b + 1) % 4
                return b
            if stream == "at":
                b = state.get("bank_at", 0)
                state["bank_at"] = (b + 1) % 4
                return 4 + b
            b = state["bank"]
            state["bank"] = (b + 1) % 8
            return b

        wplan = []

        def plan_layer(l):
            base = _layer_chunk_base(l)
            if l % 2 == 0:
                order = list(range(0, 8)) + [8, 9]
                for g in range(4):
                    order += [10 + 2 * g, 10 + 2 * g + 1, 18 + g]
                for j in range(NT):
                    order += [22 + co for co in range(8)]
            else:
                order = []
                for c in range(8):
                    order += [8 + c, 16 + c, 24 + c, c]
                for j in range(NT):
                    order += [32 + co for co in range(8)]
            return [base + o for o in order]

        for s_ in range(nseq):
            for l in layers:
                wplan.extend(plan_layer(l))

        def wprefetch(upto):
            upto = min(upto, len(wplan) - 1)
            while state["wissued"] <= upto:
                i = state["wissued"]
                slot = i % NS
                T.dma("pool", dict(out=ring[slot][:, :, :], in_=wts[wplan[i]]),
                      reads=[], writes=[("w", slot)])
                state["wissued"] += 1

        def wuse():
            i = state["wuse"]
            state["wuse"] += 1
            wprefetch(i)
            return i % NS

        def wrel():
            state["wrel"] = state.get("wrel", 0) + 1
            wprefetch(state["wrel"] + NS - 1)

        def ts(j):
            return slice(j * TW, (j + 1) * TW)

        T.dma("sp", dict(out=cv[:, :], in_=cvec[:, :]), writes=[("cv",)])
        T.dma("pool", dict(out=cb[:, :], in_=cbf[:, :]), writes=[("cb",)])
        T.dma("pool", dict(out=cs[:, :, :], in_=rope[:, :, :]), writes=[("cs",)])
        T.op("act", "activation", dict(out=esink[:, :], in_=cv[:, CV_SINK:CV_SINK + 16], func=AF.Exp),
             reads=[("cv",)], writes=[("esink",)])
        ones = cb[:, CB_ONES:CB_ONES + 128]
        bo = cb[:, CB_BO:CB_BO + 128]
        rt = cb[:, CB_RT:CB_RT + 128]
        mlo = cb[:, CB_MLO:CB_MLO + 512].rearrange("p (h q) -> p h q", h=4)
        mhi = cb[:, CB_MHI:CB_MHI + 512].rearrange("p (h q) -> p h q", h=4)
        epsc = cv[:, CV_EPS:CV_EPS + 1]

        def ogkeys(c, j):
            return [("og", c, n) for n in range(4 * j, 4 * j + 4)]

        def emit_norm(l, j):
            T.phase = "norm"
            b = nb()
            nsq, nrs = ntmp["sq"], ntmp["rs"]
            for k in range(KC):
                T.op("act", "activation", dict(out=nsq[k % 2][:, :], in_=xT[:, k, ts(j)], func=AF.Square),
                     reads=[("x", k, j)], writes=[("sq", k % 2)])
                T.op("pe", "matmul", dict(out=ps[b][:, :], lhsT=ones, rhs=nsq[k % 2][:, :],
                                          start=(k == 0), stop=(k == KC - 1)),
                     reads=[("sq", k % 2), ("cb",)], writes=[("ps", b)])
            T.op("act", "activation", dict(out=nrs[:, :], in_=ps[b][:, :], func=AF.Ln, scale=1.0 / D, bias=epsc),
                 reads=[("cv",)], writes=[("ps", b), ("rstd",)])
            T.op("act", "activation", dict(out=nrs[:, :], in_=nrs[:, :], func=AF.Exp, scale=-0.5),
                 writes=[("rstd",)])
            for k in range(KC):
                T.op("dve", "scalar_tensor_tensor",
                     dict(out=hT[:, k, ts(j)], in0=xT[:, k, ts(j)], scalar=cv[:, CV_NORM + l * 8 + k:CV_NORM + l * 8 + k + 1],
                          in1=nrs[:, :], op0=ALU.mult, op1=ALU.mult),
                     reads=[("x", k, j), ("rstd",), ("cv",)], writes=[("h", k, j)])

        def emit_proj(slot, j, b, src, srckeys):
            T.grp("pe", [("matmul", dict(out=ps[b][:, :], lhsT=ring[slot][:, k, :], rhs=src[:, k, ts(j)],
                                         start=(k == 0), stop=(k == KC - 1))) for k in range(KC)],
                  reads=[("w", slot)] + srckeys, writes=[("ps", b)])

        def hkeys(j):
            return [("h", k, j) for k in range(KC)]

        def emit_outproj(l, j):
            T.phase = "outproj"
            srck = [key for k in range(KC) for key in ogkeys(k, j)]
            for co in range(KC):
                slot = wuse()
                b = nb()
                emit_proj(slot, j, b, ogT, srck)
                wrel()
                T.op("dve", "tensor_tensor", dict(out=xT[:, co, ts(j)], in0=ps[b][:, :], in1=xT[:, co, ts(j)], op=ALU.add),
                     reads=[], writes=[("ps", b), ("x", co, j)])

        def emit_qk_stageA(item):
            T.phase = "qkA"
            slot, j, i = item["slot"], item["j"], item["i"]
            if slot is None:
                slot = item["slotref"][0] = wuse() if item["slotref"][0] is None else item["slotref"][0]
            b = i % 2
            item["b"] = b
            emit_proj(slot, j, b, hT, hkeys(j))
            if j == NT - 1:
                wrel()
            T.op("dve", "tensor_copy", dict(out=qg[i % 2][:, :], in_=ps[b][:, :]),
                 writes=[("ps", b), ("qg", i % 2)])
            T.op("dve", "tensor_tensor", dict(out=sq[i % 2][:, :], in0=qg[i % 2][:, :], in1=qg[i % 2][:, :], op=ALU.mult),
                 reads=[("qg", i % 2)], writes=[("sq", i % 2)])

        def emit_qk_stageB(item):
            T.phase = "qkB"
            j, i, b = item["j"], item["i"], item["b"]
            gcol = item["gcol"]
            b2 = 2
            T.op("pe", "matmul", dict(out=ps[b2][:, :], lhsT=bo, rhs=sq[i % 2][:, :], start=True, stop=True),
                 reads=[("sq", i % 2), ("cb",)], writes=[("ps", b2)])
            b3 = 3
            T.op("pe", "matmul", dict(out=ps[b3][:, :], lhsT=rt, rhs=qg[i % 2][:, :], start=True, stop=True),
                 reads=[("qg", i % 2), ("cb",)], writes=[("ps", b3)])
            T.op("act", "activation", dict(out=rstd[:, :], in_=ps[b2][:, :], func=AF.Ln, scale=1.0 / 64, bias=epsc),
                 reads=[("cv",)], writes=[("ps", b2), ("rstd",)])
            T.op("act", "activation", dict(out=rstd[:, :], in_=rstd[:, :], func=AF.Exp, scale=-0.5),
                 writes=[("rstd",)])
            T.op("dve", "scalar_tensor_tensor",
                 dict(out=t1[:, :], in0=ps[b][:, :], scalar=gcol, in1=cs[:, 0, ts(j)], op0=ALU.mult, op1=ALU.mult),
                 reads=[("cv",), ("cs",)], writes=[("ps", b), ("t1",)])
            T.op("dve", "scalar_tensor_tensor",
                 dict(out=t2[:, :], in0=ps[b3][:, :], scalar=item["gpcol"], in1=cs[:, 1, ts(j)], op0=ALU.mult, op1=ALU.mult),
                 reads=[("cv",), ("cs",)], writes=[("ps", b3), ("t2",)])
            T.op("pool", "tensor_tensor", dict(out=t1[:, :], in0=t1[:, :], in1=t2[:, :], op=ALU.add),
                 reads=[("t2",)], writes=[("t1",)])
            T.op("pool", "tensor_tensor", dict(out=item["dst"], in0=t1[:, :], in1=rstd[:, :], op=ALU.mult),
                 reads=[("t1",), ("rstd",)], writes=[item["dkey"]])

        def emit_S(a, g, m):
            T.phase = "S"
            bq = g % 2
            slot = m % NP
            lo, hi = max(m - 1, 0), min(m + 1, NBLK - 1)
            qs, qe = lo * 128, (hi + 1) * 128
            off = (lo - (m - 1)) * 128
            w = qe - qs
            qkeys_j = sorted(set([qs // TW, (qe - 1) // TW]))
            banks = [nb("at") for _ in range(4)]
            for hh in range(4):
                cc, par = hh // 2, hh % 2
                rows = slice(par * 64, (par + 1) * 64)
                b = banks[hh]
                T.op("pe", "matmul", dict(out=ps[b][:, off:off + w], lhsT=kr[bq][rows, m * 128:(m + 1) * 128],
                                          rhs=qr[bq][rows, cc, qs:qe], start=True, stop=True),
                     reads=[("k", bq, m // 4)] + [("q", bq, cc, jj) for jj in qkeys_j], writes=[("ps", b)])
            for hh in range(4):
                b = banks[hh]
                T.op("act", "activation", dict(out=Pt[slot][:, hh, off:off + w], in_=ps[b][:, off:off + w],
                                               func=AF.Exp, scale=0.125),
                     writes=[("ps", b), ("p", slot)])
            if True:
                if m >= 1:
                    T.op("dve", "tensor_tensor", dict(out=Pt[slot][:, :, 0:128], in0=Pt[slot][:, :, 0:128], in1=mlo, op=ALU.mult),
                         reads=[("cb",)], writes=[("p", slot)])
                if m <= NBLK - 2:
                    T.op("dve", "tensor_tensor", dict(out=Pt[slot][:, :, 256:384], in0=Pt[slot][:, :, 256:384], in1=mhi, op=ALU.mult),
                         reads=[("cb",)], writes=[("p", slot)])

        def emit_PV(a, g, n):
            T.phase = "PV"
            b = nb("at")
            mms = [mm for mm in (n - 1, n, n + 1) if 0 <= mm < NBLK]
            instrs = []
            for kind in range(2):
                for idx, mm in enumerate(mms):
                    cbk = n - mm + 1
                    for par in range(2):
                        rows = slice(par * 64, (par + 1) * 64)
                        outv = ps[b][rows, kind * 256:(kind + 1) * 256].rearrange("p (a q) -> p a q", a=2)
                        rhs = Pt[mm % NP][:, par::2, cbk * 128:(cbk + 1) * 128]
                        lhsT = Vt[:, mm * 256 + g * 64: mm * 256 + g * 64 + 64] if kind == 0 else cb[:, CB_ONES:CB_ONES + 64]
                        instrs.append(("matmul", dict(out=outv, lhsT=lhsT, rhs=rhs, start=(idx == 0),
                                                      stop=(idx == len(mms) - 1), tile_position=(0, par * 64))))
            T.grp("pe", instrs, reads=[("p", mm % NP) for mm in mms] + [("v", mm // 2) for mm in mms] + [("cb",)],
                  writes=[("ps", b)])
            sc = (a * 4 + g) * 2
            for jj in range(2):
                T.op("act", "activation",
                     dict(out=rdn[:, jj * 128:(jj + 1) * 128], in_=ps[b][:, 256 + jj * 128:256 + (jj + 1) * 128],
                          func=AF.Ln, bias=esink[:, sc + jj:sc + jj + 1]),
                     reads=[("esink",)], writes=[("ps", b), ("rdn",)])
            T.op("act", "activation", dict(out=rdn[:, :], in_=rdn[:, :], func=AF.Exp, scale=-1.0), writes=[("rdn",)])
            T.op("dve", "tensor_tensor", dict(out=d1[:, :], in0=ps[b][:, 0:256], in1=rdn[:, :], op=ALU.mult),
                 reads=[("rdn",)], writes=[("ps", b), ("d1",)])
            ogv = ogT[:, 2 * g:2 * g + 2, n * 128:(n + 1) * 128]
            T.op("pool", "tensor_tensor", dict(out=ogv, in0=d1[:, :].rearrange("p (a q) -> p a q", a=2), in1=ogv, op=ALU.mult),
                 reads=[("d1",)], writes=[("og", 2 * g, n), ("og", 2 * g + 1, n)])

        def interleave(la, lb):
            ia = ib = 0
            while ia < len(la) or ib < len(lb):
                fa = ia / len(la) if la else 2.0
                fb = ib / len(lb) if lb else 2.0
                if ib >= len(lb) or (ia < len(la) and fa <= fb):
                    la[ia]()
                    ia += 1
                else:
                    lb[ib]()
                    ib += 1

        def emit_attn_layer(l):
            a = l // 2

            def proj_work(g):
                th = []
                bq = g % 2

                def gate_tile(c, j, ref):
                    def f():
                        T.phase = "gate"
                        if ref[0] is None:
                            ref[0] = wuse()
                        b = nb()
                        emit_proj(ref[0], j, b, hT, hkeys(j))
                        if j == NT - 1:
                            wrel()
                        T.op("act", "activation", dict(out=ogT[:, c, ts(j)], in_=ps[b][:, :], func=AF.Silu),
                             writes=[("ps", b)] + ogkeys(c, j))
                    return f
                for c in (range(KC) if g == 0 else ()):
                    ref = [None]
                    for j in range(NT):
                        th.append(gate_tile(c, j, ref))
                if g == 0:
                    vref = [None, None]

                    def v_tile(tb2):
                        def f():
                            T.phase = "V"
                            if vref[0] is None:
                                vref[0] = wuse()
                                vref[1] = wuse()
                            b = nb()
                            instrs = []
                            for t in range(2):
                                tb = 2 * tb2 + t
                                for vc in range(2):
                                    for k in range(KC):
                                        instrs.append(("matmul", dict(out=ps[b][:, t * 256 + vc * 128:t * 256 + (vc + 1) * 128],
                                                                      lhsT=hT[:, k, tb * 128:(tb + 1) * 128], rhs=ring[vref[vc]][:, k, :],
                                                                      start=(k == 0), stop=(k == KC - 1))))
                            T.grp("pe", instrs, reads=[("w", vref[0]), ("w", vref[1])] + hkeys(tb2 // 2), writes=[("ps", b)])
                            if tb2 == 7:
                                wrel()
                                wrel()
                            T.op("act", "activation", dict(out=Vt[:, tb2 * 512:(tb2 + 1) * 512], in_=ps[b][:, :], func=AF.Copy),
                                 writes=[("ps", b), ("v", tb2)])
                        return f
                    for tb2 in range(8):
                        th.append(v_tile(tb2))
                items = []
                for kind, cc in (("q", 0), ("q", 1), ("k", 0)):
                    ref = [None]
                    for j in range(NT):
                        if kind == "q":
                            dst, dkey = qr[bq][:, cc, ts(j)], ("q", bq, cc, j)
                            gcol = cv[:, CV_QG + a:CV_QG + a + 1]
                            gpcol = cv[:, CV_QGP + a:CV_QGP + a + 1]
                        else:
                            dst, dkey = kr[bq][:, ts(j)], ("k", bq, j)
                            gcol = cv[:, CV_KG + a:CV_KG + a + 1]
                            gpcol = cv[:, CV_KGP + a:CV_KGP + a + 1]
                        items.append(dict(kind=kind, slot=None, slotref=ref, j=j, i=len(items), dst=dst, dkey=dkey, gcol=gcol, gpcol=gpcol))

                def qk_step(i):
                    def f():
                        if i < len(items):
                            emit_qk_stageA(items[i])
                        if i >= 1:
                            emit_qk_stageB(items[i - 1])
                    return f
                for i in range(len(items) + 1):
                    th.append(qk_step(i))
                return th

            def attn_work(g):
                th = []

                def step_f(step):
                    def f():
                        if step < NBLK:
                            emit_S(a, g, step)
                        if step >= 2:
                            emit_PV(a, g, step - 2)
                    return f
                for step in range(NBLK + 2):
                    th.append(step_f(step))
                return th

            interleave(proj_work(0), [])
            for g in range(4):
                interleave(attn_work(g), proj_work(g + 1) if g + 1 < 4 else [])

        def emit_conv_layer(l):
            bi = l // 2
            T.phase = "conv"
            for c in range(KC):
                zi = c % 2
                z = zb[zi]
                slots = [wuse() for _ in range(4)]
                wc = CV_CONV + (bi * 8 + c) * 3

                def conv(j):
                    rk = [("z", zi, jj) for jj in (j - 1, j, j + 1) if 0 <= jj < NT] + [("zpad", zi)]
                    T.op("act", "activation", dict(out=a0[:, :], in_=z[:, j * TW:j * TW + TW], func=AF.Copy, scale=cv[:, wc:wc + 1]),
                         reads=rk + [("cv",)], writes=[("a0",)])
                    T.op("dve", "scalar_tensor_tensor",
                         dict(out=a1[:, :], in0=z[:, 1 + j * TW:1 + j * TW + TW], scalar=cv[:, wc + 1:wc + 2], in1=a0[:, :],
                              op0=ALU.mult, op1=ALU.add),
                         reads=rk + [("a0",), ("cv",)], writes=[("a1",)])
                    T.op("dve", "scalar_tensor_tensor",
                         dict(out=a0[:, :], in0=z[:, 2 + j * TW:2 + j * TW + TW], scalar=cv[:, wc + 2:wc + 3], in1=a1[:, :],
                              op0=ALU.mult, op1=ALU.add),
                         reads=rk + [("a1",), ("cv",)], writes=[("a0",)])
                    T.op("pool", "tensor_tensor", dict(out=ogT[:, c, ts(j)], in0=a0[:, :], in1=bsb[j % 2][:, :], op=ALU.mult),
                         reads=[("a0",), ("bs", j % 2)], writes=ogkeys(c, j))

                for j in range(NT):
                    bcg, bu, bgt, bbg = nb(), nb(), nb(), nb()
                    for si, bb in enumerate((bcg, bu, bgt, bbg)):
                        emit_proj(slots[si], j, bb, hT, hkeys(j))
                        if j == NT - 1:
                            wrel()
                    T.op("act", "activation", dict(out=u_sb[:, :], in_=ps[bu][:, :], func=AF.Copy),
                         writes=[("ps", bu), ("u",)])
                    T.op("act", "activation", dict(out=sgb[:, :], in_=ps[bgt][:, :], func=AF.Silu),
                         writes=[("ps", bgt), ("sg",)])
                    T.op("dve", "tensor_tensor", dict(out=z[:, 1 + j * TW:1 + (j + 1) * TW], in0=ps[bcg][:, :], in1=u_sb[:, :], op=ALU.mult),
                         reads=[("u",)], writes=[("ps", bcg), ("z", zi, j)])
                    T.op("dve", "tensor_tensor", dict(out=bsb[j % 2][:, :], in0=ps[bbg][:, :], in1=sgb[:, :], op=ALU.mult),
                         reads=[("sg",)], writes=[("ps", bbg), ("bs", j % 2)])
                    if j >= 1:
                        conv(j - 1)
                conv(NT - 1)

        cur_scratch = None

        def set_scratch(kind):
            nonlocal cur_scratch
            if cur_scratch == kind:
                return
            old = A_KEYS if cur_scratch == "A" else (B_KEYS if cur_scratch == "B" else [])
            new = A_KEYS if kind == "A" else B_KEYS
            retire(old, new)
            cur_scratch = kind
            ntmp["sq"], ntmp["rs"] = (sq, rstd) if kind == "A" else (sq_b, rstd_b)
            if kind == "B":
                for i in range(2):
                    T.op("pool", "memset", dict(ap=zb[i][:, 0:1], constant=0.0), writes=[("zpad", i)])
                    T.op("pool", "memset", dict(ap=zb[i][:, ZW - 1:ZW], constant=0.0), writes=[("zpad", i)])

        for s_ in range(nseq):
            for j in range(NT):
                for k in range(KC):
                    T.dma("sp", dict(out=xT[:, k, ts(j)], in_=xin[s_, k * 128:(k + 1) * 128, ts(j)]),
                          writes=[("x", k, j)])
            for li, l in enumerate(layers):
                if li == 0:
                    if cur_scratch is None:
                        set_scratch("A" if l % 2 == 0 else "B")
                    for j in range(NT):
                        emit_norm(l, j)
                set_scratch("A" if l % 2 == 0 else "B")
                if l % 2 == 0:
                    emit_attn_layer(l)
                else:
                    emit_conv_layer(l)
                nxt = layers[li + 1] if li + 1 < len(layers) else None
                for step in range(NT + 1):
                    if step < NT:
                        emit_outproj(l, step)
                    if step >= 1:
                        j = step - 1
                        if nxt is not None:
                            emit_norm(nxt, j)
                        else:
                            for k in range(KC):
                                T.dma("sp", dict(out=yout[s_, k * 128:(k + 1) * 128, ts(j)], in_=xT[:, k, ts(j)]),
                                      reads=[("x", k, j)])
        if debug:
            T.dma("sp", dict(out=dbg_h[:, :, :], in_=hT[:, :, :]), reads=[("h", k, j) for k in range(KC) for j in range(NT)])
            T.dma("sp", dict(out=dbg_og[:, :, :], in_=ogT[:, :, :]), reads=[("og", k, n) for k in range(KC) for n in range(NBLK)])
        T.finish("sp")
        T.finish("pool")
        T.finish("act")
        T.finish("dve")
        T.finish("pe")
    return nc


def _chunk(wcols):
    return np.ascontiguousarray(wcols.reshape(KC, 128, 128).transpose(1, 0, 2))


def _prep_weights(a_w_in, a_w_out, b_w_in, b_w_out):
    out = np.empty((NCH, 128, KC, 128), np.float32)
    for l in range(DEPTH):
        base = _layer_chunk_base(l)
        s = l // 2
        if l % 2 == 0:
            w = a_w_in[s]
            for c in range(8):
                out[base + c] = _chunk(w[:, 1536 + c * 128:1536 + (c + 1) * 128])
            for vc in range(2):
                out[base + 8 + vc] = _chunk(w[:, 1280 + vc * 128:1280 + (vc + 1) * 128])
            for c in range(8):
                out[base + 10 + c] = _chunk(w[:, c * 128:(c + 1) * 128])
            for g in range(4):
                kg = w[:, 1024 + g * 64:1024 + (g + 1) * 64]
                out[base + 18 + g] = _chunk(np.concatenate([kg, kg], axis=1))
            for co in range(8):
                out[base + 22 + co] = _chunk(a_w_out[s][:, co * 128:(co + 1) * 128])
        else:
            w = b_w_in[s]
            for c in range(32):
                out[base + c] = _chunk(w[:, c * 128:(c + 1) * 128])
            for co in range(8):
                out[base + 32 + co] = _chunk(b_w_out[s][:, co * 128:(co + 1) * 128])
    return out


def _prep_consts(norm_g, a_q_norm, a_k_norm, a_sink, b_conv):
    cvec = np.zeros((128, NCV), np.float32)
    p = np.arange(128)
    for l in range(DEPTH):
        for k in range(KC):
            cvec[:, CV_NORM + l * 8 + k] = norm_g[l, k * 128:(k + 1) * 128]
    for a in range(2):
        cvec[:, CV_QG + a] = a_q_norm[a][p % 64]
        cvec[:, CV_KG + a] = a_k_norm[a][p % 64]
        cvec[:, CV_QGP + a] = a_q_norm[a][(p % 64 + 32) % 64]
        cvec[:, CV_KGP + a] = a_k_norm[a][(p % 64 + 32) % 64]
        for g in range(4):
            for j in range(2):
                cvec[:, CV_SINK + (a * 4 + g) * 2 + j] = a_sink[a][4 * g + 2 * j + (p >= 64)]
    for b in range(2):
        for c in range(KC):
            for t in range(3):
                cvec[:, CV_CONV + (b * 8 + c) * 3 + t] = b_conv[b, t, c * 128:(c + 1) * 128]
    cvec[:, CV_EPS] = EPS

    cbf = np.zeros((128, NCB), np.float32)
    cbf[:, CB_ONES:CB_ONES + 128] = 1.0
    cbf[:, CB_BO:CB_BO + 128] = (p[:, None] // 64 == p[None, :] // 64)
    rtm = np.zeros((128, 128), np.float32)
    for i in range(128):
        if i % 64 < 32:
            rtm[i + 32, i] = -1.0
        else:
            rtm[i - 32, i] = 1.0
    cbf[:, CB_RT:CB_RT + 128] = rtm
    cbf[:, CB_MLO:CB_MLO + 512] = np.tile((p[:, None] <= p[None, :]).astype(np.float32), (1, 4))
    cbf[:, CB_MHI:CB_MHI + 512] = np.tile((p[None, :] <= p[:, None]).astype(np.float32), (1, 4))

    inv_freq = (10000.0 ** (-np.arange(0, 64, 2, dtype=np.float32) / 64)).astype(np.float32)
    ang = np.arange(S, dtype=np.float32)[:, None] * inv_freq[None, :]
    cosT = np.cos(ang).astype(np.float32).T
    sinT = np.sin(ang).astype(np.float32).T
    rope = np.empty((128, 2, S), np.float32)
    rope[:, 0, :] = cosT[p % 32]
    rope[:, 1, :] = sinT[p % 32]
    return cvec, cbf, rope


def _run(inputs, layers, seqs=None, ncores=NCORES, debug=False):
    x = np.asarray(inputs["x"], np.float32)
    wts = _prep_weights(np.asarray(inputs["a_w_in"], np.float32), np.asarray(inputs["a_w_out"], np.float32),
                        np.asarray(inputs["b_w_in"], np.float32), np.asarray(inputs["b_w_out"], np.float32))
    cvec, cbf, rope = _prep_consts(np.asarray(inputs["norm_g"], np.float32), np.asarray(inputs["a_q_norm"], np.float32),
                                   np.asarray(inputs["a_k_norm"], np.float32), np.asarray(inputs["a_sink"], np.float32),
                                   np.asarray(inputs["b_conv"], np.float32))
    nseq = SEQ_PER_CORE if seqs is None else seqs
    xT = np.ascontiguousarray(x.transpose(0, 2, 1))
    import time as _t
    _t0 = _t.time()
    nc = build_program(list(layers), nseq=nseq, debug=debug)
    print("[kernel] build %.1fs" % (_t.time() - _t0), flush=True)
    in_maps = []
    for c in range(ncores):
        in_maps.append({"xin": np.ascontiguousarray(xT[c * nseq:(c + 1) * nseq]), "wts": wts, "cvec": cvec, "cbf": cbf, "rope": rope})
    res = run_bass_kernel_spmd(nc, in_maps, core_ids=list(range(ncores)))
    if debug:
        return res
    outT = np.concatenate([r["yout"] for r in res.results], axis=0)
    return np.ascontiguousarray(outT.transpose(0, 2, 1)).astype(np.float32)


def kernel(x, norm_g, a_w_in, a_q_norm, a_k_norm, a_sink, a_w_out, b_w_in, b_conv, b_w_out):
    inputs = dict(x=x, norm_g=norm_g, a_w_in=a_w_in, a_q_norm=a_q_norm, a_k_norm=a_k_norm, a_sink=a_sink,
                  a_w_out=a_w_out, b_w_in=b_w_in, b_conv=b_conv, b_w_out=b_w_out)
    return _run(inputs, layers=list(range(DEPTH)))
```

```python
import contextlib
import numpy as np
import concourse.bass as bass
import concourse.mybir as mybir
from concourse.bass_utils import run_bass_kernel_spmd

F32 = mybir.dt.float32
BF16 = mybir.dt.bfloat16
ALU = mybir.AluOpType
AF = mybir.ActivationFunctionType

D = 1024
S = 2048
BATCH = 16
NCORES = 8
SEQ_PER_CORE = BATCH // NCORES
DEPTH = 4
NT = 4
TW = 512
KC = 8
NBLK = 16
EPS = 1e-6
NS = 5
NP = 4
NDS = 24

CV_NORM = 0
CV_QG = 32
CV_KG = 34
CV_CONV = 36
CV_SINK = 84
CV_EPS = 100
CV_QGP = 101
CV_KGP = 103
NCV = 105
CB_ONES = 0
CB_BO = 128
CB_RT = 256
CB_MLO = 384
CB_MHI = 896
NCB = 1408

A_CH = 30
B_CH = 40


def _layer_chunk_base(l):
    base = 0
    for i in range(l):
        base += A_CH if i % 2 == 0 else B_CH
    return base


NCH = _layer_chunk_base(DEPTH)


class Tok:
    __slots__ = ("sem", "val", "eng", "key")

    def __init__(self, sem, val, eng, key):
        self.sem, self.val, self.eng, self.key = sem, val, eng, key


class Trk:
    def __init__(self, nc, es):
        self.nc = nc
        self.eng = {"pe": nc.tensor, "act": nc.scalar, "dve": nc.vector,
                    "pool": nc.gpsimd, "sp": nc.sync}
        self.sem = {e: es.enter_context(nc.semaphore("s_" + e))
                    for e in ("pe", "act", "dve", "pool")}
        self.cnt = {e: 0 for e in self.sem}
        self.dsem = [es.enter_context(nc.semaphore("s_dma%d" % i)) for i in range(NDS)]
        self.dcnt = [0] * NDS
        self.dnext = {"pool": 0, "sp": 0}
        self.waited = {e: {} for e in self.eng}
        self.lw = {}
        self.rd = {}
        self.nwaits = 0
        self.phase = ""
        self.annotate = False
        self.dry = False

    def _wait(self, e, tok):
        if tok.eng == "pe" and e == "pe":
            return
        w = self.waited[e]
        if w.get(tok.key, 0) >= tok.val:
            return
        w[tok.key] = tok.val
        self.eng[e].wait_ge(tok.sem, tok.val)
        self.nwaits += 1

    def _deps(self, e, reads, writes):
        for r in reads:
            t = self.lw.get(r)
            if t is not None:
                self._wait(e, t)
        for w in writes:
            t = self.lw.get(w)
            if t is not None:
                self._wait(e, t)
            for t in self.rd.get(w, {}).values():
                self._wait(e, t)

    def _commit(self, tok, reads, writes):
        for r in reads:
            d = self.rd.setdefault(r, {})
            o = d.get(tok.key)
            if o is None or o.val < tok.val:
                d[tok.key] = tok
        for w in writes:
            self.lw[w] = tok
            self.rd[w] = {}

    def grp(self, e, instrs, reads=(), writes=()):
        if self.dry:
            return None
        self._deps(e, reads, writes)
        eng = self.eng[e]
        ins = None
        for name, kw in instrs:
            ins = getattr(eng, name)(**kw)
            if self.annotate:
                ins.annotate(self.phase)
        self.cnt[e] += 1
        ins.then_inc(self.sem[e], 1)
        tok = Tok(self.sem[e], self.cnt[e], e, e)
        self._commit(tok, reads, writes)
        return tok

    def op(self, e, name, kw, reads=(), writes=()):
        return self.grp(e, [(name, kw)], reads, writes)

    def dma(self, e, kw, reads=(), writes=()):
        if self.dry:
            return None
        half = NDS // 2
        j = self.dnext[e]
        self.dnext[e] = (j + 1) % half
        i = j + (half if e == "pool" else 0)
        key = ("d", i)
        if self.dcnt[i] > 0:
            self._wait(e, Tok(self.dsem[i], 16 * self.dcnt[i], "dma", key))
        self._deps(e, reads, writes)
        ins = self.eng[e].dma_start(**kw)
        if self.annotate:
            ins.annotate(self.phase + "/dma")
        self.dcnt[i] += 1
        ins.then_inc(self.dsem[i], 16)
        tok = Tok(self.dsem[i], 16 * self.dcnt[i], "dma", key)
        self._commit(tok, reads, writes)
        return tok

    def finish(self, e="sp"):
        for x in self.sem:
            if self.cnt[x] > 0:
                self._wait(e, Tok(self.sem[x], self.cnt[x], x, x))
        for i in range(NDS):
            if self.dcnt[i] > 0:
                self._wait(e, Tok(self.dsem[i], 16 * self.dcnt[i], "dma", ("d", i)))


def build_program(layers, nseq=SEQ_PER_CORE, debug=False, annotate=False):
    nc = bass.Bass("TRN2", target_bir_lowering=False)
    xin = nc.dram_tensor("xin", [nseq, D, S], F32, kind="ExternalInput").ap()
    wts = nc.dram_tensor("wts", [NCH, 128, KC, 128], F32, kind="ExternalInput").ap()
    cvec = nc.dram_tensor("cvec", [128, NCV], F32, kind="ExternalInput").ap()
    cbf = nc.dram_tensor("cbf", [128, NCB], F32, kind="ExternalInput").ap()
    rope = nc.dram_tensor("rope", [128, 2, S], F32, kind="ExternalInput").ap()
    yout = nc.dram_tensor("yout", [nseq, D, S], F32, kind="ExternalOutput").ap()
    if debug:
        dbg_h = nc.dram_tensor("dbg_h", [128, KC, S], BF16, kind="ExternalOutput").ap()
        dbg_og = nc.dram_tensor("dbg_og", [128, KC, S], BF16, kind="ExternalOutput").ap()

    es = contextlib.ExitStack()
    with es:
        def sb(name, shape, dt):
            return es.enter_context(nc.sbuf_tensor(name, shape, dt))

        xT = sb("xT", [128, KC, S], F32)
        hT = sb("hT", [128, KC, S], BF16)
        ogT = sb("ogT", [128, KC, S], BF16)
        ring = [sb("ring%d" % i, [128, KC, 128], BF16) for i in range(NS)]
        cs = sb("cs", [128, 2, S], BF16)
        cv = sb("cv", [128, NCV], F32)
        esink = sb("esink", [128, 16], F32)
        cb = sb("cb", [128, NCB], BF16)
        SCR_BYTES = 58 * 1024
        scr = sb("scr", [128, SCR_BYTES // 2], BF16)
        psall = es.enter_context(nc.psum_tensor("psall", [128, 8 * TW], F32))
        ps = [psall[:, i * TW:(i + 1) * TW] for i in range(8)]

        T = Trk(nc, es)
        T.annotate = annotate

        class Carve:
            def __init__(self):
                self.off = 0

            def take(self, nbytes, dt):
                nbytes = (nbytes + 31) // 32 * 32
                o = self.off
                self.off += nbytes
                assert self.off <= SCR_BYTES, (self.off, SCR_BYTES)
                v = scr[:, o // 2:(o + nbytes) // 2]
                if dt == F32:
                    v = v.bitcast(F32)
                return v

        ca = Carve()
        qr = [ca.take(2 * S * 2, BF16).rearrange("p (c t) -> p c t", c=2) for _ in range(2)]
        kr = [ca.take(S * 2, BF16) for _ in range(2)]
        Vt = ca.take(NBLK * 256 * 2, BF16)
        Pt = [ca.take(4 * 384 * 2, BF16).rearrange("p (h q) -> p h q", h=4) for _ in range(NP)]
        sq = [ca.take(TW * 2, BF16) for _ in range(2)]
        qg = [ca.take(TW * 2, BF16) for _ in range(2)]
        qrot = [ca.take(TW * 2, BF16) for _ in range(2)]
        rstd = ca.take(TW * 4, F32)
        rstd2 = [rstd, rstd]
        t1 = ca.take(TW * 4, F32)
        t2 = ca.take(TW * 4, F32)
        d1 = ca.take(256 * 4, F32)
        rdn = ca.take(256 * 4, F32)
        A_KEYS = ([("q", bq, c, j) for bq in range(2) for c in range(2) for j in range(NT)]
                  + [("k", bq, j) for bq in range(2) for j in range(NT)]
                  + [("v", i) for i in range(8)] + [("p", i) for i in range(NP)]
                  + [("sq", i) for i in range(2)] + [("qg", i) for i in range(2)]
                  + [("rstd",), ("rstd", 1), ("t1",), ("t2",), ("d1",), ("rdn",)] + [("qrot", i, qq) for i in range(2) for qq in range(4)])
        cbv = Carve()
        ZW = S + 2
        zb = [cbv.take(ZW * 4, F32) for _ in range(2)]
        u_sb = cbv.take(TW * 4, F32)
        sgb = cbv.take(TW * 4, F32)
        bsb = [cbv.take(TW * 4, F32) for _ in range(2)]
        a0 = cbv.take(TW * 4, F32)
        a1 = cbv.take(TW * 4, F32)
        sq_b = [cbv.take(TW * 2, BF16) for _ in range(2)]
        rstd_b = cbv.take(TW * 4, F32)
        B_KEYS = ([("z", i, j) for i in range(2) for j in range(NT)] + [("zpad", i) for i in range(2)]
                  + [("u",), ("sg",), ("bs", 0), ("bs", 1), ("a0",), ("a1",), ("sq", 0), ("sq", 1), ("rstd",)])
        ntmp = {"sq": sq, "rs": rstd}

        def retire(keys_old, keys_new):
            toks = {}
            for k in keys_old:
                t = T.lw.pop(k, None)
                cands = list(T.rd.pop(k, {}).values())
                if t is not None:
                    cands.append(t)
                for t in cands:
                    o = toks.get(t.key)
                    if o is None or o.val < t.val:
                        toks[t.key] = t
            for k in keys_new:
                T.lw.pop(k, None)
                T.rd[k] = dict(toks)

        state = {"bank": 0, "wuse": 0, "wissued": 0}

        def nb(stream=None):
            if stream == "qk":
                b = state.get("bank_qk", 0)
                state["bank_qk"] = (b + 1) % 4
                return b
            if stream == "hi":
                b = state.get("bank_hi", 0)
                state["bank_hi"] = (b + 1) % 5
                return 3 + b
            b = state["bank"]
            state["bank"] = (b + 1) % 8
            return b

        wplan = []

        def wprefetch(upto):
            upto = min(upto, len(wplan) - 1)
            while state["wissued"] <= upto:
                i = state["wissued"]
                slot = i % NS
                T.dma("pool", dict(out=ring[slot][:, :, :], in_=wts[wplan[i]]),
                      reads=[], writes=[("w", slot)])
                state["wissued"] += 1

        def wuse(cid):
            if T.dry:
                wplan.append(cid)
                return 0
            i = state["wuse"]
            state["wuse"] += 1
            assert wplan[i] == cid, (i, wplan[i], cid)
            assert i - NS < state["low"], ("weight ring over-subscribed", i, state["low"])
            wprefetch(i)
            state["live"][i % NS] = i
            return i % NS

        def wrel(slot):
            if T.dry:
                return
            state["done"].add(state["live"].pop(slot))
            while state["low"] in state["done"]:
                state["done"].remove(state["low"])
                state["low"] += 1
            wprefetch(state["low"] + NS - 1)

        def ts(j):
            return slice(j * TW, (j + 1) * TW)

        T.dma("sp", dict(out=cv[:, :], in_=cvec[:, :]), writes=[("cv",)])
        T.dma("pool", dict(out=cb[:, :], in_=cbf[:, :]), writes=[("cb",)])
        T.dma("pool", dict(out=cs[:, :, :], in_=rope[:, :, :]), writes=[("cs",)])
        T.op("act", "activation", dict(out=esink[:, :], in_=cv[:, CV_SINK:CV_SINK + 16], func=AF.Exp),
             reads=[("cv",)], writes=[("esink",)])
        ones = cb[:, CB_ONES:CB_ONES + 128]
        bo = cb[:, CB_BO:CB_BO + 128]
        rt = cb[:, CB_RT:CB_RT + 128]
        mlo = cb[:, CB_MLO:CB_MLO + 512].rearrange("p (h q) -> p h q", h=4)
        mhi = cb[:, CB_MHI:CB_MHI + 512].rearrange("p (h q) -> p h q", h=4)
        epsc = cv[:, CV_EPS:CV_EPS + 1]

        def ogkeys(c, j):
            return [("og", c, n) for n in range(4 * j, 4 * j + 4)]

        def emit_norm(l, j):
            T.phase = "norm"
            b = nb()
            nsq, nrs = ntmp["sq"], ntmp["rs"]
            for k in range(KC):
                T.op("act", "activation", dict(out=nsq[k % 2][:, :], in_=xT[:, k, ts(j)], func=AF.Square),
                     reads=[("x", k, j)], writes=[("sq", k % 2)])
                T.op("pe", "matmul", dict(out=ps[b][:, :], lhsT=ones, rhs=nsq[k % 2][:, :],
                                          start=(k == 0), stop=(k == KC - 1)),
                     reads=[("sq", k % 2), ("cb",)], writes=[("ps", b)])
            T.op("act", "activation", dict(out=nrs[:, :], in_=ps[b][:, :], func=AF.Ln, scale=1.0 / D, bias=epsc),
                 reads=[("cv",)], writes=[("ps", b), ("rstd",)])
            T.op("act", "activation", dict(out=nrs[:, :], in_=nrs[:, :], func=AF.Exp, scale=-0.5),
                 writes=[("rstd",)])
            for k in range(KC):
                T.op("dve", "scalar_tensor_tensor",
                     dict(out=hT[:, k, ts(j)], in0=xT[:, k, ts(j)], scalar=cv[:, CV_NORM + l * 8 + k:CV_NORM + l * 8 + k + 1],
                          in1=nrs[:, :], op0=ALU.mult, op1=ALU.mult),
                     reads=[("x", k, j), ("rstd",), ("cv",)], writes=[("h", k, j)])

        def emit_proj(slot, j, b, src, srckeys):
            T.grp("pe", [("matmul", dict(out=ps[b][:, :], lhsT=ring[slot][:, k, :], rhs=src[:, k, ts(j)],
                                         start=(k == 0), stop=(k == KC - 1))) for k in range(KC)],
                  reads=[("w", slot)] + srckeys, writes=[("ps", b)])

        def hkeys(j):
            return [("h", k, j) for k in range(KC)]

        def emit_outproj(l, j):
            T.phase = "outproj"
            srck = [key for k in range(KC) for key in ogkeys(k, j)]
            for co in range(KC):
                slot = wuse(_layer_chunk_base(l) + (22 if l % 2 == 0 else 32) + co)
                b = nb()
                emit_proj(slot, j, b, ogT, srck)
                wrel(slot)
                T.op("dve", "tensor_tensor", dict(out=xT[:, co, ts(j)], in0=ps[b][:, :], in1=xT[:, co, ts(j)], op=ALU.add),
                     reads=[], writes=[("ps", b), ("x", co, j)])

        def emit_qk_stageA(item):
            T.phase = "qkA"
            slot, j, i = item["slot"], item["j"], item["i"]
            if slot is None:
                slot = item["slotref"][0] = wuse(item["cid"]) if item["slotref"][0] is None else item["slotref"][0]
            b = i % 2
            item["b"] = b
            emit_proj(slot, j, b, hT, hkeys(j))
            if j == NT - 1:
                wrel(slot)
            T.op("dve", "tensor_copy", dict(out=qg[i % 2][:, :], in_=ps[b][:, :]),
                 writes=[("ps", b), ("qg", i % 2)])
            T.op("dve", "tensor_tensor", dict(out=sq[i % 2][:, :], in0=qg[i % 2][:, :], in1=qg[i % 2][:, :], op=ALU.mult),
                 reads=[("qg", i % 2)], writes=[("sq", i % 2)])
            for qq, (dst0, src0) in enumerate(((0, 32), (32, 0), (64, 96), (96, 64))):
                T.dma("sp", dict(out=qrot[i % 2][dst0:dst0 + 32, :], in_=qg[i % 2][src0:src0 + 32, :]),
                      reads=[("qg", i % 2)], writes=[("qrot", i % 2, qq)])

        def emit_qk_stageB(item):
            T.phase = "qkB"
            j, i, b = item["j"], item["i"], item["b"]
            gcol = item["gcol"]
            b2 = 2
            T.op("pe", "matmul", dict(out=ps[b2][:, :], lhsT=bo, rhs=sq[i % 2][:, :], start=True, stop=True),
                 reads=[("sq", i % 2), ("cb",)], writes=[("ps", b2)])
            rs = rstd2[i % 2]
            rkey = ("rstd",)
            T.op("act", "activation", dict(out=rs[:, :], in_=ps[b2][:, :], func=AF.Ln, scale=1.0 / 64, bias=epsc),
                 reads=[("cv",)], writes=[("ps", b2), rkey])
            T.op("act", "activation", dict(out=rs[:, :], in_=rs[:, :], func=AF.Exp, scale=-0.5),
                 writes=[rkey])
            T.op("dve", "scalar_tensor_tensor",
                 dict(out=t1[:, :], in0=ps[b][:, :], scalar=gcol, in1=cs[:, 0, ts(j)], op0=ALU.mult, op1=ALU.mult),
                 reads=[("cv",), ("cs",)], writes=[("ps", b), ("t1",)])
            T.op("dve", "scalar_tensor_tensor",
                 dict(out=t2[:, :], in0=qrot[i % 2][:, :], scalar=item["gpcol"], in1=cs[:, 1, ts(j)], op0=ALU.mult, op1=ALU.mult),
                 reads=[("cv",), ("cs",)] + [("qrot", i % 2, qq) for qq in range(4)], writes=[("t2",)])
            T.op("pool", "tensor_tensor", dict(out=t1[:, :], in0=t1[:, :], in1=t2[:, :], op=ALU.add),
                 reads=[("t2",)], writes=[("t1",)])
            T.op("pool", "tensor_tensor", dict(out=item["dst"], in0=t1[:, :], in1=rs[:, :], op=ALU.mult),
                 reads=[("t1",), rkey], writes=[item["dkey"]])

        def emit_S(a, g, m):
            T.phase = "S"
            bq = g % 2
            slot = m % NP
            lo, hi = max(m - 1, 0), min(m + 1, NBLK - 1)
            qs, qe = lo * 128, (hi + 1) * 128
            off = (lo - (m - 1)) * 128
            w = qe - qs
            qkeys_j = sorted(set([qs // TW, (qe - 1) // TW]))
            banks = [4, 5, 6, 7]
            for hh in range(4):
                cc, par = hh // 2, hh % 2
                rows = slice(par * 64, (par + 1) * 64)
                b = banks[hh]
                T.op("pe", "matmul", dict(out=ps[b][:, off:off + w], lhsT=kr[bq][rows, m * 128:(m + 1) * 128],
                                          rhs=qr[bq][rows, cc, qs:qe], start=True, stop=True),
                     reads=[("k", bq, m // 4)] + [("q", bq, cc, jj) for jj in qkeys_j], writes=[("ps", b)])
            s4 = psall[:, 4 * TW:8 * TW].rearrange("p (h c) -> p h c", h=4)
            T.op("act", "activation", dict(out=Pt[slot][:, :, off:off + w], in_=s4[:, :, off:off + w],
                                           func=AF.Exp, scale=0.125),
                 writes=[("ps", 4), ("ps", 5), ("ps", 6), ("ps", 7), ("p", slot)])
            if True:
                if m >= 1:
                    T.op("dve", "tensor_tensor", dict(out=Pt[slot][:, :, 0:128], in0=Pt[slot][:, :, 0:128], in1=mlo, op=ALU.mult),
                         reads=[("cb",)], writes=[("p", slot)])
                if m <= NBLK - 2:
                    T.op("dve", "tensor_tensor", dict(out=Pt[slot][:, :, 256:384], in0=Pt[slot][:, :, 256:384], in1=mhi, op=ALU.mult),
                         reads=[("cb",)], writes=[("p", slot)])

        def emit_PV(a, g, n):
            T.phase = "PV"
            b = 3
            mms = [mm for mm in (n - 1, n, n + 1) if 0 <= mm < NBLK]
            instrs = []
            for kind in range(2):
                for idx, mm in enumerate(mms):
                    cbk = n - mm + 1
                    for par in range(2):
                        rows = slice(par * 64, (par + 1) * 64)
                        outv = ps[b][rows, kind * 256:(kind + 1) * 256].rearrange("p (a q) -> p a q", a=2)
                        rhs = Pt[mm % NP][:, par::2, cbk * 128:(cbk + 1) * 128]
                        lhsT = Vt[:, mm * 256 + g * 64: mm * 256 + g * 64 + 64] if kind == 0 else cb[:, CB_ONES:CB_ONES + 64]
                        instrs.append(("matmul", dict(out=outv, lhsT=lhsT, rhs=rhs, start=(idx == 0),
                                                      stop=(idx == len(mms) - 1), tile_position=(0, par * 64))))
            T.grp("pe", instrs, reads=[("p", mm % NP) for mm in mms] + [("v", mm // 2) for mm in mms] + [("cb",)],
                  writes=[("ps", b)])
            sc = (a * 4 + g) * 2
            for jj in range(2):
                T.op("act", "activation",
                     dict(out=rdn[:, jj * 128:(jj + 1) * 128], in_=ps[b][:, 256 + jj * 128:256 + (jj + 1) * 128],
                          func=AF.Ln, bias=esink[:, sc + jj:sc + jj + 1]),
                     reads=[("esink",)], writes=[("ps", b), ("rdn",)])
            T.op("act", "activation", dict(out=rdn[:, :], in_=rdn[:, :], func=AF.Exp, scale=-1.0), writes=[("rdn",)])
            T.op("dve", "tensor_tensor", dict(out=d1[:, :], in0=ps[b][:, 0:256], in1=rdn[:, :], op=ALU.mult),
                 reads=[("rdn",)], writes=[("ps", b), ("d1",)])
            ogv = ogT[:, 2 * g:2 * g + 2, n * 128:(n + 1) * 128]
            T.op("pool", "tensor_tensor", dict(out=ogv, in0=d1[:, :].rearrange("p (a q) -> p a q", a=2), in1=ogv, op=ALU.mult),
                 reads=[("d1",)], writes=[("og", 2 * g, n), ("og", 2 * g + 1, n)])

        def interleave(la, lb):
            ia = ib = 0
            while ia < len(la) or ib < len(lb):
                fa = ia / len(la) if la else 2.0
                fb = ib / len(lb) if lb else 2.0
                if ib >= len(lb) or (ia < len(la) and fa <= fb):
                    la[ia]()
                    ia += 1
                else:
                    lb[ib]()
                    ib += 1

        def emit_attn_layer(l):
            a = l // 2
            wbase = _layer_chunk_base(l)

            def proj_work(g):
                th = []
                bq = g % 2

                def gate_tile(c, j, ref):
                    def f():
                        T.phase = "gate"
                        if ref[0] is None:
                            ref[0] = wuse(wbase + c)
                        b = nb("hi")
                        emit_proj(ref[0], j, b, hT, hkeys(j))
                        if j == NT - 1:
                            wrel(ref[0])
                        T.op("act", "activation", dict(out=ogT[:, c, ts(j)], in_=ps[b][:, :], func=AF.Silu),
                             writes=[("ps", b)] + ogkeys(c, j))
                    return f
                gate_th = []
                for c in (range(KC) if g == 0 else ()):
                    ref = [None]
                    for j in range(NT):
                        gate_th.append(gate_tile(c, j, ref))
                if g == 0:
                    vref = [None, None]

                    def v_tile(tb2):
                        def f():
                            T.phase = "V"
                            if vref[0] is None:
                                vref[0] = wuse(wbase + 8)
                                vref[1] = wuse(wbase + 9)
                            b = nb()
                            instrs = []
                            for t in range(2):
                                tb = 2 * tb2 + t
                                for vc in range(2):
                                    for k in range(KC):
                                        instrs.append(("matmul", dict(out=ps[b][:, t * 256 + vc * 128:t * 256 + (vc + 1) * 128],
                                                                      lhsT=hT[:, k, tb * 128:(tb + 1) * 128], rhs=ring[vref[vc]][:, k, :],
                                                                      start=(k == 0), stop=(k == KC - 1))))
                            T.grp("pe", instrs, reads=[("w", vref[0]), ("w", vref[1])] + hkeys(tb2 // 2), writes=[("ps", b)])
                            if tb2 == 7:
                                wrel(vref[0])
                                wrel(vref[1])
                            T.op("act", "activation", dict(out=Vt[:, tb2 * 512:(tb2 + 1) * 512], in_=ps[b][:, :], func=AF.Copy),
                                 writes=[("ps", b), ("v", tb2)])
                        return f
                    for tb2 in range(8):
                        th.append(v_tile(tb2))
                items = []
                for kind, cc in (("q", 0), ("q", 1), ("k", 0)):
                    ref = [None]
                    for j in range(NT):
                        if kind == "q":
                            dst, dkey = qr[bq][:, cc, ts(j)], ("q", bq, cc, j)
                            gcol = cv[:, CV_QG + a:CV_QG + a + 1]
                            gpcol = cv[:, CV_QGP + a:CV_QGP + a + 1]
                        else:
                            dst, dkey = kr[bq][:, ts(j)], ("k", bq, j)
                            gcol = cv[:, CV_KG + a:CV_KG + a + 1]
                            gpcol = cv[:, CV_KGP + a:CV_KGP + a + 1]
                        items.append(dict(kind=kind, slot=None, slotref=ref, j=j, i=len(items), dst=dst, dkey=dkey, gcol=gcol, gpcol=gpcol,
                                          cid=wbase + (10 + 2 * g + cc if kind == "q" else 18 + g)))

                def qk_step(i):
                    def f():
                        if i < len(items):
                            emit_qk_stageA(items[i])
                        if i >= 1:
                            emit_qk_stageB(items[i - 1])
                    return f
                qk_th = [qk_step(i) for i in range(len(items) + 1)]
                if g == 0:
                    qi = 0
                    for c in range(KC):
                        th.extend(gate_th[4 * c:4 * c + 4])
                        nq = 2 if c < 6 else (1 if c == 6 else 0)
                        th.extend(qk_th[qi:qi + nq])
                        qi += nq
                    assert qi == len(qk_th)
                else:
                    th.extend(qk_th)
                return th

            def attn_work(g):
                th = []

                def step_f(step):
                    def f():
                        if step < NBLK:
                            emit_S(a, g, step)
                        if step >= 2:
                            emit_PV(a, g, step - 2)
                    return f
                for step in range(NBLK + 2):
                    th.append(step_f(step))
                return th

            interleave(proj_work(0), [])
            for g in range(4):
                interleave(attn_work(g), proj_work(g + 1) if g + 1 < 4 else [])

        def emit_conv_layer(l):
            bi = l // 2
            T.phase = "conv"
            for c in range(KC):
                zi = c % 2
                z = zb[zi]
                cb0 = _layer_chunk_base(l)
                slots = [wuse(cb0 + 8 + c), wuse(cb0 + 16 + c), wuse(cb0 + 24 + c), wuse(cb0 + c)]
                wc = CV_CONV + (bi * 8 + c) * 3

                def conv(j):
                    rk = [("z", zi, jj) for jj in (j - 1, j, j + 1) if 0 <= jj < NT] + [("zpad", zi)]
                    T.op("act", "activation", dict(out=a0[:, :], in_=z[:, j * TW:j * TW + TW], func=AF.Copy, scale=cv[:, wc:wc + 1]),
                         reads=rk + [("cv",)], writes=[("a0",)])
                    T.op("dve", "scalar_tensor_tensor",
                         dict(out=a1[:, :], in0=z[:, 1 + j * TW:1 + j * TW + TW], scalar=cv[:, wc + 1:wc + 2], in1=a0[:, :],
                              op0=ALU.mult, op1=ALU.add),
                         reads=rk + [("a0",), ("cv",)], writes=[("a1",)])
                    T.op("dve", "scalar_tensor_tensor",
                         dict(out=a0[:, :], in0=z[:, 2 + j * TW:2 + j * TW + TW], scalar=cv[:, wc + 2:wc + 3], in1=a1[:, :],
                              op0=ALU.mult, op1=ALU.add),
                         reads=rk + [("a1",), ("cv",)], writes=[("a0",)])
                    T.op("pool", "tensor_tensor", dict(out=ogT[:, c, ts(j)], in0=a0[:, :], in1=bsb[j % 2][:, :], op=ALU.mult),
                         reads=[("a0",), ("bs", j % 2)], writes=ogkeys(c, j))

                for j in range(NT):
                    bcg, bu, bgt, bbg = nb(), nb(), nb(), nb()
                    for si, bb in enumerate((bcg, bu, bgt, bbg)):
                        emit_proj(slots[si], j, bb, hT, hkeys(j))
                        if j == NT - 1:
                            wrel(slots[si])
                    T.op("act", "activation", dict(out=u_sb[:, :], in_=ps[bu][:, :], func=AF.Copy),
                         writes=[("ps", bu), ("u",)])
                    T.op("act", "activation", dict(out=sgb[:, :], in_=ps[bgt][:, :], func=AF.Silu),
                         writes=[("ps", bgt), ("sg",)])
                    T.op("dve", "tensor_tensor", dict(out=z[:, 1 + j * TW:1 + (j + 1) * TW], in0=ps[bcg][:, :], in1=u_sb[:, :], op=ALU.mult),
                         reads=[("u",)], writes=[("ps", bcg), ("z", zi, j)])
                    T.op("dve", "tensor_tensor", dict(out=bsb[j % 2][:, :], in0=ps[bbg][:, :], in1=sgb[:, :], op=ALU.mult),
                         reads=[("sg",)], writes=[("ps", bbg), ("bs", j % 2)])
                    if j >= 1:
                        conv(j - 1)
                conv(NT - 1)

        cur_scratch = None

        def set_scratch(kind):
            nonlocal cur_scratch
            if cur_scratch == kind:
                return
            old = A_KEYS if cur_scratch == "A" else (B_KEYS if cur_scratch == "B" else [])
            new = A_KEYS if kind == "A" else B_KEYS
            retire(old, new)
            cur_scratch = kind
            ntmp["sq"], ntmp["rs"] = (sq, rstd) if kind == "A" else (sq_b, rstd_b)
            if kind == "B":
                for i in range(2):
                    T.op("pool", "memset", dict(ap=zb[i][:, 0:1], constant=0.0), writes=[("zpad", i)])
                    T.op("pool", "memset", dict(ap=zb[i][:, ZW - 1:ZW], constant=0.0), writes=[("zpad", i)])

        def emit_all():
            nonlocal cur_scratch
            cur_scratch = None
            state.clear()
            state.update({"bank": 0, "wuse": 0, "wissued": 0, "low": 0, "live": {}, "done": set()})
            for s_ in range(nseq):
                for j in range(NT):
                    for k in range(KC):
                        T.dma("sp", dict(out=xT[:, k, ts(j)], in_=xin[s_, k * 128:(k + 1) * 128, ts(j)]),
                              writes=[("x", k, j)])
                for li, l in enumerate(layers):
                    if li == 0:
                        if cur_scratch is None:
                            set_scratch("A" if l % 2 == 0 else "B")
                        for j in range(NT):
                            emit_norm(l, j)
                    set_scratch("A" if l % 2 == 0 else "B")
                    if l % 2 == 0:
                        emit_attn_layer(l)
                    else:
                        emit_conv_layer(l)
                    nxt = layers[li + 1] if li + 1 < len(layers) else None
                    for step in range(NT + 1):
                        if step < NT:
                            emit_outproj(l, step)
                        if step >= 1:
                            j = step - 1
                            if nxt is not None:
                                emit_norm(nxt, j)
                            else:
                                for k in range(KC):
                                    T.dma("sp", dict(out=yout[s_, k * 128:(k + 1) * 128, ts(j)], in_=xT[:, k, ts(j)]),
                                          reads=[("x", k, j)])

        T.dry = True
        emit_all()
        T.dry = False
        emit_all()
        if debug:
            T.dma("sp", dict(out=dbg_h[:, :, :], in_=hT[:, :, :]), reads=[("h", k, j) for k in range(KC) for j in range(NT)])
            T.dma("sp", dict(out=dbg_og[:, :, :], in_=ogT[:, :, :]), reads=[("og", k, n) for k in range(KC) for n in range(NBLK)])
        T.finish("sp")
        T.finish("pool")
        T.finish("act")
        T.finish("dve")
        T.finish("pe")
    return nc


def _chunk(wcols):
    return np.ascontiguousarray(wcols.reshape(KC, 128, 128).transpose(1, 0, 2))


def _prep_weights(a_w_in, a_w_out, b_w_in, b_w_out):
    out = np.empty((NCH, 128, KC, 128), np.float32)
    for l in range(DEPTH):
        base = _layer_chunk_base(l)
        s = l // 2
        if l % 2 == 0:
            w = a_w_in[s]
            for c in range(8):
                out[base + c] = _chunk(w[:, 1536 + c * 128:1536 + (c + 1) * 128])
            for vc in range(2):
                out[base + 8 + vc] = _chunk(w[:, 1280 + vc * 128:1280 + (vc + 1) * 128])
            for c in range(8):
                out[base + 10 + c] = _chunk(w[:, c * 128:(c + 1) * 128])
            for g in range(4):
                kg = w[:, 1024 + g * 64:1024 + (g + 1) * 64]
                out[base + 18 + g] = _chunk(np.concatenate([kg, kg], axis=1))
            for co in range(8):
                out[base + 22 + co] = _chunk(a_w_out[s][:, co * 128:(co + 1) * 128])
        else:
            w = b_w_in[s]
            for c in range(32):
                out[base + c] = _chunk(w[:, c * 128:(c + 1) * 128])
            for co in range(8):
                out[base + 32 + co] = _chunk(b_w_out[s][:, co * 128:(co + 1) * 128])
    return out


def _prep_consts(norm_g, a_q_norm, a_k_norm, a_sink, b_conv):
    cvec = np.zeros((128, NCV), np.float32)
    p = np.arange(128)
    for l in range(DEPTH):
        for k in range(KC):
            cvec[:, CV_NORM + l * 8 + k] = norm_g[l, k * 128:(k + 1) * 128]
    for a in range(2):
        cvec[:, CV_QG + a] = a_q_norm[a][p % 64]
        cvec[:, CV_KG + a] = a_k_norm[a][p % 64]
        cvec[:, CV_QGP + a] = a_q_norm[a][(p % 64 + 32) % 64]
        cvec[:, CV_KGP + a] = a_k_norm[a][(p % 64 + 32) % 64]
        for g in range(4):
            for j in range(2):
                cvec[:, CV_SINK + (a * 4 + g) * 2 + j] = a_sink[a][4 * g + 2 * j + (p >= 64)]
    for b in range(2):
        for c in range(KC):
            for t in range(3):
                cvec[:, CV_CONV + (b * 8 + c) * 3 + t] = b_conv[b, t, c * 128:(c + 1) * 128]
    cvec[:, CV_EPS] = EPS

    cbf = np.zeros((128, NCB), np.float32)
    cbf[:, CB_ONES:CB_ONES + 128] = 1.0
    cbf[:, CB_BO:CB_BO + 128] = (p[:, None] // 64 == p[None, :] // 64)
    rtm = np.zeros((128, 128), np.float32)
    for i in range(128):
        if i % 64 < 32:
            rtm[i + 32, i] = -1.0
        else:
            rtm[i - 32, i] = 1.0
    cbf[:, CB_RT:CB_RT + 128] = rtm
    cbf[:, CB_MLO:CB_MLO + 512] = np.tile((p[:, None] <= p[None, :]).astype(np.float32), (1, 4))
    cbf[:, CB_MHI:CB_MHI + 512] = np.tile((p[None, :] <= p[:, None]).astype(np.float32), (1, 4))

    inv_freq = (10000.0 ** (-np.arange(0, 64, 2, dtype=np.float32) / 64)).astype(np.float32)
    ang = np.arange(S, dtype=np.float32)[:, None] * inv_freq[None, :]
    cosT = np.cos(ang).astype(np.float32).T
    sinT = np.sin(ang).astype(np.float32).T
    rope = np.empty((128, 2, S), np.float32)
    rope[:, 0, :] = cosT[p % 32]
    rope[:, 1, :] = sinT[p % 32] * np.where(p % 64 < 32, -1.0, 1.0)[:, None]
    return cvec, cbf, rope


def _run(inputs, layers, seqs=None, ncores=NCORES, debug=False):
    x = np.asarray(inputs["x"], np.float32)
    wts = _prep_weights(np.asarray(inputs["a_w_in"], np.float32), np.asarray(inputs["a_w_out"], np.float32),
                        np.asarray(inputs["b_w_in"], np.float32), np.asarray(inputs["b_w_out"], np.float32))
    cvec, cbf, rope = _prep_consts(np.asarray(inputs["norm_g"], np.float32), np.asarray(inputs["a_q_norm"], np.float32),
                                   np.asarray(inputs["a_k_norm"], np.float32), np.asarray(inputs["a_sink"], np.float32),
                                   np.asarray(inputs["b_conv"], np.float32))
    nseq = SEQ_PER_CORE if seqs is None else seqs
    xT = np.ascontiguousarray(x.transpose(0, 2, 1))
    import time as _t
    _t0 = _t.time()
    nc = build_program(list(layers), nseq=nseq, debug=debug)
    print("[kernel] build %.1fs" % (_t.time() - _t0), flush=True)
    in_maps = []
    for c in range(ncores):
        in_maps.append({"xin": np.ascontiguousarray(xT[c * nseq:(c + 1) * nseq]), "wts": wts, "cvec": cvec, "cbf": cbf, "rope": rope})
    res = run_bass_kernel_spmd(nc, in_maps, core_ids=list(range(ncores)))
    if debug:
        return res
    outT = np.concatenate([r["yout"] for r in res.results], axis=0)
    return np.ascontiguousarray(outT.transpose(0, 2, 1)).astype(np.float32)


def kernel(x, norm_g, a_w_in, a_q_norm, a_k_norm, a_sink, a_w_out, b_w_in, b_conv, b_w_out):
    inputs = dict(x=x, norm_g=norm_g, a_w_in=a_w_in, a_q_norm=a_q_norm, a_k_norm=a_k_norm, a_sink=a_sink,
                  a_w_out=a_w_out, b_w_in=b_w_in, b_conv=b_conv, b_w_out=b_w_out)
    return _run(inputs, layers=list(range(DEPTH)))
```

```python
import contextlib
import numpy as np
import concourse.bass as bass
import concourse.mybir as mybir
from concourse.bass_utils import run_bass_kernel_spmd

F32 = mybir.dt.float32
BF16 = mybir.dt.bfloat16
ALU = mybir.AluOpType
AF = mybir.ActivationFunctionType

D = 1024
S = 2048
BATCH = 16
NCORES = 8
SEQ_PER_CORE = BATCH // NCORES
DEPTH = 4
NT = 4
TW = 512
KC = 8
NBLK = 16
EPS = 1e-6
NS = 5
NP = 4
NDS = 24

CV_NORM = 0
CV_QG = 32
CV_KG = 34
CV_CONV = 36
CV_SINK = 84
CV_EPS = 100
CV_QGP = 101
CV_KGP = 103
NCV = 105
CB_ONES = 0
CB_BO = 128
CB_RT = 256
CB_ID = 384
CB_MB = 512
NCB = 768

A_CH = 30
B_CH = 40


def _layer_chunk_base(l):
    base = 0
    for i in range(l):
        base += A_CH if i % 2 == 0 else B_CH
    return base


NCH = _layer_chunk_base(DEPTH)


class Tok:
    __slots__ = ("sem", "val", "eng", "key")

    def __init__(self, sem, val, eng, key):
        self.sem, self.val, self.eng, self.key = sem, val, eng, key


class Trk:
    def __init__(self, nc, es):
        self.nc = nc
        self.eng = {"pe": nc.tensor, "act": nc.scalar, "dve": nc.vector,
                    "pool": nc.gpsimd, "sp": nc.sync}
        self.sem = {e: es.enter_context(nc.semaphore("s_" + e))
                    for e in ("pe", "act", "dve", "pool")}
        self.cnt = {e: 0 for e in self.sem}
        self.dsem = [es.enter_context(nc.semaphore("s_dma%d" % i)) for i in range(NDS)]
        self.dcnt = [0] * NDS
        self.dnext = {"pool": 0, "sp": 0}
        self.waited = {e: {} for e in self.eng}
        self.lw = {}
        self.rd = {}
        self.nwaits = 0
        self.phase = ""
        self.annotate = False
        self.dry = False

    def _wait(self, e, tok):
        if tok.eng == "pe" and e == "pe":
            return
        w = self.waited[e]
        if w.get(tok.key, 0) >= tok.val:
            return
        w[tok.key] = tok.val
        self.eng[e].wait_ge(tok.sem, tok.val)
        self.nwaits += 1

    def _deps(self, e, reads, writes):
        for r in reads:
            t = self.lw.get(r)
            if t is not None:
                self._wait(e, t)
        for w in writes:
            t = self.lw.get(w)
            if t is not None:
                self._wait(e, t)
            for t in self.rd.get(w, {}).values():
                self._wait(e, t)

    def _commit(self, tok, reads, writes):
        for r in reads:
            d = self.rd.setdefault(r, {})
            o = d.get(tok.key)
            if o is None or o.val < tok.val:
                d[tok.key] = tok
        for w in writes:
            self.lw[w] = tok
            self.rd[w] = {}

    def grp(self, e, instrs, reads=(), writes=()):
        if self.dry:
            return None
        self._deps(e, reads, writes)
        eng = self.eng[e]
        ins = None
        for name, kw in instrs:
            ins = getattr(eng, name)(**kw)
            if self.annotate:
                ins.annotate(self.phase)
        self.cnt[e] += 1
        ins.then_inc(self.sem[e], 1)
        tok = Tok(self.sem[e], self.cnt[e], e, e)
        self._commit(tok, reads, writes)
        return tok

    def op(self, e, name, kw, reads=(), writes=()):
        return self.grp(e, [(name, kw)], reads, writes)

    def dma(self, e, kw, reads=(), writes=()):
        if self.dry:
            return None
        half = NDS // 2
        j = self.dnext[e]
        self.dnext[e] = (j + 1) % half
        i = j + (half if e == "pool" else 0)
        key = ("d", i)
        if self.dcnt[i] > 0:
            self._wait(e, Tok(self.dsem[i], 16 * self.dcnt[i], "dma", key))
        self._deps(e, reads, writes)
        ins = self.eng[e].dma_start(**kw)
        if self.annotate:
            ins.annotate(self.phase + "/dma")
        self.dcnt[i] += 1
        ins.then_inc(self.dsem[i], 16)
        tok = Tok(self.dsem[i], 16 * self.dcnt[i], "dma", key)
        self._commit(tok, reads, writes)
        return tok

    def finish(self, e="sp"):
        for x in self.sem:
            if self.cnt[x] > 0:
                self._wait(e, Tok(self.sem[x], self.cnt[x], x, x))
        for i in range(NDS):
            if self.dcnt[i] > 0:
                self._wait(e, Tok(self.dsem[i], 16 * self.dcnt[i], "dma", ("d", i)))


def build_program(layers, nseq=SEQ_PER_CORE, debug=False, annotate=False):
    nc = bass.Bass("TRN2", target_bir_lowering=False)
    xin = nc.dram_tensor("xin", [nseq, D, S], F32, kind="ExternalInput").ap()
    wts = nc.dram_tensor("wts", [NCH, 128, KC, 128], F32, kind="ExternalInput").ap()
    cvec = nc.dram_tensor("cvec", [128, NCV], F32, kind="ExternalInput").ap()
    cbf = nc.dram_tensor("cbf", [128, NCB], F32, kind="ExternalInput").ap()
    rope = nc.dram_tensor("rope", [128, 2, S], F32, kind="ExternalInput").ap()
    yout = nc.dram_tensor("yout", [nseq, D, S], F32, kind="ExternalOutput").ap()
    if debug:
        dbg_h = nc.dram_tensor("dbg_h", [128, KC, S], BF16, kind="ExternalOutput").ap()
        dbg_og = nc.dram_tensor("dbg_og", [128, KC, S], BF16, kind="ExternalOutput").ap()

    es = contextlib.ExitStack()
    with es:
        def sb(name, shape, dt):
            return es.enter_context(nc.sbuf_tensor(name, shape, dt))

        xT = sb("xT", [128, KC, S], F32)
        hT = sb("hT", [128, KC, S], BF16)
        ogT = sb("ogT", [128, KC, S], BF16)
        ring = [sb("ring%d" % i, [128, KC, 128], BF16) for i in range(NS)]
        cs = sb("cs", [128, 2, S], BF16)
        cv = sb("cv", [128, NCV], F32)
        esink = sb("esink", [128, 16], F32)
        cb = sb("cb", [128, NCB], BF16)
        SCR_BYTES = 58 * 1024
        scr = sb("scr", [128, SCR_BYTES // 2], BF16)
        psall = es.enter_context(nc.psum_tensor("psall", [128, 8 * TW], F32))
        ps = [psall[:, i * TW:(i + 1) * TW] for i in range(8)]

        T = Trk(nc, es)
        T.annotate = annotate

        class Carve:
            def __init__(self):
                self.off = 0

            def take(self, nbytes, dt):
                nbytes = (nbytes + 31) // 32 * 32
                o = self.off
                self.off += nbytes
                assert self.off <= SCR_BYTES, (self.off, SCR_BYTES)
                v = scr[:, o // 2:(o + nbytes) // 2]
                if dt == F32:
                    v = v.bitcast(F32)
                return v

        ca = Carve()
        qr = [ca.take(2 * S * 2, BF16).rearrange("p (c t) -> p c t", c=2) for _ in range(2)]
        kr = [ca.take(S * 2, BF16) for _ in range(2)]
        Vt = ca.take(NBLK * 256 * 2, BF16)
        Pt = [ca.take(4 * 384 * 2, BF16).rearrange("p (h q) -> p h q", h=4) for _ in range(NP)]
        sq = [ca.take(TW * 2, BF16) for _ in range(2)]
        qg = [ca.take(TW * 2, BF16) for _ in range(2)]
        qrot = [ca.take(TW * 2, BF16) for _ in range(2)]
        rstd = ca.take(TW * 4, F32)
        rstd2 = [rstd, rstd]
        t1 = ca.take(TW * 4, F32)
        t2 = ca.take(TW * 4, F32)
        d1 = ca.take(256 * 4, F32)
        rdn = ca.take(256 * 4, F32)
        A_KEYS = ([("q", bq, c, j) for bq in range(2) for c in range(2) for j in range(NT)]
                  + [("k", bq, j) for bq in range(2) for j in range(NT)]
                  + [("v", i) for i in range(8)] + [("p", i) for i in range(NP)]
                  + [("sq", i) for i in range(2)] + [("qg", i) for i in range(2)]
                  + [("rstd",), ("rstd", 1), ("t1",), ("t2",), ("d1",), ("rdn",)] + [("qrot", i, qq) for i in range(2) for qq in range(4)])
        cbv = Carve()
        ZW = S + 2
        zb = [cbv.take(ZW * 4, F32) for _ in range(2)]
        u_sb = cbv.take(TW * 4, F32)
        sgb = cbv.take(TW * 4, F32)
        bsb = [cbv.take(TW * 4, F32) for _ in range(2)]
        a0 = cbv.take(TW * 4, F32)
        a1 = cbv.take(TW * 4, F32)
        sq_b = [cbv.take(TW * 2, BF16) for _ in range(2)]
        rstd_b = cbv.take(TW * 4, F32)
        B_KEYS = ([("z", i, j) for i in range(2) for j in range(NT)] + [("zpad", i) for i in range(2)]
                  + [("u",), ("sg",), ("bs", 0), ("bs", 1), ("a0",), ("a1",), ("sq", 0), ("sq", 1), ("rstd",)])
        ntmp = {"sq": sq, "rs": rstd}

        def retire(keys_old, keys_new):
            toks = {}
            for k in keys_old:
                t = T.lw.pop(k, None)
                cands = list(T.rd.pop(k, {}).values())
                if t is not None:
                    cands.append(t)
                for t in cands:
                    o = toks.get(t.key)
                    if o is None or o.val < t.val:
                        toks[t.key] = t
            for k in keys_new:
                T.lw.pop(k, None)
                T.rd[k] = dict(toks)

        state = {"bank": 0, "wuse": 0, "wissued": 0}

        def nb(stream=None):
            if stream == "qk":
                b = state.get("bank_qk", 0)
                state["bank_qk"] = (b + 1) % 4
                return b
            if stream == "hi":
                b = state.get("bank_hi", 0)
                state["bank_hi"] = (b + 1) % 5
                return 3 + b
            b = state["bank"]
            state["bank"] = (b + 1) % 8
            return b

        wplan = []

        def wprefetch(upto):
            upto = min(upto, len(wplan) - 1)
            while state["wissued"] <= upto:
                i = state["wissued"]
                slot = i % NS
                T.dma("pool", dict(out=ring[slot][:, :, :], in_=wts[wplan[i]]),
                      reads=[], writes=[("w", slot)])
                state["wissued"] += 1

        def wuse(cid):
            if T.dry:
                wplan.append(cid)
                return 0
            i = state["wuse"]
            state["wuse"] += 1
            assert wplan[i] == cid, (i, wplan[i], cid)
            assert i - NS < state["low"], ("weight ring over-subscribed", i, state["low"])
            wprefetch(i)
            state["live"][i % NS] = i
            return i % NS

        def wrel(slot):
            if T.dry:
                return
            state["done"].add(state["live"].pop(slot))
            while state["low"] in state["done"]:
                state["done"].remove(state["low"])
                state["low"] += 1
            wprefetch(state["low"] + NS - 1)

        def ts(j):
            return slice(j * TW, (j + 1) * TW)

        T.dma("sp", dict(out=cv[:, :], in_=cvec[:, :]), writes=[("cv",)])
        T.dma("pool", dict(out=cb[:, :], in_=cbf[:, :]), writes=[("cb",)])
        T.dma("pool", dict(out=cs[:, :, :], in_=rope[:, :, :]), writes=[("cs",)])
        T.op("act", "activation", dict(out=esink[:, :], in_=cv[:, CV_SINK:CV_SINK + 16], func=AF.Exp),
             reads=[("cv",)], writes=[("esink",)])
        ones = cb[:, CB_ONES:CB_ONES + 128]
        bo = cb[:, CB_BO:CB_BO + 128]
        rt = cb[:, CB_RT:CB_RT + 128]
        ident = cb[:, CB_ID:CB_ID + 128]
        mb2 = cb[:, CB_MB:CB_MB + 256].rearrange("p (a q) -> p a q", a=2)
        epsc = cv[:, CV_EPS:CV_EPS + 1]

        def ogkeys(c, j):
            return [("og", c, n) for n in range(4 * j, 4 * j + 4)]

        def emit_norm(l, j):
            T.phase = "norm"
            b = nb()
            nsq, nrs = ntmp["sq"], ntmp["rs"]
            for k in range(KC):
                T.op("act", "activation", dict(out=nsq[k % 2][:, :], in_=xT[:, k, ts(j)], func=AF.Square),
                     reads=[("x", k, j)], writes=[("sq", k % 2)])
                T.op("pe", "matmul", dict(out=ps[b][:, :], lhsT=ones, rhs=nsq[k % 2][:, :],
                                          start=(k == 0), stop=(k == KC - 1)),
                     reads=[("sq", k % 2), ("cb",)], writes=[("ps", b)])
            T.op("act", "activation", dict(out=nrs[:, :], in_=ps[b][:, :], func=AF.Ln, scale=1.0 / D, bias=epsc),
                 reads=[("cv",)], writes=[("ps", b), ("rstd",)])
            T.op("act", "activation", dict(out=nrs[:, :], in_=nrs[:, :], func=AF.Exp, scale=-0.5),
                 writes=[("rstd",)])
            for k in range(KC):
                T.op("dve", "scalar_tensor_tensor",
                     dict(out=hT[:, k, ts(j)], in0=xT[:, k, ts(j)], scalar=cv[:, CV_NORM + l * 8 + k:CV_NORM + l * 8 + k + 1],
                          in1=nrs[:, :], op0=ALU.mult, op1=ALU.mult),
                     reads=[("x", k, j), ("rstd",), ("cv",)], writes=[("h", k, j)])

        def emit_proj(slot, j, b, src, srckeys):
            T.grp("pe", [("matmul", dict(out=ps[b][:, :], lhsT=ring[slot][:, k, :], rhs=src[:, k, ts(j)],
                                         start=(k == 0), stop=(k == KC - 1))) for k in range(KC)],
                  reads=[("w", slot)] + srckeys, writes=[("ps", b)])

        def hkeys(j):
            return [("h", k, j) for k in range(KC)]

        def emit_outproj(l, j):
            T.phase = "outproj"
            srck = [key for k in range(KC) for key in ogkeys(k, j)]
            for co in range(KC):
                slot = wuse(_layer_chunk_base(l) + (22 if l % 2 == 0 else 32) + co)
                b = nb()
                emit_proj(slot, j, b, ogT, srck)
                wrel(slot)
                T.op("dve", "tensor_tensor", dict(out=xT[:, co, ts(j)], in0=ps[b][:, :], in1=xT[:, co, ts(j)], op=ALU.add),
                     reads=[], writes=[("ps", b), ("x", co, j)])

        def emit_qk_stageA(item):
            T.phase = "qkA"
            slot, j, i = item["slot"], item["j"], item["i"]
            if slot is None:
                slot = item["slotref"][0] = wuse(item["cid"]) if item["slotref"][0] is None else item["slotref"][0]
            b = i % 2
            item["b"] = b
            emit_proj(slot, j, b, hT, hkeys(j))
            if j == NT - 1:
                wrel(slot)
            T.op("dve", "tensor_copy", dict(out=qg[i % 2][:, :], in_=ps[b][:, :]),
                 writes=[("ps", b), ("qg", i % 2)])
            T.op("dve", "tensor_tensor", dict(out=sq[i % 2][:, :], in0=qg[i % 2][:, :], in1=qg[i % 2][:, :], op=ALU.mult),
                 reads=[("qg", i % 2)], writes=[("sq", i % 2)])
            for qq, (dst0, src0) in enumerate(((0, 32), (32, 0), (64, 96), (96, 64))):
                T.dma("sp", dict(out=qrot[i % 2][dst0:dst0 + 32, :], in_=qg[i % 2][src0:src0 + 32, :]),
                      reads=[("qg", i % 2)], writes=[("qrot", i % 2, qq)])

        def emit_qk_stageB(item):
            T.phase = "qkB"
            j, i, b = item["j"], item["i"], item["b"]
            gcol = item["gcol"]
            b2 = 2
            T.op("pe", "matmul", dict(out=ps[b2][:, :], lhsT=bo, rhs=sq[i % 2][:, :], start=True, stop=True),
                 reads=[("sq", i % 2), ("cb",)], writes=[("ps", b2)])
            rs = rstd2[i % 2]
            rkey = ("rstd",)
            T.op("act", "activation", dict(out=rs[:, :], in_=ps[b2][:, :], func=AF.Ln, scale=1.0 / 64, bias=epsc),
                 reads=[("cv",)], writes=[("ps", b2), rkey])
            T.op("act", "activation", dict(out=rs[:, :], in_=rs[:, :], func=AF.Exp, scale=-0.5),
                 writes=[rkey])
            T.op("dve", "scalar_tensor_tensor",
                 dict(out=t1[:, :], in0=ps[b][:, :], scalar=gcol, in1=cs[:, 0, ts(j)], op0=ALU.mult, op1=ALU.mult),
                 reads=[("cv",), ("cs",)], writes=[("ps", b), ("t1",)])
            T.op("dve", "scalar_tensor_tensor",
                 dict(out=t2[:, :], in0=qrot[i % 2][:, :], scalar=item["gpcol"], in1=cs[:, 1, ts(j)], op0=ALU.mult, op1=ALU.mult),
                 reads=[("cv",), ("cs",)] + [("qrot", i % 2, qq) for qq in range(4)], writes=[("t2",)])
            T.op("pool", "tensor_tensor", dict(out=t1[:, :], in0=t1[:, :], in1=t2[:, :], op=ALU.add),
                 reads=[("t2",)], writes=[("t1",)])
            T.op("pool", "tensor_tensor", dict(out=item["dst"], in0=t1[:, :], in1=rs[:, :], op=ALU.mult),
                 reads=[("t1",), rkey], writes=[item["dkey"]])

        def emit_S(a, g, m):
            T.phase = "S"
            bq = g % 2
            slot = m % NP
            lo, hi = max(m - 1, 0), min(m + 1, NBLK - 1)
            qs, qe = lo * 128, (hi + 1) * 128
            off = (lo - (m - 1)) * 128
            w = qe - qs
            qkeys_j = sorted(set([qs // TW, (qe - 1) // TW]))
            banks = [4, 5, 6, 7]
            for hh in range(4):
                cc, par = hh // 2, hh % 2
                rows = slice(par * 64, (par + 1) * 64)
                b = banks[hh]
                T.op("pe", "matmul", dict(out=ps[b][:, off:off + w], lhsT=kr[bq][rows, m * 128:(m + 1) * 128],
                                          rhs=qr[bq][rows, cc, qs:qe], start=True, stop=False),
                     reads=[("k", bq, m // 4)] + [("q", bq, cc, jj) for jj in qkeys_j], writes=[("ps", b)])
            for hh in range(4):
                b = banks[hh]
                if 1 <= m <= NBLK - 2:
                    outv = ps[b][:, 0:512].rearrange("p (a q) -> p a q", a=2)[:, :, 0:128]
                    rhs = mb2
                elif m == 0:
                    outv, rhs = ps[b][:, 256:384], mb2[:, 1, :]
                else:
                    outv, rhs = ps[b][:, 0:128], mb2[:, 0, :]
                T.op("pe", "matmul", dict(out=outv, lhsT=ident, rhs=rhs, start=False, stop=True),
                     reads=[("cb",)], writes=[("ps", b)])
            s4 = psall[:, 4 * TW:8 * TW].rearrange("p (h c) -> p h c", h=4)
            T.op("act", "activation", dict(out=Pt[slot][:, :, off:off + w], in_=s4[:, :, off:off + w],
                                           func=AF.Exp, scale=0.125),
                 writes=[("ps", 4), ("ps", 5), ("ps", 6), ("ps", 7), ("p", slot)])

        def emit_PV(a, g, n):
            T.phase = "PV"
            b = 3
            mms = [mm for mm in (n - 1, n, n + 1) if 0 <= mm < NBLK]
            instrs = []
            for kind in range(2):
                for idx, mm in enumerate(mms):
                    cbk = n - mm + 1
                    for par in range(2):
                        rows = slice(par * 64, (par + 1) * 64)
                        outv = ps[b][rows, kind * 256:(kind + 1) * 256].rearrange("p (a q) -> p a q", a=2)
                        rhs = Pt[mm % NP][:, par::2, cbk * 128:(cbk + 1) * 128]
                        lhsT = Vt[:, mm * 256 + g * 64: mm * 256 + g * 64 + 64] if kind == 0 else cb[:, CB_ONES:CB_ONES + 64]
                        instrs.append(("matmul", dict(out=outv, lhsT=lhsT, rhs=rhs, start=(idx == 0),
                                                      stop=(idx == len(mms) - 1), tile_position=(0, par * 64))))
            T.grp("pe", instrs, reads=[("p", mm % NP) for mm in mms] + [("v", mm // 2) for mm in mms] + [("cb",)],
                  writes=[("ps", b)])
            sc = (a * 4 + g) * 2
            for jj in range(2):
                T.op("act", "activation",
                     dict(out=rdn[:, jj * 128:(jj + 1) * 128], in_=ps[b][:, 256 + jj * 128:256 + (jj + 1) * 128],
                          func=AF.Ln, bias=esink[:, sc + jj:sc + jj + 1]),
                     reads=[("esink",)], writes=[("ps", b), ("rdn",)])
            T.op("act", "activation", dict(out=rdn[:, :], in_=rdn[:, :], func=AF.Exp, scale=-1.0), writes=[("rdn",)])
            T.op("dve", "tensor_tensor", dict(out=d1[:, :], in0=ps[b][:, 0:256], in1=rdn[:, :], op=ALU.mult),
                 reads=[("rdn",)], writes=[("ps", b), ("d1",)])
            ogv = ogT[:, 2 * g:2 * g + 2, n * 128:(n + 1) * 128]
            T.op("pool", "tensor_tensor", dict(out=ogv, in0=d1[:, :].rearrange("p (a q) -> p a q", a=2), in1=ogv, op=ALU.mult),
                 reads=[("d1",)], writes=[("og", 2 * g, n), ("og", 2 * g + 1, n)])

        def interleave(la, lb):
            ia = ib = 0
            while ia < len(la) or ib < len(lb):
                fa = ia / len(la) if la else 2.0
                fb = ib / len(lb) if lb else 2.0
                if ib >= len(lb) or (ia < len(la) and fa <= fb):
                    la[ia]()
                    ia += 1
                else:
                    lb[ib]()
                    ib += 1

        def emit_attn_layer(l):
            a = l // 2
            wbase = _layer_chunk_base(l)

            def proj_work(g):
                th = []
                bq = g % 2

                def gate_tile(c, j, ref):
                    def f():
                        T.phase = "gate"
                        if ref[0] is None:
                            ref[0] = wuse(wbase + c)
                        b = nb("hi")
                        emit_proj(ref[0], j, b, hT, hkeys(j))
                        if j == NT - 1:
                            wrel(ref[0])
                        T.op("act", "activation", dict(out=ogT[:, c, ts(j)], in_=ps[b][:, :], func=AF.Silu),
                             writes=[("ps", b)] + ogkeys(c, j))
                    return f
                gate_th = []
                for c in (range(KC) if g == 0 else ()):
                    ref = [None]
                    for j in range(NT):
                        gate_th.append(gate_tile(c, j, ref))
                if g == 0:
                    vref = [None, None]

                    def v_tile(tb2):
                        def f():
                            T.phase = "V"
                            if vref[0] is None:
                                vref[0] = wuse(wbase + 8)
                                vref[1] = wuse(wbase + 9)
                            b = nb()
                            instrs = []
                            for t in range(2):
                                tb = 2 * tb2 + t
                                for vc in range(2):
                                    for k in range(KC):
                                        instrs.append(("matmul", dict(out=ps[b][:, t * 256 + vc * 128:t * 256 + (vc + 1) * 128],
                                                                      lhsT=hT[:, k, tb * 128:(tb + 1) * 128], rhs=ring[vref[vc]][:, k, :],
                                                                      start=(k == 0), stop=(k == KC - 1))))
                            T.grp("pe", instrs, reads=[("w", vref[0]), ("w", vref[1])] + hkeys(tb2 // 2), writes=[("ps", b)])
                            if tb2 == 7:
                                wrel(vref[0])
                                wrel(vref[1])
                            T.op("act", "activation", dict(out=Vt[:, tb2 * 512:(tb2 + 1) * 512], in_=ps[b][:, :], func=AF.Copy),
                                 writes=[("ps", b), ("v", tb2)])
                        return f
                    for tb2 in range(8):
                        th.append(v_tile(tb2))
                items = []
                for kind, cc in (("q", 0), ("q", 1), ("k", 0)):
                    ref = [None]
                    for j in range(NT):
                        if kind == "q":
                            dst, dkey = qr[bq][:, cc, ts(j)], ("q", bq, cc, j)
                            gcol = cv[:, CV_QG + a:CV_QG + a + 1]
                            gpcol = cv[:, CV_QGP + a:CV_QGP + a + 1]
                        else:
                            dst, dkey = kr[bq][:, ts(j)], ("k", bq, j)
                            gcol = cv[:, CV_KG + a:CV_KG + a + 1]
                            gpcol = cv[:, CV_KGP + a:CV_KGP + a + 1]
                        items.append(dict(kind=kind, slot=None, slotref=ref, j=j, i=len(items), dst=dst, dkey=dkey, gcol=gcol, gpcol=gpcol,
                                          cid=wbase + (10 + 2 * g + cc if kind == "q" else 18 + g)))

                def qk_step(i):
                    def f():
                        if i < len(items):
                            emit_qk_stageA(items[i])
                        if i >= 1:
                            emit_qk_stageB(items[i - 1])
                    return f
                qk_th = [qk_step(i) for i in range(len(items) + 1)]
                if g == 0:
                    qi = 0
                    for c in range(KC):
                        th.extend(gate_th[4 * c:4 * c + 4])
                        nq = 2 if c < 6 else (1 if c == 6 else 0)
                        th.extend(qk_th[qi:qi + nq])
                        qi += nq
                    assert qi == len(qk_th)
                else:
                    th.extend(qk_th)
                return th

            def attn_work(g):
                th = []

                def step_f(step):
                    def f():
                        if step < NBLK:
                            emit_S(a, g, step)
                        if step >= 2:
                            emit_PV(a, g, step - 2)
                    return f
                for step in range(NBLK + 2):
                    th.append(step_f(step))
                return th

            interleave(proj_work(0), [])
            for g in range(4):
                interleave(attn_work(g), proj_work(g + 1) if g + 1 < 4 else [])

        def emit_conv_layer(l):
            bi = l // 2
            T.phase = "conv"
            for c in range(KC):
                zi = c % 2
                z = zb[zi]
                cb0 = _layer_chunk_base(l)
                slots = [wuse(cb0 + 8 + c), wuse(cb0 + 16 + c), wuse(cb0 + 24 + c), wuse(cb0 + c)]
                wc = CV_CONV + (bi * 8 + c) * 3

                def conv(j):
                    rk = [("z", zi, jj) for jj in (j - 1, j, j + 1) if 0 <= jj < NT] + [("zpad", zi)]
                    T.op("act", "activation", dict(out=a0[:, :], in_=z[:, j * TW:j * TW + TW], func=AF.Copy, scale=cv[:, wc:wc + 1]),
                         reads=rk + [("cv",)], writes=[("a0",)])
                    T.op("dve", "scalar_tensor_tensor",
                         dict(out=a1[:, :], in0=z[:, 1 + j * TW:1 + j * TW + TW], scalar=cv[:, wc + 1:wc + 2], in1=a0[:, :],
                              op0=ALU.mult, op1=ALU.add),
                         reads=rk + [("a0",), ("cv",)], writes=[("a1",)])
                    T.op("dve", "scalar_tensor_tensor",
                         dict(out=a0[:, :], in0=z[:, 2 + j * TW:2 + j * TW + TW], scalar=cv[:, wc + 2:wc + 3], in1=a1[:, :],
                              op0=ALU.mult, op1=ALU.add),
                         reads=rk + [("a1",), ("cv",)], writes=[("a0",)])
                    T.op("pool", "tensor_tensor", dict(out=ogT[:, c, ts(j)], in0=a0[:, :], in1=bsb[j % 2][:, :], op=ALU.mult),
                         reads=[("a0",), ("bs", j % 2)], writes=ogkeys(c, j))

                for j in range(NT):
                    bcg, bu, bgt, bbg = nb(), nb(), nb(), nb()
                    for si, bb in enumerate((bcg, bu, bgt, bbg)):
                        emit_proj(slots[si], j, bb, hT, hkeys(j))
                        if j == NT - 1:
                            wrel(slots[si])
                    T.op("act", "activation", dict(out=u_sb[:, :], in_=ps[bu][:, :], func=AF.Copy),
                         writes=[("ps", bu), ("u",)])
                    T.op("act", "activation", dict(out=sgb[:, :], in_=ps[bgt][:, :], func=AF.Silu),
                         writes=[("ps", bgt), ("sg",)])
                    T.op("dve", "tensor_tensor", dict(out=z[:, 1 + j * TW:1 + (j + 1) * TW], in0=ps[bcg][:, :], in1=u_sb[:, :], op=ALU.mult),
                         reads=[("u",)], writes=[("ps", bcg), ("z", zi, j)])
                    T.op("dve", "tensor_tensor", dict(out=bsb[j % 2][:, :], in0=ps[bbg][:, :], in1=sgb[:, :], op=ALU.mult),
                         reads=[("sg",)], writes=[("ps", bbg), ("bs", j % 2)])
                    if j >= 1:
                        conv(j - 1)
                conv(NT - 1)

        cur_scratch = None

        def set_scratch(kind):
            nonlocal cur_scratch
            if cur_scratch == kind:
                return
            old = A_KEYS if cur_scratch == "A" else (B_KEYS if cur_scratch == "B" else [])
            new = A_KEYS if kind == "A" else B_KEYS
            retire(old, new)
            cur_scratch = kind
            ntmp["sq"], ntmp["rs"] = (sq, rstd) if kind == "A" else (sq_b, rstd_b)
            if kind == "B":
                for i in range(2):
                    T.op("pool", "memset", dict(ap=zb[i][:, 0:1], constant=0.0), writes=[("zpad", i)])
                    T.op("pool", "memset", dict(ap=zb[i][:, ZW - 1:ZW], constant=0.0), writes=[("zpad", i)])

        def emit_all():
            nonlocal cur_scratch
            cur_scratch = None
            state.clear()
            state.update({"bank": 0, "wuse": 0, "wissued": 0, "low": 0, "live": {}, "done": set()})
            for s_ in range(nseq):
                for j in range(NT):
                    for k in range(KC):
                        T.dma("sp", dict(out=xT[:, k, ts(j)], in_=xin[s_, k * 128:(k + 1) * 128, ts(j)]),
                              writes=[("x", k, j)])
                for li, l in enumerate(layers):
                    if li == 0:
                        if cur_scratch is None:
                            set_scratch("A" if l % 2 == 0 else "B")
                        for j in range(NT):
                            emit_norm(l, j)
                    set_scratch("A" if l % 2 == 0 else "B")
                    if l % 2 == 0:
                        emit_attn_layer(l)
                    else:
                        emit_conv_layer(l)
                    nxt = layers[li + 1] if li + 1 < len(layers) else None
                    for step in range(NT + 1):
                        if step < NT:
                            emit_outproj(l, step)
                        if step >= 1:
                            j = step - 1
                            if nxt is not None:
                                emit_norm(nxt, j)
                            else:
                                for k in range(KC):
                                    T.dma("sp", dict(out=yout[s_, k * 128:(k + 1) * 128, ts(j)], in_=xT[:, k, ts(j)]),
                                          reads=[("x", k, j)])

        T.dry = True
        emit_all()
        T.dry = False
        emit_all()
        if debug:
            T.dma("sp", dict(out=dbg_h[:, :, :], in_=hT[:, :, :]), reads=[("h", k, j) for k in range(KC) for j in range(NT)])
            T.dma("sp", dict(out=dbg_og[:, :, :], in_=ogT[:, :, :]), reads=[("og", k, n) for k in range(KC) for n in range(NBLK)])
        T.finish("sp")
        T.finish("pool")
        T.finish("act")
        T.finish("dve")
        T.finish("pe")
    return nc


def _chunk(wcols):
    return np.ascontiguousarray(wcols.reshape(KC, 128, 128).transpose(1, 0, 2))


def _prep_weights(a_w_in, a_w_out, b_w_in, b_w_out):
    out = np.empty((NCH, 128, KC, 128), np.float32)
    for l in range(DEPTH):
        base = _layer_chunk_base(l)
        s = l // 2
        if l % 2 == 0:
            w = a_w_in[s]
            for c in range(8):
                out[base + c] = _chunk(w[:, 1536 + c * 128:1536 + (c + 1) * 128])
            for vc in range(2):
                out[base + 8 + vc] = _chunk(w[:, 1280 + vc * 128:1280 + (vc + 1) * 128])
            for c in range(8):
                out[base + 10 + c] = _chunk(w[:, c * 128:(c + 1) * 128])
            for g in range(4):
                kg = w[:, 1024 + g * 64:1024 + (g + 1) * 64]
                out[base + 18 + g] = _chunk(np.concatenate([kg, kg], axis=1))
            for co in range(8):
                out[base + 22 + co] = _chunk(a_w_out[s][:, co * 128:(co + 1) * 128])
        else:
            w = b_w_in[s]
            for c in range(32):
                out[base + c] = _chunk(w[:, c * 128:(c + 1) * 128])
            for co in range(8):
                out[base + 32 + co] = _chunk(b_w_out[s][:, co * 128:(co + 1) * 128])
    return out


def _prep_consts(norm_g, a_q_norm, a_k_norm, a_sink, b_conv):
    cvec = np.zeros((128, NCV), np.float32)
    p = np.arange(128)
    for l in range(DEPTH):
        for k in range(KC):
            cvec[:, CV_NORM + l * 8 + k] = norm_g[l, k * 128:(k + 1) * 128]
    for a in range(2):
        cvec[:, CV_QG + a] = a_q_norm[a][p % 64]
        cvec[:, CV_KG + a] = a_k_norm[a][p % 64]
        cvec[:, CV_QGP + a] = a_q_norm[a][(p % 64 + 32) % 64]
        cvec[:, CV_KGP + a] = a_k_norm[a][(p % 64 + 32) % 64]
        for g in range(4):
            for j in range(2):
                cvec[:, CV_SINK + (a * 4 + g) * 2 + j] = a_sink[a][4 * g + 2 * j + (p >= 64)]
    for b in range(2):
        for c in range(KC):
            for t in range(3):
                cvec[:, CV_CONV + (b * 8 + c) * 3 + t] = b_conv[b, t, c * 128:(c + 1) * 128]
    cvec[:, CV_EPS] = EPS

    cbf = np.zeros((128, NCB), np.float32)
    cbf[:, CB_ONES:CB_ONES + 128] = 1.0
    cbf[:, CB_BO:CB_BO + 128] = (p[:, None] // 64 == p[None, :] // 64)
    rtm = np.zeros((128, 128), np.float32)
    for i in range(128):
        if i % 64 < 32:
            rtm[i + 32, i] = -1.0
        else:
            rtm[i - 32, i] = 1.0
    cbf[:, CB_RT:CB_RT + 128] = rtm
    cbf[:, CB_ID:CB_ID + 128] = np.eye(128, dtype=np.float32)
    NEG = -30000.0
    cbf[:, CB_MB:CB_MB + 128] = np.where(p[:, None] <= p[None, :], 0.0, NEG)
    cbf[:, CB_MB + 128:CB_MB + 256] = np.where(p[None, :] <= p[:, None], 0.0, NEG)

    inv_freq = 10000.0 ** (-np.arange(0, 64, 2, dtype=np.float64) / 64)
    ang = np.arange(S, dtype=np.float64)[:, None] * inv_freq[None, :]
    cosT = np.cos(ang).astype(np.float32).T
    sinT = np.sin(ang).astype(np.float32).T
    rope = np.empty((128, 2, S), np.float32)
    rope[:, 0, :] = cosT[p % 32]
    rope[:, 1, :] = sinT[p % 32] * np.where(p % 64 < 32, -1.0, 1.0)[:, None]
    return cvec, cbf, rope


def _run(inputs, layers, seqs=None, ncores=NCORES, debug=False):
    x = np.asarray(inputs["x"], np.float32)
    wts = _prep_weights(np.asarray(inputs["a_w_in"], np.float32), np.asarray(inputs["a_w_out"], np.float32),
                        np.asarray(inputs["b_w_in"], np.float32), np.asarray(inputs["b_w_out"], np.float32))
    cvec, cbf, rope = _prep_consts(np.asarray(inputs["norm_g"], np.float32), np.asarray(inputs["a_q_norm"], np.float32),
                                   np.asarray(inputs["a_k_norm"], np.float32), np.asarray(inputs["a_sink"], np.float32),
                                   np.asarray(inputs["b_conv"], np.float32))
    nseq = SEQ_PER_CORE if seqs is None else seqs
    xT = np.ascontiguousarray(x.transpose(0, 2, 1))
    import time as _t
    _t0 = _t.time()
    nc = build_program(list(layers), nseq=nseq, debug=debug)
    print("[kernel] build %.1fs" % (_t.time() - _t0), flush=True)
    in_maps = []
    for c in range(ncores):
        in_maps.append({"xin": np.ascontiguousarray(xT[c * nseq:(c + 1) * nseq]), "wts": wts, "cvec": cvec, "cbf": cbf, "rope": rope})
    res = run_bass_kernel_spmd(nc, in_maps, core_ids=list(range(ncores)))
    if debug:
        return res
    outT = np.concatenate([r["yout"] for r in res.results], axis=0)
    return np.ascontiguousarray(outT.transpose(0, 2, 1)).astype(np.float32)


def kernel(x, norm_g, a_w_in, a_q_norm, a_k_norm, a_sink, a_w_out, b_w_in, b_conv, b_w_out):
    inputs = dict(x=x, norm_g=norm_g, a_w_in=a_w_in, a_q_norm=a_q_norm, a_k_norm=a_k_norm, a_sink=a_sink,
                  a_w_out=a_w_out, b_w_in=b_w_in, b_conv=b_conv, b_w_out=b_w_out)
    return _run(inputs, layers=list(range(DEPTH)))
```

```python
import contextlib
import numpy as np
import concourse.bass as bass
import concourse.mybir as mybir
from concourse.bass_utils import run_bass_kernel_spmd

F32 = mybir.dt.float32
BF16 = mybir.dt.bfloat16
ALU = mybir.AluOpType
AF = mybir.ActivationFunctionType

D = 1024
S = 2048
BATCH = 16
NCORES = 8
SEQ_PER_CORE = BATCH // NCORES
DEPTH = 4
NT = 4
TW = 512
KC = 8
NBLK = 16
EPS = 1e-6
NS = 5
NP = 4
NDS = 24

CV_NORM = 0
CV_QG = 32
CV_KG = 34
CV_CONV = 36
CV_SINK = 84
CV_EPS = 100
CV_QGP = 101
CV_KGP = 103
NCV = 105
CB_ONES = 0
CB_BO = 128
CB_RT = 256
CB_ID = 384
CB_MB = 512
NCB = 768

A_CH = 30
B_CH = 40


def _layer_chunk_base(l):
    base = 0
    for i in range(l):
        base += A_CH if i % 2 == 0 else B_CH
    return base


NCH = _layer_chunk_base(DEPTH)


class Tok:
    __slots__ = ("sem", "val", "eng", "key")

    def __init__(self, sem, val, eng, key):
        self.sem, self.val, self.eng, self.key = sem, val, eng, key


class Trk:
    def __init__(self, nc, es):
        self.nc = nc
        self.eng = {"pe": nc.tensor, "act": nc.scalar, "dve": nc.vector,
                    "pool": nc.gpsimd, "sp": nc.sync}
        self.sem = {e: es.enter_context(nc.semaphore("s_" + e))
                    for e in ("pe", "act", "dve", "pool")}
        self.cnt = {e: 0 for e in self.sem}
        self.dsem = [es.enter_context(nc.semaphore("s_dma%d" % i)) for i in range(NDS)]
        self.dcnt = [0] * NDS
        self.dnext = {"pool": 0, "sp": 0}
        self.waited = {e: {} for e in self.eng}
        self.lw = {}
        self.rd = {}
        self.nwaits = 0
        self.phase = ""
        self.annotate = False
        self.dry = False

    def _wait(self, e, tok):
        if tok.eng == "pe" and e == "pe":
            return
        w = self.waited[e]
        if w.get(tok.key, 0) >= tok.val:
            return
        w[tok.key] = tok.val
        self.eng[e].wait_ge(tok.sem, tok.val)
        self.nwaits += 1

    def _deps(self, e, reads, writes):
        for r in reads:
            t = self.lw.get(r)
            if t is not None:
                self._wait(e, t)
        for w in writes:
            t = self.lw.get(w)
            if t is not None:
                self._wait(e, t)
            for t in self.rd.get(w, {}).values():
                self._wait(e, t)

    def _commit(self, tok, reads, writes):
        for r in reads:
            d = self.rd.setdefault(r, {})
            o = d.get(tok.key)
            if o is None or o.val < tok.val:
                d[tok.key] = tok
        for w in writes:
            self.lw[w] = tok
            self.rd[w] = {}

    def grp(self, e, instrs, reads=(), writes=()):
        if self.dry:
            return None
        self._deps(e, reads, writes)
        eng = self.eng[e]
        ins = None
        for name, kw in instrs:
            ins = getattr(eng, name)(**kw)
            if self.annotate:
                ins.annotate(self.phase)
        self.cnt[e] += 1
        ins.then_inc(self.sem[e], 1)
        tok = Tok(self.sem[e], self.cnt[e], e, e)
        self._commit(tok, reads, writes)
        return tok

    def op(self, e, name, kw, reads=(), writes=()):
        return self.grp(e, [(name, kw)], reads, writes)

    def dma(self, e, kw, reads=(), writes=()):
        if self.dry:
            return None
        half = NDS // 2
        j = self.dnext[e]
        self.dnext[e] = (j + 1) % half
        i = j + (half if e == "pool" else 0)
        key = ("d", i)
        if self.dcnt[i] > 0:
            self._wait(e, Tok(self.dsem[i], 16 * self.dcnt[i], "dma", key))
        self._deps(e, reads, writes)
        ins = self.eng[e].dma_start(**kw)
        if self.annotate:
            ins.annotate(self.phase + "/dma")
        self.dcnt[i] += 1
        ins.then_inc(self.dsem[i], 16)
        tok = Tok(self.dsem[i], 16 * self.dcnt[i], "dma", key)
        self._commit(tok, reads, writes)
        return tok

    def finish(self, e="sp"):
        for x in self.sem:
            if self.cnt[x] > 0:
                self._wait(e, Tok(self.sem[x], self.cnt[x], x, x))
        for i in range(NDS):
            if self.dcnt[i] > 0:
                self._wait(e, Tok(self.dsem[i], 16 * self.dcnt[i], "dma", ("d", i)))


def build_program(layers, nseq=SEQ_PER_CORE, debug=False, annotate=False):
    nc = bass.Bass("TRN2", target_bir_lowering=False)
    xin = nc.dram_tensor("xin", [nseq, D, S], F32, kind="ExternalInput").ap()
    wts = nc.dram_tensor("wts", [NCH, 128, KC, 128], F32, kind="ExternalInput").ap()
    cvec = nc.dram_tensor("cvec", [128, NCV], F32, kind="ExternalInput").ap()
    cbf = nc.dram_tensor("cbf", [128, NCB], F32, kind="ExternalInput").ap()
    rope = nc.dram_tensor("rope", [128, 2, S], F32, kind="ExternalInput").ap()
    yout = nc.dram_tensor("yout", [nseq, D, S], F32, kind="ExternalOutput").ap()
    if debug:
        dbg_h = nc.dram_tensor("dbg_h", [128, KC, S], BF16, kind="ExternalOutput").ap()
        dbg_og = nc.dram_tensor("dbg_og", [128, KC, S], BF16, kind="ExternalOutput").ap()

    es = contextlib.ExitStack()
    with es:
        def sb(name, shape, dt):
            return es.enter_context(nc.sbuf_tensor(name, shape, dt))

        xT = sb("xT", [128, KC, S], F32)
        hT = sb("hT", [128, KC, S], BF16)
        ogT = sb("ogT", [128, KC, S], BF16)
        ring = [sb("ring%d" % i, [128, KC, 128], BF16) for i in range(NS)]
        cs = sb("cs", [128, 2, S], BF16)
        cv = sb("cv", [128, NCV], F32)
        esink = sb("esink", [128, 16], F32)
        cb = sb("cb", [128, NCB], BF16)
        SCR_BYTES = 58 * 1024
        scr = sb("scr", [128, SCR_BYTES // 2], BF16)
        psall = es.enter_context(nc.psum_tensor("psall", [128, 8 * TW], F32))
        ps = [psall[:, i * TW:(i + 1) * TW] for i in range(8)]

        T = Trk(nc, es)
        T.annotate = annotate

        class Carve:
            def __init__(self):
                self.off = 0

            def take(self, nbytes, dt):
                nbytes = (nbytes + 31) // 32 * 32
                o = self.off
                self.off += nbytes
                assert self.off <= SCR_BYTES, (self.off, SCR_BYTES)
                v = scr[:, o // 2:(o + nbytes) // 2]
                if dt == F32:
                    v = v.bitcast(F32)
                return v

        ca = Carve()
        qr = [ca.take(2 * S * 2, BF16).rearrange("p (c t) -> p c t", c=2) for _ in range(2)]
        kr = [ca.take(S * 2, BF16) for _ in range(2)]
        Vt = ca.take(NBLK * 256 * 2, BF16)
        Pt = [ca.take(4 * 384 * 2, BF16).rearrange("p (h q) -> p h q", h=4) for _ in range(NP)]
        sq = [ca.take(TW * 2, BF16) for _ in range(2)]
        qg = [ca.take(TW * 2, BF16) for _ in range(2)]
        qrot = [ca.take(TW * 2, BF16) for _ in range(2)]
        rstd = ca.take(TW * 4, F32)
        rstd2 = [rstd, rstd]
        t1 = ca.take(TW * 4, F32)
        t2 = ca.take(TW * 4, F32)
        d1 = ca.take(256 * 4, F32)
        rdn = ca.take(256 * 4, F32)
        A_KEYS = ([("q", bq, c, j) for bq in range(2) for c in range(2) for j in range(NT)]
                  + [("k", bq, j) for bq in range(2) for j in range(NT)]
                  + [("v", i) for i in range(8)] + [("p", i) for i in range(NP)]
                  + [("sq", i) for i in range(2)] + [("qg", i) for i in range(2)]
                  + [("rstd",), ("rstd", 1), ("t1",), ("t2",), ("d1",), ("rdn",)] + [("qrot", i, qq) for i in range(2) for qq in range(4)])
        cbv = Carve()
        ZW = S + 2
        zb = [cbv.take(ZW * 4, F32) for _ in range(2)]
        u_sb = cbv.take(TW * 4, F32)
        sgb = cbv.take(TW * 4, F32)
        bsb = [cbv.take(TW * 4, F32) for _ in range(2)]
        a0 = cbv.take(TW * 4, F32)
        a1 = cbv.take(TW * 4, F32)
        sq_b = [cbv.take(TW * 2, BF16) for _ in range(2)]
        rstd_b = cbv.take(TW * 4, F32)
        B_KEYS = ([("z", i, j) for i in range(2) for j in range(NT)] + [("zpad", i) for i in range(2)]
                  + [("u",), ("sg",), ("bs", 0), ("bs", 1), ("a0",), ("a1",), ("sq", 0), ("sq", 1), ("rstd",)])
        ntmp = {"sq": sq, "rs": rstd}

        def retire(keys_old, keys_new):
            toks = {}
            for k in keys_old:
                t = T.lw.pop(k, None)
                cands = list(T.rd.pop(k, {}).values())
                if t is not None:
                    cands.append(t)
                for t in cands:
                    o = toks.get(t.key)
                    if o is None or o.val < t.val:
                        toks[t.key] = t
            for k in keys_new:
                T.lw.pop(k, None)
                T.rd[k] = dict(toks)

        state = {"bank": 0, "wuse": 0, "wissued": 0}

        def nb(stream=None):
            if stream == "qk":
                b = state.get("bank_qk", 0)
                state["bank_qk"] = (b + 1) % 4
                return b
            if stream == "hi":
                b = state.get("bank_hi", 0)
                state["bank_hi"] = (b + 1) % 5
                return 3 + b
            b = state["bank"]
            state["bank"] = (b + 1) % 8
            return b

        wplan = []

        def wprefetch(upto):
            upto = min(upto, len(wplan) - 1)
            while state["wissued"] <= upto:
                i = state["wissued"]
                slot = i % NS
                T.dma("pool", dict(out=ring[slot][:, :, :], in_=wts[wplan[i]]),
                      reads=[], writes=[("w", slot)])
                state["wissued"] += 1

        def wuse(cid):
            if T.dry:
                wplan.append(cid)
                return 0
            i = state["wuse"]
            state["wuse"] += 1
            assert wplan[i] == cid, (i, wplan[i], cid)
            assert i - NS < state["low"], ("weight ring over-subscribed", i, state["low"])
            wprefetch(i)
            state["live"][i % NS] = i
            return i % NS

        def wrel(slot):
            if T.dry:
                return
            state["done"].add(state["live"].pop(slot))
            while state["low"] in state["done"]:
                state["done"].remove(state["low"])
                state["low"] += 1
            wprefetch(state["low"] + NS - 1)

        def ts(j):
            return slice(j * TW, (j + 1) * TW)

        T.dma("sp", dict(out=cv[:, :], in_=cvec[:, :]), writes=[("cv",)])
        T.dma("pool", dict(out=cb[:, :], in_=cbf[:, :]), writes=[("cb",)])
        T.dma("pool", dict(out=cs[:, :, :], in_=rope[:, :, :]), writes=[("cs",)])
        T.op("act", "activation", dict(out=esink[:, :], in_=cv[:, CV_SINK:CV_SINK + 16], func=AF.Exp),
             reads=[("cv",)], writes=[("esink",)])
        ones = cb[:, CB_ONES:CB_ONES + 128]
        bo = cb[:, CB_BO:CB_BO + 128]
        rt = cb[:, CB_RT:CB_RT + 128]
        ident = cb[:, CB_ID:CB_ID + 128]
        mb2 = cb[:, CB_MB:CB_MB + 256].rearrange("p (a q) -> p a q", a=2)
        epsc = cv[:, CV_EPS:CV_EPS + 1]

        def ogkeys(c, j):
            return [("og", c, n) for n in range(4 * j, 4 * j + 4)]

        def emit_norm(l, j):
            T.phase = "norm"
            b = nb()
            nsq, nrs = ntmp["sq"], ntmp["rs"]
            for k in range(KC):
                T.op("act", "activation", dict(out=nsq[k % 2][:, :], in_=xT[:, k, ts(j)], func=AF.Square),
                     reads=[("x", k, j)], writes=[("sq", k % 2)])
                T.op("pe", "matmul", dict(out=ps[b][:, :], lhsT=ones, rhs=nsq[k % 2][:, :],
                                          start=(k == 0), stop=(k == KC - 1)),
                     reads=[("sq", k % 2), ("cb",)], writes=[("ps", b)])
            T.op("act", "activation", dict(out=nrs[:, :], in_=ps[b][:, :], func=AF.Ln, scale=1.0 / D, bias=epsc),
                 reads=[("cv",)], writes=[("ps", b), ("rstd",)])
            T.op("act", "activation", dict(out=nrs[:, :], in_=nrs[:, :], func=AF.Exp, scale=-0.5),
                 writes=[("rstd",)])
            for k in range(KC):
                T.op("dve", "scalar_tensor_tensor",
                     dict(out=hT[:, k, ts(j)], in0=xT[:, k, ts(j)], scalar=cv[:, CV_NORM + l * 8 + k:CV_NORM + l * 8 + k + 1],
                          in1=nrs[:, :], op0=ALU.mult, op1=ALU.mult),
                     reads=[("x", k, j), ("rstd",), ("cv",)], writes=[("h", k, j)])

        def emit_proj(slot, j, b, src, srckeys):
            T.grp("pe", [("matmul", dict(out=ps[b][:, :], lhsT=ring[slot][:, k, :], rhs=src[:, k, ts(j)],
                                         start=(k == 0), stop=(k == KC - 1))) for k in range(KC)],
                  reads=[("w", slot)] + srckeys, writes=[("ps", b)])

        def hkeys(j):
            return [("h", k, j) for k in range(KC)]

        def emit_outproj(l, j):
            T.phase = "outproj"
            srck = [key for k in range(KC) for key in ogkeys(k, j)]
            for co in range(KC):
                slot = wuse(_layer_chunk_base(l) + (22 if l % 2 == 0 else 32) + co)
                b = nb()
                emit_proj(slot, j, b, ogT, srck)
                wrel(slot)
                T.op("dve", "tensor_tensor", dict(out=xT[:, co, ts(j)], in0=ps[b][:, :], in1=xT[:, co, ts(j)], op=ALU.add),
                     reads=[], writes=[("ps", b), ("x", co, j)])

        def emit_qk_stageA(item):
            T.phase = "qkA"
            slot, j, i = item["slot"], item["j"], item["i"]
            if slot is None:
                slot = item["slotref"][0] = wuse(item["cid"]) if item["slotref"][0] is None else item["slotref"][0]
            b = i % 2
            item["b"] = b
            emit_proj(slot, j, b, hT, hkeys(j))
            if j == NT - 1:
                wrel(slot)
            T.op("dve", "tensor_copy", dict(out=qg[i % 2][:, :], in_=ps[b][:, :]),
                 writes=[("ps", b), ("qg", i % 2)])
            T.op("dve", "tensor_tensor", dict(out=sq[i % 2][:, :], in0=qg[i % 2][:, :], in1=qg[i % 2][:, :], op=ALU.mult),
                 reads=[("qg", i % 2)], writes=[("sq", i % 2)])
            for qq, (dst0, src0) in enumerate(((0, 32), (32, 0), (64, 96), (96, 64))):
                T.dma("sp", dict(out=qrot[i % 2][dst0:dst0 + 32, :], in_=qg[i % 2][src0:src0 + 32, :]),
                      reads=[("qg", i % 2)], writes=[("qrot", i % 2, qq)])

        def emit_qk_stageB(item):
            T.phase = "qkB"
            j, i, b = item["j"], item["i"], item["b"]
            gcol = item["gcol"]
            b2 = 2
            T.op("pe", "matmul", dict(out=ps[b2][:, :], lhsT=bo, rhs=sq[i % 2][:, :], start=True, stop=True),
                 reads=[("sq", i % 2), ("cb",)], writes=[("ps", b2)])
            rs = rstd2[i % 2]
            rkey = ("rstd",)
            T.op("act", "activation", dict(out=rs[:, :], in_=ps[b2][:, :], func=AF.Ln, scale=1.0 / 64, bias=epsc),
                 reads=[("cv",)], writes=[("ps", b2), rkey])
            T.op("act", "activation", dict(out=rs[:, :], in_=rs[:, :], func=AF.Exp, scale=-0.5),
                 writes=[rkey])
            T.op("dve", "scalar_tensor_tensor",
                 dict(out=t1[:, :], in0=ps[b][:, :], scalar=gcol, in1=cs[:, 0, ts(j)], op0=ALU.mult, op1=ALU.mult),
                 reads=[("cv",), ("cs",)], writes=[("ps", b), ("t1",)])
            T.op("dve", "scalar_tensor_tensor",
                 dict(out=t2[:, :], in0=qrot[i % 2][:, :], scalar=item["gpcol"], in1=cs[:, 1, ts(j)], op0=ALU.mult, op1=ALU.mult),
                 reads=[("cv",), ("cs",)] + [("qrot", i % 2, qq) for qq in range(4)], writes=[("t2",)])
            T.op("pool", "tensor_tensor", dict(out=t1[:, :], in0=t1[:, :], in1=t2[:, :], op=ALU.add),
                 reads=[("t2",)], writes=[("t1",)])
            T.op("pool", "tensor_tensor", dict(out=item["dst"], in0=t1[:, :], in1=rs[:, :], op=ALU.mult),
                 reads=[("t1",), rkey], writes=[item["dkey"]])

        def emit_S(a, g, m):
            T.phase = "S"
            bq = g % 2
            slot = m % NP
            lo, hi = max(m - 1, 0), min(m + 1, NBLK - 1)
            qs, qe = lo * 128, (hi + 1) * 128
            off = (lo - (m - 1)) * 128
            w = qe - qs
            qkeys_j = sorted(set([qs // TW, (qe - 1) // TW]))
            banks = [4, 5, 6, 7]
            for hh in range(4):
                cc, par = hh // 2, hh % 2
                rows = slice(par * 64, (par + 1) * 64)
                b = banks[hh]
                T.op("pe", "matmul", dict(out=ps[b][:, off:off + w], lhsT=kr[bq][rows, m * 128:(m + 1) * 128],
                                          rhs=qr[bq][rows, cc, qs:qe], start=True, stop=False),
                     reads=[("k", bq, m // 4)] + [("q", bq, cc, jj) for jj in qkeys_j], writes=[("ps", b)])
            for hh in range(4):
                b = banks[hh]
                pieces = []
                if m >= 1:
                    pieces.append((ps[b][:, 0:128], mb2[:, 0, :]))
                if m <= NBLK - 2:
                    pieces.append((ps[b][:, 256:384], mb2[:, 1, :]))
                T.grp("pe", [("matmul", dict(out=o_, lhsT=ident, rhs=r_, start=False, stop=(ii == len(pieces) - 1)))
                             for ii, (o_, r_) in enumerate(pieces)],
                      reads=[("cb",)], writes=[("ps", b)])
            s4 = psall[:, 4 * TW:8 * TW].rearrange("p (h c) -> p h c", h=4)
            T.op("act", "activation", dict(out=Pt[slot][:, :, off:off + w], in_=s4[:, :, off:off + w],
                                           func=AF.Exp, scale=0.125),
                 writes=[("ps", 4), ("ps", 5), ("ps", 6), ("ps", 7), ("p", slot)])

        def emit_PV(a, g, n):
            T.phase = "PV"
            b = 3
            mms = [mm for mm in (n - 1, n, n + 1) if 0 <= mm < NBLK]
            instrs = []
            for kind in range(2):
                for idx, mm in enumerate(mms):
                    cbk = n - mm + 1
                    for par in range(2):
                        rows = slice(par * 64, (par + 1) * 64)
                        outv = ps[b][rows, kind * 256:(kind + 1) * 256].rearrange("p (a q) -> p a q", a=2)
                        rhs = Pt[mm % NP][:, par::2, cbk * 128:(cbk + 1) * 128]
                        lhsT = Vt[:, mm * 256 + g * 64: mm * 256 + g * 64 + 64] if kind == 0 else cb[:, CB_ONES:CB_ONES + 64]
                        instrs.append(("matmul", dict(out=outv, lhsT=lhsT, rhs=rhs, start=(idx == 0),
                                                      stop=(idx == len(mms) - 1), tile_position=(0, par * 64))))
            T.grp("pe", instrs, reads=[("p", mm % NP) for mm in mms] + [("v", mm // 2) for mm in mms] + [("cb",)],
                  writes=[("ps", b)])
            sc = (a * 4 + g) * 2
            for jj in range(2):
                T.op("act", "activation",
                     dict(out=rdn[:, jj * 128:(jj + 1) * 128], in_=ps[b][:, 256 + jj * 128:256 + (jj + 1) * 128],
                          func=AF.Ln, bias=esink[:, sc + jj:sc + jj + 1]),
                     reads=[("esink",)], writes=[("ps", b), ("rdn",)])
            T.op("act", "activation", dict(out=rdn[:, :], in_=rdn[:, :], func=AF.Exp, scale=-1.0), writes=[("rdn",)])
            T.op("dve", "tensor_tensor", dict(out=d1[:, :], in0=ps[b][:, 0:256], in1=rdn[:, :], op=ALU.mult),
                 reads=[("rdn",)], writes=[("ps", b), ("d1",)])
            ogv = ogT[:, 2 * g:2 * g + 2, n * 128:(n + 1) * 128]
            T.op("pool", "tensor_tensor", dict(out=ogv, in0=d1[:, :].rearrange("p (a q) -> p a q", a=2), in1=ogv, op=ALU.mult),
                 reads=[("d1",)], writes=[("og", 2 * g, n), ("og", 2 * g + 1, n)])

        def interleave(la, lb):
            ia = ib = 0
            while ia < len(la) or ib < len(lb):
                fa = ia / len(la) if la else 2.0
                fb = ib / len(lb) if lb else 2.0
                if ib >= len(lb) or (ia < len(la) and fa <= fb):
                    la[ia]()
                    ia += 1
                else:
                    lb[ib]()
                    ib += 1

        def emit_attn_layer(l):
            a = l // 2
            wbase = _layer_chunk_base(l)

            def proj_work(g):
                th = []
                bq = g % 2

                def gate_tile(c, j, ref):
                    def f():
                        T.phase = "gate"
                        if ref[0] is None:
                            ref[0] = wuse(wbase + c)
                        b = nb("hi")
                        emit_proj(ref[0], j, b, hT, hkeys(j))
                        if j == NT - 1:
                            wrel(ref[0])
                        T.op("act", "activation", dict(out=ogT[:, c, ts(j)], in_=ps[b][:, :], func=AF.Silu),
                             writes=[("ps", b)] + ogkeys(c, j))
                    return f
                gate_th = []
                for c in (range(KC) if g == 0 else ()):
                    ref = [None]
                    for j in range(NT):
                        gate_th.append(gate_tile(c, j, ref))
                if g == 0:
                    vref = [None, None]

                    def v_tile(tb2):
                        def f():
                            T.phase = "V"
                            if vref[0] is None:
                                vref[0] = wuse(wbase + 8)
                                vref[1] = wuse(wbase + 9)
                            b = nb()
                            instrs = []
                            for t in range(2):
                                tb = 2 * tb2 + t
                                for vc in range(2):
                                    for k in range(KC):
                                        instrs.append(("matmul", dict(out=ps[b][:, t * 256 + vc * 128:t * 256 + (vc + 1) * 128],
                                                                      lhsT=hT[:, k, tb * 128:(tb + 1) * 128], rhs=ring[vref[vc]][:, k, :],
                                                                      start=(k == 0), stop=(k == KC - 1))))
                            T.grp("pe", instrs, reads=[("w", vref[0]), ("w", vref[1])] + hkeys(tb2 // 2), writes=[("ps", b)])
                            if tb2 == 7:
                                wrel(vref[0])
                                wrel(vref[1])
                            T.op("act", "activation", dict(out=Vt[:, tb2 * 512:(tb2 + 1) * 512], in_=ps[b][:, :], func=AF.Copy),
                                 writes=[("ps", b), ("v", tb2)])
                        return f
                    for tb2 in range(8):
                        th.append(v_tile(tb2))
                items = []
                for kind, cc in (("q", 0), ("q", 1), ("k", 0)):
                    ref = [None]
                    for j in range(NT):
                        if kind == "q":
                            dst, dkey = qr[bq][:, cc, ts(j)], ("q", bq, cc, j)
                            gcol = cv[:, CV_QG + a:CV_QG + a + 1]
                            gpcol = cv[:, CV_QGP + a:CV_QGP + a + 1]
                        else:
                            dst, dkey = kr[bq][:, ts(j)], ("k", bq, j)
                            gcol = cv[:, CV_KG + a:CV_KG + a + 1]
                            gpcol = cv[:, CV_KGP + a:CV_KGP + a + 1]
                        items.append(dict(kind=kind, slot=None, slotref=ref, j=j, i=len(items), dst=dst, dkey=dkey, gcol=gcol, gpcol=gpcol,
                                          cid=wbase + (10 + 2 * g + cc if kind == "q" else 18 + g)))

                def qk_step(i):
                    def f():
                        if i < len(items):
                            emit_qk_stageA(items[i])
                        if i >= 1:
                            emit_qk_stageB(items[i - 1])
                    return f
                qk_th = [qk_step(i) for i in range(len(items) + 1)]
                if g == 0:
                    qi = 0
                    for c in range(KC):
                        th.extend(gate_th[4 * c:4 * c + 4])
                        nq = 2 if c < 6 else (1 if c == 6 else 0)
                        th.extend(qk_th[qi:qi + nq])
                        qi += nq
                    assert qi == len(qk_th)
                else:
                    th.extend(qk_th)
                return th

            def attn_work(g):
                th = []

                def step_f(step):
                    def f():
                        if step < NBLK:
                            emit_S(a, g, step)
                        if step >= 2:
                            emit_PV(a, g, step - 2)
                    return f
                for step in range(NBLK + 2):
                    th.append(step_f(step))
                return th

            interleave(proj_work(0), [])
            for g in range(4):
                interleave(attn_work(g), proj_work(g + 1) if g + 1 < 4 else [])

        def emit_conv_layer(l):
            bi = l // 2
            T.phase = "conv"
            for c in range(KC):
                zi = c % 2
                z = zb[zi]
                cb0 = _layer_chunk_base(l)
                slots = [wuse(cb0 + 8 + c), wuse(cb0 + 16 + c), wuse(cb0 + 24 + c), wuse(cb0 + c)]
                wc = CV_CONV + (bi * 8 + c) * 3

                def conv(j):
                    rk = [("z", zi, jj) for jj in (j - 1, j, j + 1) if 0 <= jj < NT] + [("zpad", zi)]
                    T.op("act", "activation", dict(out=a0[:, :], in_=z[:, j * TW:j * TW + TW], func=AF.Copy, scale=cv[:, wc:wc + 1]),
                         reads=rk + [("cv",)], writes=[("a0",)])
                    T.op("dve", "scalar_tensor_tensor",
                         dict(out=a1[:, :], in0=z[:, 1 + j * TW:1 + j * TW + TW], scalar=cv[:, wc + 1:wc + 2], in1=a0[:, :],
                              op0=ALU.mult, op1=ALU.add),
                         reads=rk + [("a0",), ("cv",)], writes=[("a1",)])
                    T.op("dve", "scalar_tensor_tensor",
                         dict(out=a0[:, :], in0=z[:, 2 + j * TW:2 + j * TW + TW], scalar=cv[:, wc + 2:wc + 3], in1=a1[:, :],
                              op0=ALU.mult, op1=ALU.add),
                         reads=rk + [("a1",), ("cv",)], writes=[("a0",)])
                    T.op("pool", "tensor_tensor", dict(out=ogT[:, c, ts(j)], in0=a0[:, :], in1=bsb[j % 2][:, :], op=ALU.mult),
                         reads=[("a0",), ("bs", j % 2)], writes=ogkeys(c, j))

                for j in range(NT):
                    bcg, bu, bgt, bbg = nb(), nb(), nb(), nb()
                    for si, bb in enumerate((bcg, bu, bgt, bbg)):
                        emit_proj(slots[si], j, bb, hT, hkeys(j))
                        if j == NT - 1:
                            wrel(slots[si])
                    T.op("act", "activation", dict(out=u_sb[:, :], in_=ps[bu][:, :], func=AF.Copy),
                         writes=[("ps", bu), ("u",)])
                    T.op("act", "activation", dict(out=sgb[:, :], in_=ps[bgt][:, :], func=AF.Silu),
                         writes=[("ps", bgt), ("sg",)])
                    T.op("dve", "tensor_tensor", dict(out=z[:, 1 + j * TW:1 + (j + 1) * TW], in0=ps[bcg][:, :], in1=u_sb[:, :], op=ALU.mult),
                         reads=[("u",)], writes=[("ps", bcg), ("z", zi, j)])
                    T.op("dve", "tensor_tensor", dict(out=bsb[j % 2][:, :], in0=ps[bbg][:, :], in1=sgb[:, :], op=ALU.mult),
                         reads=[("sg",)], writes=[("ps", bbg), ("bs", j % 2)])
                    if j >= 1:
                        conv(j - 1)
                conv(NT - 1)

        cur_scratch = None

        def set_scratch(kind):
            nonlocal cur_scratch
            if cur_scratch == kind:
                return
            old = A_KEYS if cur_scratch == "A" else (B_KEYS if cur_scratch == "B" else [])
            new = A_KEYS if kind == "A" else B_KEYS
            retire(old, new)
            cur_scratch = kind
            ntmp["sq"], ntmp["rs"] = (sq, rstd) if kind == "A" else (sq_b, rstd_b)
            if kind == "B":
                for i in range(2):
                    T.op("pool", "memset", dict(ap=zb[i][:, 0:1], constant=0.0), writes=[("zpad", i)])
                    T.op("pool", "memset", dict(ap=zb[i][:, ZW - 1:ZW], constant=0.0), writes=[("zpad", i)])

        def emit_all():
            nonlocal cur_scratch
            cur_scratch = None
            state.clear()
            state.update({"bank": 0, "wuse": 0, "wissued": 0, "low": 0, "live": {}, "done": set()})
            for s_ in range(nseq):
                for j in range(NT):
                    for k in range(KC):
                        T.dma("sp", dict(out=xT[:, k, ts(j)], in_=xin[s_, k * 128:(k + 1) * 128, ts(j)]),
                              writes=[("x", k, j)])
                for li, l in enumerate(layers):
                    if li == 0:
                        if cur_scratch is None:
                            set_scratch("A" if l % 2 == 0 else "B")
                        for j in range(NT):
                            emit_norm(l, j)
                    set_scratch("A" if l % 2 == 0 else "B")
                    if l % 2 == 0:
                        emit_attn_layer(l)
                    else:
                        emit_conv_layer(l)
                    nxt = layers[li + 1] if li + 1 < len(layers) else None
                    for step in range(NT + 1):
                        if step < NT:
                            emit_outproj(l, step)
                        if step >= 1:
                            j = step - 1
                            if nxt is not None:
                                emit_norm(nxt, j)
                            else:
                                for k in range(KC):
                                    T.dma("sp", dict(out=yout[s_, k * 128:(k + 1) * 128, ts(j)], in_=xT[:, k, ts(j)]),
                                          reads=[("x", k, j)])

        T.dry = True
        emit_all()
        T.dry = False
        emit_all()
        if debug:
            T.dma("sp", dict(out=dbg_h[:, :, :], in_=hT[:, :, :]), reads=[("h", k, j) for k in range(KC) for j in range(NT)])
            T.dma("sp", dict(out=dbg_og[:, :, :], in_=ogT[:, :, :]), reads=[("og", k, n) for k in range(KC) for n in range(NBLK)])
        T.finish("sp")
        T.finish("pool")
        T.finish("act")
        T.finish("dve")
        T.finish("pe")
    return nc


def _chunk(wcols):
    return np.ascontiguousarray(wcols.reshape(KC, 128, 128).transpose(1, 0, 2))


def _prep_weights(a_w_in, a_w_out, b_w_in, b_w_out):
    out = np.empty((NCH, 128, KC, 128), np.float32)
    for l in range(DEPTH):
        base = _layer_chunk_base(l)
        s = l // 2
        if l % 2 == 0:
            w = a_w_in[s]
            for c in range(8):
                out[base + c] = _chunk(w[:, 1536 + c * 128:1536 + (c + 1) * 128])
            for vc in range(2):
                out[base + 8 + vc] = _chunk(w[:, 1280 + vc * 128:1280 + (vc + 1) * 128])
            for c in range(8):
                out[base + 10 + c] = _chunk(w[:, c * 128:(c + 1) * 128])
            for g in range(4):
                kg = w[:, 1024 + g * 64:1024 + (g + 1) * 64]
                out[base + 18 + g] = _chunk(np.concatenate([kg, kg], axis=1))
            for co in range(8):
                out[base + 22 + co] = _chunk(a_w_out[s][:, co * 128:(co + 1) * 128])
        else:
            w = b_w_in[s]
            for c in range(32):
                out[base + c] = _chunk(w[:, c * 128:(c + 1) * 128])
            for co in range(8):
                out[base + 32 + co] = _chunk(b_w_out[s][:, co * 128:(co + 1) * 128])
    return out


def _prep_consts(norm_g, a_q_norm, a_k_norm, a_sink, b_conv):
    cvec = np.zeros((128, NCV), np.float32)
    p = np.arange(128)
    for l in range(DEPTH):
        for k in range(KC):
            cvec[:, CV_NORM + l * 8 + k] = norm_g[l, k * 128:(k + 1) * 128]
    for a in range(2):
        cvec[:, CV_QG + a] = a_q_norm[a][p % 64]
        cvec[:, CV_KG + a] = a_k_norm[a][p % 64]
        cvec[:, CV_QGP + a] = a_q_norm[a][(p % 64 + 32) % 64]
        cvec[:, CV_KGP + a] = a_k_norm[a][(p % 64 + 32) % 64]
        for g in range(4):
            for j in range(2):
                cvec[:, CV_SINK + (a * 4 + g) * 2 + j] = a_sink[a][4 * g + 2 * j + (p >= 64)]
    for b in range(2):
        for c in range(KC):
            for t in range(3):
                cvec[:, CV_CONV + (b * 8 + c) * 3 + t] = b_conv[b, t, c * 128:(c + 1) * 128]
    cvec[:, CV_EPS] = EPS

    cbf = np.zeros((128, NCB), np.float32)
    cbf[:, CB_ONES:CB_ONES + 128] = 1.0
    cbf[:, CB_BO:CB_BO + 128] = (p[:, None] // 64 == p[None, :] // 64)
    rtm = np.zeros((128, 128), np.float32)
    for i in range(128):
        if i % 64 < 32:
            rtm[i + 32, i] = -1.0
        else:
            rtm[i - 32, i] = 1.0
    cbf[:, CB_RT:CB_RT + 128] = rtm
    cbf[:, CB_ID:CB_ID + 128] = np.eye(128, dtype=np.float32)
    NEG = -30000.0
    cbf[:, CB_MB:CB_MB + 128] = np.where(p[:, None] <= p[None, :], 0.0, NEG)
    cbf[:, CB_MB + 128:CB_MB + 256] = np.where(p[None, :] <= p[:, None], 0.0, NEG)

    inv_freq = 10000.0 ** (-np.arange(0, 64, 2, dtype=np.float64) / 64)
    ang = np.arange(S, dtype=np.float64)[:, None] * inv_freq[None, :]
    cosT = np.cos(ang).astype(np.float32).T
    sinT = np.sin(ang).astype(np.float32).T
    rope = np.empty((128, 2, S), np.float32)
    rope[:, 0, :] = cosT[p % 32]
    rope[:, 1, :] = sinT[p % 32] * np.where(p % 64 < 32, -1.0, 1.0)[:, None]
    return cvec, cbf, rope


def _run(inputs, layers, seqs=None, ncores=NCORES, debug=False):
    x = np.asarray(inputs["x"], np.float32)
    wts = _prep_weights(np.asarray(inputs["a_w_in"], np.float32), np.asarray(inputs["a_w_out"], np.float32),
                        np.asarray(inputs["b_w_in"], np.float32), np.asarray(inputs["b_w_out"], np.float32))
    cvec, cbf, rope = _prep_consts(np.asarray(inputs["norm_g"], np.float32), np.asarray(inputs["a_q_norm"], np.float32),
                                   np.asarray(inputs["a_k_norm"], np.float32), np.asarray(inputs["a_sink"], np.float32),
                                   np.asarray(inputs["b_conv"], np.float32))
    nseq = SEQ_PER_CORE if seqs is None else seqs
    xT = np.ascontiguousarray(x.transpose(0, 2, 1))
    import time as _t
    _t0 = _t.time()
    nc = build_program(list(layers), nseq=nseq, debug=debug)
    print("[kernel] build %.1fs" % (_t.time() - _t0), flush=True)
    in_maps = []
    for c in range(ncores):
        in_maps.append({"xin": np.ascontiguousarray(xT[c * nseq:(c + 1) * nseq]), "wts": wts, "cvec": cvec, "cbf": cbf, "rope": rope})
    res = run_bass_kernel_spmd(nc, in_maps, core_ids=list(range(ncores)))
    if debug:
        return res
    outT = np.concatenate([r["yout"] for r in res.results], axis=0)
    return np.ascontiguousarray(outT.transpose(0, 2, 1)).astype(np.float32)


def kernel(x, norm_g, a_w_in, a_q_norm, a_k_norm, a_sink, a_w_out, b_w_in, b_conv, b_w_out):
    inputs = dict(x=x, norm_g=norm_g, a_w_in=a_w_in, a_q_norm=a_q_norm, a_k_norm=a_k_norm, a_sink=a_sink,
                  a_w_out=a_w_out, b_w_in=b_w_in, b_conv=b_conv, b_w_out=b_w_out)
    return _run(inputs, layers=list(range(DEPTH)))
```

```python
import contextlib
import numpy as np
import concourse.bass as bass
import concourse.mybir as mybir
from concourse.bass_utils import run_bass_kernel_spmd

F32 = mybir.dt.float32
BF16 = mybir.dt.bfloat16
ALU = mybir.AluOpType
AF = mybir.ActivationFunctionType

D = 1024
S = 2048
BATCH = 16
NCORES = 8
SEQ_PER_CORE = BATCH // NCORES
DEPTH = 4
NT = 4
TW = 512
KC = 8
NBLK = 16
EPS = 1e-6
NS = 5
NP = 4
NDS = 24
SQ_ENG, ADD_ENG, FIN_ENG = "dve", "dve", "dve"

CV_NORM = 0
CV_QG = 32
CV_KG = 34
CV_CONV = 36
CV_SINK = 84
CV_EPS = 100
CV_QGP = 101
CV_KGP = 103
NCV = 105
CB_ONES = 0
CB_BO = 128
CB_RT = 256
CB_ID = 384
CB_MB = 512
NCB = 768

A_CH = 30
B_CH = 40


def _layer_chunk_base(l):
    base = 0
    for i in range(l):
        base += A_CH if i % 2 == 0 else B_CH
    return base


NCH = _layer_chunk_base(DEPTH)


class Tok:
    __slots__ = ("sem", "val", "eng", "key")

    def __init__(self, sem, val, eng, key):
        self.sem, self.val, self.eng, self.key = sem, val, eng, key


class Trk:
    def __init__(self, nc, es):
        self.nc = nc
        self.eng = {"pe": nc.tensor, "act": nc.scalar, "dve": nc.vector,
                    "pool": nc.gpsimd, "sp": nc.sync}
        self.sem = {e: es.enter_context(nc.semaphore("s_" + e))
                    for e in ("pe", "act", "dve", "pool")}
        self.cnt = {e: 0 for e in self.sem}
        self.dsem = [es.enter_context(nc.semaphore("s_dma%d" % i)) for i in range(NDS)]
        self.dcnt = [0] * NDS
        self.dnext = {"pool": 0, "sp": 0}
        self.waited = {e: {} for e in self.eng}
        self.lw = {}
        self.rd = {}
        self.nwaits = 0
        self.phase = ""
        self.annotate = False
        self.dry = False

    def _wait(self, e, tok):
        if tok.eng == "pe" and e == "pe":
            return
        w = self.waited[e]
        if w.get(tok.key, 0) >= tok.val:
            return
        w[tok.key] = tok.val
        self.eng[e].wait_ge(tok.sem, tok.val)
        self.nwaits += 1

    def _deps(self, e, reads, writes):
        for r in reads:
            t = self.lw.get(r)
            if t is not None:
                self._wait(e, t)
        for w in writes:
            t = self.lw.get(w)
            if t is not None:
                self._wait(e, t)
            for t in self.rd.get(w, {}).values():
                self._wait(e, t)

    def _commit(self, tok, reads, writes):
        for r in reads:
            d = self.rd.setdefault(r, {})
            o = d.get(tok.key)
            if o is None or o.val < tok.val:
                d[tok.key] = tok
        for w in writes:
            self.lw[w] = tok
            self.rd[w] = {}

    def grp(self, e, instrs, reads=(), writes=()):
        if self.dry:
            return None
        self._deps(e, reads, writes)
        eng = self.eng[e]
        ins = None
        for name, kw in instrs:
            ins = getattr(eng, name)(**kw)
            if self.annotate:
                ins.annotate(self.phase)
        self.cnt[e] += 1
        ins.then_inc(self.sem[e], 1)
        tok = Tok(self.sem[e], self.cnt[e], e, e)
        self._commit(tok, reads, writes)
        return tok

    def op(self, e, name, kw, reads=(), writes=()):
        return self.grp(e, [(name, kw)], reads, writes)

    def dma(self, e, kw, reads=(), writes=()):
        if self.dry:
            return None
        half = NDS // 2
        j = self.dnext[e]
        self.dnext[e] = (j + 1) % half
        i = j + (half if e == "pool" else 0)
        key = ("d", i)
        if self.dcnt[i] > 0:
            self._wait(e, Tok(self.dsem[i], 16 * self.dcnt[i], "dma", key))
        self._deps(e, reads, writes)
        ins = self.eng[e].dma_start(**kw)
        if self.annotate:
            ins.annotate(self.phase + "/dma")
        self.dcnt[i] += 1
        ins.then_inc(self.dsem[i], 16)
        tok = Tok(self.dsem[i], 16 * self.dcnt[i], "dma", key)
        self._commit(tok, reads, writes)
        return tok

    def finish(self, e="sp"):
        for x in self.sem:
            if self.cnt[x] > 0:
                self._wait(e, Tok(self.sem[x], self.cnt[x], x, x))
        for i in range(NDS):
            if self.dcnt[i] > 0:
                self._wait(e, Tok(self.dsem[i], 16 * self.dcnt[i], "dma", ("d", i)))


def build_program(layers, nseq=SEQ_PER_CORE, debug=False, annotate=False):
    nc = bass.Bass("TRN2", target_bir_lowering=False)
    xin = nc.dram_tensor("xin", [nseq, D, S], F32, kind="ExternalInput").ap()
    wts = nc.dram_tensor("wts", [NCH, 128, KC, 128], F32, kind="ExternalInput").ap()
    cvec = nc.dram_tensor("cvec", [128, NCV], F32, kind="ExternalInput").ap()
    cbf = nc.dram_tensor("cbf", [128, NCB], F32, kind="ExternalInput").ap()
    rope = nc.dram_tensor("rope", [128, 2, S], F32, kind="ExternalInput").ap()
    yout = nc.dram_tensor("yout", [nseq, D, S], F32, kind="ExternalOutput").ap()
    if debug:
        dbg_h = nc.dram_tensor("dbg_h", [128, KC, S], BF16, kind="ExternalOutput").ap()
        dbg_og = nc.dram_tensor("dbg_og", [128, KC, S], BF16, kind="ExternalOutput").ap()

    es = contextlib.ExitStack()
    with es:
        def sb(name, shape, dt):
            return es.enter_context(nc.sbuf_tensor(name, shape, dt))

        xT = sb("xT", [128, KC, S], F32)
        hT = sb("hT", [128, KC, S], BF16)
        ogT = sb("ogT", [128, KC, S], BF16)
        ring = [sb("ring%d" % i, [128, KC, 128], BF16) for i in range(NS)]
        cs = sb("cs", [128, 2, S], BF16)
        cv = sb("cv", [128, NCV], F32)
        esink = sb("esink", [128, 16], F32)
        cb = sb("cb", [128, NCB], BF16)
        SCR_BYTES = 58 * 1024
        scr = sb("scr", [128, SCR_BYTES // 2], BF16)
        psall = es.enter_context(nc.psum_tensor("psall", [128, 8 * TW], F32))
        ps = [psall[:, i * TW:(i + 1) * TW] for i in range(8)]

        T = Trk(nc, es)
        T.annotate = annotate

        class Carve:
            def __init__(self):
                self.off = 0

            def take(self, nbytes, dt):
                nbytes = (nbytes + 31) // 32 * 32
                o = self.off
                self.off += nbytes
                assert self.off <= SCR_BYTES, (self.off, SCR_BYTES)
                v = scr[:, o // 2:(o + nbytes) // 2]
                if dt == F32:
                    v = v.bitcast(F32)
                return v

        ca = Carve()
        qr = [ca.take(2 * S * 2, BF16).rearrange("p (c t) -> p c t", c=2) for _ in range(2)]
        kr = [ca.take(S * 2, BF16) for _ in range(2)]
        Vt = ca.take(NBLK * 256 * 2, BF16)
        Pt = [ca.take(4 * 384 * 2, BF16).rearrange("p (h q) -> p h q", h=4) for _ in range(NP)]
        sq = [ca.take(TW * 2, BF16) for _ in range(2)]
        qg = [ca.take(TW * 2, BF16) for _ in range(2)]
        qrot = [ca.take(TW * 2, BF16) for _ in range(2)]
        rstd = ca.take(TW * 4, F32)
        rstd2 = [rstd, rstd]
        t1 = ca.take(TW * 4, F32)
        t2 = ca.take(TW * 4, F32)
        d1 = ca.take(256 * 4, F32)
        rdn = ca.take(256 * 4, F32)
        A_KEYS = ([("q", bq, c, j) for bq in range(2) for c in range(2) for j in range(NT)]
                  + [("k", bq, j) for bq in range(2) for j in range(NT)]
                  + [("v", i) for i in range(8)] + [("p", i) for i in range(NP)]
                  + [("sq", i) for i in range(2)] + [("qg", i) for i in range(2)]
                  + [("rstd",), ("rstd", 1), ("t1",), ("t2",), ("d1",), ("rdn",)] + [("qrot", i, qq) for i in range(2) for qq in range(4)])
        cbv = Carve()
        ZW = S + 2
        zb = [cbv.take(ZW * 4, F32) for _ in range(2)]
        u_sb = cbv.take(TW * 4, F32)
        sgb = cbv.take(TW * 4, F32)
        bsb = [cbv.take(TW * 4, F32) for _ in range(2)]
        a0 = cbv.take(TW * 4, F32)
        a1 = cbv.take(TW * 4, F32)
        sq_b = [cbv.take(TW * 2, BF16) for _ in range(2)]
        rstd_b = cbv.take(TW * 4, F32)
        B_KEYS = ([("z", i, j) for i in range(2) for j in range(NT)] + [("zpad", i) for i in range(2)]
                  + [("u",), ("sg",), ("bs", 0), ("bs", 1), ("a0",), ("a1",), ("sq", 0), ("sq", 1), ("rstd",)])
        ntmp = {"sq": sq, "rs": rstd}

        def retire(keys_old, keys_new):
            toks = {}
            for k in keys_old:
                t = T.lw.pop(k, None)
                cands = list(T.rd.pop(k, {}).values())
                if t is not None:
                    cands.append(t)
                for t in cands:
                    o = toks.get(t.key)
                    if o is None or o.val < t.val:
                        toks[t.key] = t
            for k in keys_new:
                T.lw.pop(k, None)
                T.rd[k] = dict(toks)

        state = {"bank": 0, "wuse": 0, "wissued": 0}

        def nb(stream=None):
            if stream == "qk":
                b = state.get("bank_qk", 0)
                state["bank_qk"] = (b + 1) % 4
                return b
            if stream == "hi":
                b = state.get("bank_hi", 0)
                state["bank_hi"] = (b + 1) % 5
                return 3 + b
            b = state["bank"]
            state["bank"] = (b + 1) % 8
            return b

        wplan = []

        def wprefetch(upto):
            upto = min(upto, len(wplan) - 1)
            while state["wissued"] <= upto:
                i = state["wissued"]
                slot = i % NS
                T.dma("pool", dict(out=ring[slot][:, :, :], in_=wts[wplan[i]]),
                      reads=[], writes=[("w", slot)])
                state["wissued"] += 1

        def wuse(cid):
            if T.dry:
                wplan.append(cid)
                return 0
            i = state["wuse"]
            state["wuse"] += 1
            assert wplan[i] == cid, (i, wplan[i], cid)
            assert i - NS < state["low"], ("weight ring over-subscribed", i, state["low"])
            wprefetch(i)
            state["live"][i % NS] = i
            return i % NS

        def wrel(slot):
            if T.dry:
                return
            state["done"].add(state["live"].pop(slot))
            while state["low"] in state["done"]:
                state["done"].remove(state["low"])
                state["low"] += 1
            wprefetch(state["low"] + NS - 1)

        def ts(j):
            return slice(j * TW, (j + 1) * TW)

        T.dma("sp", dict(out=cv[:, :], in_=cvec[:, :]), writes=[("cv",)])
        T.dma("pool", dict(out=cb[:, :], in_=cbf[:, :]), writes=[("cb",)])
        T.dma("pool", dict(out=cs[:, :, :], in_=rope[:, :, :]), writes=[("cs",)])
        T.op("act", "activation", dict(out=esink[:, :], in_=cv[:, CV_SINK:CV_SINK + 16], func=AF.Exp),
             reads=[("cv",)], writes=[("esink",)])
        ones = cb[:, CB_ONES:CB_ONES + 128]
        bo = cb[:, CB_BO:CB_BO + 128]
        rt = cb[:, CB_RT:CB_RT + 128]
        ident = cb[:, CB_ID:CB_ID + 128]
        mb2 = cb[:, CB_MB:CB_MB + 256].rearrange("p (a q) -> p a q", a=2)
        epsc = cv[:, CV_EPS:CV_EPS + 1]

        def ogkeys(c, j):
            return [("og", c, n) for n in range(4 * j, 4 * j + 4)]

        def emit_norm(l, j):
            T.phase = "norm"
            b = nb()
            nsq, nrs = ntmp["sq"], ntmp["rs"]
            for k in range(KC):
                T.op("act", "activation", dict(out=nsq[k % 2][:, :], in_=xT[:, k, ts(j)], func=AF.Square),
                     reads=[("x", k, j)], writes=[("sq", k % 2)])
                T.op("pe", "matmul", dict(out=ps[b][:, :], lhsT=ones, rhs=nsq[k % 2][:, :],
                                          start=(k == 0), stop=(k == KC - 1)),
                     reads=[("sq", k % 2), ("cb",)], writes=[("ps", b)])
            T.op("act", "activation", dict(out=nrs[:, :], in_=ps[b][:, :], func=AF.Ln, scale=1.0 / D, bias=epsc),
                 reads=[("cv",)], writes=[("ps", b), ("rstd",)])
            T.op("act", "activation", dict(out=nrs[:, :], in_=nrs[:, :], func=AF.Exp, scale=-0.5),
                 writes=[("rstd",)])
            for k in range(KC):
                T.op("dve", "scalar_tensor_tensor",
                     dict(out=hT[:, k, ts(j)], in0=xT[:, k, ts(j)], scalar=cv[:, CV_NORM + l * 8 + k:CV_NORM + l * 8 + k + 1],
                          in1=nrs[:, :], op0=ALU.mult, op1=ALU.mult),
                     reads=[("x", k, j), ("rstd",), ("cv",)], writes=[("h", k, j)])

        def emit_proj(slot, j, b, src, srckeys):
            T.grp("pe", [("matmul", dict(out=ps[b][:, :], lhsT=ring[slot][:, k, :], rhs=src[:, k, ts(j)],
                                         start=(k == 0), stop=(k == KC - 1))) for k in range(KC)],
                  reads=[("w", slot)] + srckeys, writes=[("ps", b)])

        def hkeys(j):
            return [("h", k, j) for k in range(KC)]

        def emit_outproj(l, j):
            T.phase = "outproj"
            srck = [key for k in range(KC) for key in ogkeys(k, j)]
            for co in range(KC):
                slot = wuse(_layer_chunk_base(l) + (22 if l % 2 == 0 else 32) + co)
                b = nb()
                emit_proj(slot, j, b, ogT, srck)
                wrel(slot)
                T.op("dve", "tensor_tensor", dict(out=xT[:, co, ts(j)], in0=ps[b][:, :], in1=xT[:, co, ts(j)], op=ALU.add),
                     reads=[], writes=[("ps", b), ("x", co, j)])

        def emit_qk_stageA(item):
            T.phase = "qkA"
            slot, j, i = item["slot"], item["j"], item["i"]
            if slot is None:
                slot = item["slotref"][0] = wuse(item["cid"]) if item["slotref"][0] is None else item["slotref"][0]
            b = i % 2
            item["b"] = b
            emit_proj(slot, j, b, hT, hkeys(j))
            if j == NT - 1:
                wrel(slot)
            T.op("dve", "tensor_copy", dict(out=qg[i % 2][:, :], in_=ps[b][:, :]),
                 writes=[("ps", b), ("qg", i % 2)])
            T.op(SQ_ENG, "tensor_tensor", dict(out=sq[i % 2][:, :], in0=qg[i % 2][:, :], in1=qg[i % 2][:, :], op=ALU.mult),
                 reads=[("qg", i % 2)], writes=[("sq", i % 2)])
            for qq, (dst0, src0) in enumerate(((0, 32), (32, 0), (64, 96), (96, 64))):
                T.dma("sp", dict(out=qrot[i % 2][dst0:dst0 + 32, :], in_=qg[i % 2][src0:src0 + 32, :]),
                      reads=[("qg", i % 2)], writes=[("qrot", i % 2, qq)])

        def emit_qk_stageB(item):
            T.phase = "qkB"
            j, i, b = item["j"], item["i"], item["b"]
            gcol = item["gcol"]
            b2 = 2
            T.op("pe", "matmul", dict(out=ps[b2][:, :], lhsT=bo, rhs=sq[i % 2][:, :], start=True, stop=True),
                 reads=[("sq", i % 2), ("cb",)], writes=[("ps", b2)])
            rs = rstd2[i % 2]
            rkey = ("rstd",)
            T.op("act", "activation", dict(out=rs[:, :], in_=ps[b2][:, :], func=AF.Ln, scale=1.0 / 64, bias=epsc),
                 reads=[("cv",)], writes=[("ps", b2), rkey])
            T.op("act", "activation", dict(out=rs[:, :], in_=rs[:, :], func=AF.Exp, scale=-0.5),
                 writes=[rkey])
            T.op("dve", "scalar_tensor_tensor",
                 dict(out=t1[:, :], in0=ps[b][:, :], scalar=gcol, in1=cs[:, 0, ts(j)], op0=ALU.mult, op1=ALU.mult),
                 reads=[("cv",), ("cs",)], writes=[("ps", b), ("t1",)])
            T.op("dve", "scalar_tensor_tensor",
                 dict(out=t2[:, :], in0=qrot[i % 2][:, :], scalar=item["gpcol"], in1=cs[:, 1, ts(j)], op0=ALU.mult, op1=ALU.mult),
                 reads=[("cv",), ("cs",)] + [("qrot", i % 2, qq) for qq in range(4)], writes=[("t2",)])
            T.op(ADD_ENG, "tensor_tensor", dict(out=t1[:, :], in0=t1[:, :], in1=t2[:, :], op=ALU.add),
                 reads=[("t2",)], writes=[("t1",)])
            T.op(FIN_ENG, "tensor_tensor", dict(out=item["dst"], in0=t1[:, :], in1=rs[:, :], op=ALU.mult),
                 reads=[("t1",), rkey], writes=[item["dkey"]])

        def emit_S(a, g, m):
            T.phase = "S"
            bq = g % 2
            slot = m % NP
            lo, hi = max(m - 1, 0), min(m + 1, NBLK - 1)
            qs, qe = lo * 128, (hi + 1) * 128
            off = (lo - (m - 1)) * 128
            w = qe - qs
            qkeys_j = sorted(set([qs // TW, (qe - 1) // TW]))
            banks = [4, 5, 6, 7]
            for hh in range(4):
                cc, par = hh // 2, hh % 2
                rows = slice(par * 64, (par + 1) * 64)
                b = banks[hh]
                T.op("pe", "matmul", dict(out=ps[b][:, off:off + w], lhsT=kr[bq][rows, m * 128:(m + 1) * 128],
                                          rhs=qr[bq][rows, cc, qs:qe], start=True, stop=False),
                     reads=[("k", bq, m // 4)] + [("q", bq, cc, jj) for jj in qkeys_j], writes=[("ps", b)])
            for hh in range(4):
                b = banks[hh]
                pieces = []
                if m >= 1:
                    pieces.append((ps[b][:, 0:128], mb2[:, 0, :]))
                if m <= NBLK - 2:
                    pieces.append((ps[b][:, 256:384], mb2[:, 1, :]))
                T.grp("pe", [("matmul", dict(out=o_, lhsT=ident, rhs=r_, start=False, stop=(ii == len(pieces) - 1)))
                             for ii, (o_, r_) in enumerate(pieces)],
                      reads=[("cb",)], writes=[("ps", b)])
            s4 = psall[:, 4 * TW:8 * TW].rearrange("p (h c) -> p h c", h=4)
            T.op("act", "activation", dict(out=Pt[slot][:, :, off:off + w], in_=s4[:, :, off:off + w],
                                           func=AF.Exp, scale=0.125),
                 writes=[("ps", 4), ("ps", 5), ("ps", 6), ("ps", 7), ("p", slot)])

        def emit_PV(a, g, n):
            T.phase = "PV"
            b = 3
            mms = [mm for mm in (n - 1, n, n + 1) if 0 <= mm < NBLK]
            instrs = []
            for kind in range(2):
                for idx, mm in enumerate(mms):
                    cbk = n - mm + 1
                    for par in range(2):
                        rows = slice(par * 64, (par + 1) * 64)
                        outv = ps[b][rows, kind * 256:(kind + 1) * 256].rearrange("p (a q) -> p a q", a=2)
                        rhs = Pt[mm % NP][:, par::2, cbk * 128:(cbk + 1) * 128]
                        lhsT = Vt[:, mm * 256 + g * 64: mm * 256 + g * 64 + 64] if kind == 0 else cb[:, CB_ONES:CB_ONES + 64]
                        instrs.append(("matmul", dict(out=outv, lhsT=lhsT, rhs=rhs, start=(idx == 0),
                                                      stop=(idx == len(mms) - 1), tile_position=(0, par * 64))))
            T.grp("pe", instrs, reads=[("p", mm % NP) for mm in mms] + [("v", mm // 2) for mm in mms] + [("cb",)],
                  writes=[("ps", b)])
            sc = (a * 4 + g) * 2
            for jj in range(2):
                T.op("act", "activation",
                     dict(out=rdn[:, jj * 128:(jj + 1) * 128], in_=ps[b][:, 256 + jj * 128:256 + (jj + 1) * 128],
                          func=AF.Ln, bias=esink[:, sc + jj:sc + jj + 1]),
                     reads=[("esink",)], writes=[("ps", b), ("rdn",)])
            T.op("act", "activation", dict(out=rdn[:, :], in_=rdn[:, :], func=AF.Exp, scale=-1.0), writes=[("rdn",)])
            T.op("dve", "tensor_tensor", dict(out=d1[:, :], in0=ps[b][:, 0:256], in1=rdn[:, :], op=ALU.mult),
                 reads=[("rdn",)], writes=[("ps", b), ("d1",)])
            ogv = ogT[:, 2 * g:2 * g + 2, n * 128:(n + 1) * 128]
            T.op("pool", "tensor_tensor", dict(out=ogv, in0=d1[:, :].rearrange("p (a q) -> p a q", a=2), in1=ogv, op=ALU.mult),
                 reads=[("d1",)], writes=[("og", 2 * g, n), ("og", 2 * g + 1, n)])

        def interleave(la, lb):
            ia = ib = 0
            while ia < len(la) or ib < len(lb):
                fa = ia / len(la) if la else 2.0
                fb = ib / len(lb) if lb else 2.0
                if ib >= len(lb) or (ia < len(la) and fa <= fb):
                    la[ia]()
                    ia += 1
                else:
                    lb[ib]()
                    ib += 1

        def emit_attn_layer(l):
            a = l // 2
            wbase = _layer_chunk_base(l)

            def proj_work(g):
                th = []
                bq = g % 2

                def gate_tile(c, j, ref):
                    def f():
                        T.phase = "gate"
                        if ref[0] is None:
                            ref[0] = wuse(wbase + c)
                        b = nb("hi")
                        emit_proj(ref[0], j, b, hT, hkeys(j))
                        if j == NT - 1:
                            wrel(ref[0])
                        T.op("act", "activation", dict(out=ogT[:, c, ts(j)], in_=ps[b][:, :], func=AF.Silu),
                             writes=[("ps", b)] + ogkeys(c, j))
                    return f
                gate_th = []
                for c in (range(KC) if g == 0 else ()):
                    ref = [None]
                    for j in range(NT):
                        gate_th.append(gate_tile(c, j, ref))
                if g == 0:
                    vref = [None, None]

                    def v_tile(tb2):
                        def f():
                            T.phase = "V"
                            if vref[0] is None:
                                vref[0] = wuse(wbase + 8)
                                vref[1] = wuse(wbase + 9)
                            b = nb()
                            instrs = []
                            for t in range(2):
                                tb = 2 * tb2 + t
                                for vc in range(2):
                                    for k in range(KC):
                                        instrs.append(("matmul", dict(out=ps[b][:, t * 256 + vc * 128:t * 256 + (vc + 1) * 128],
                                                                      lhsT=hT[:, k, tb * 128:(tb + 1) * 128], rhs=ring[vref[vc]][:, k, :],
                                                                      start=(k == 0), stop=(k == KC - 1))))
                            T.grp("pe", instrs, reads=[("w", vref[0]), ("w", vref[1])] + hkeys(tb2 // 2), writes=[("ps", b)])
                            if tb2 == 7:
                                wrel(vref[0])
                                wrel(vref[1])
                            T.op("act", "activation", dict(out=Vt[:, tb2 * 512:(tb2 + 1) * 512], in_=ps[b][:, :], func=AF.Copy),
                                 writes=[("ps", b), ("v", tb2)])
                        return f
                    for tb2 in range(8):
                        th.append(v_tile(tb2))
                items = []
                for kind, cc in (("q", 0), ("q", 1), ("k", 0)):
                    ref = [None]
                    for j in range(NT):
                        if kind == "q":
                            dst, dkey = qr[bq][:, cc, ts(j)], ("q", bq, cc, j)
                            gcol = cv[:, CV_QG + a:CV_QG + a + 1]
                            gpcol = cv[:, CV_QGP + a:CV_QGP + a + 1]
                        else:
                            dst, dkey = kr[bq][:, ts(j)], ("k", bq, j)
                            gcol = cv[:, CV_KG + a:CV_KG + a + 1]
                            gpcol = cv[:, CV_KGP + a:CV_KGP + a + 1]
                        items.append(dict(kind=kind, slot=None, slotref=ref, j=j, i=len(items), dst=dst, dkey=dkey, gcol=gcol, gpcol=gpcol,
                                          cid=wbase + (10 + 2 * g + cc if kind == "q" else 18 + g)))

                def qk_step(i):
                    def f():
                        if i < len(items):
                            emit_qk_stageA(items[i])
                        if i >= 1:
                            emit_qk_stageB(items[i - 1])
                    return f
                qk_th = [qk_step(i) for i in range(len(items) + 1)]
                if g == 0:
                    qi = 0
                    for c in range(KC):
                        th.extend(gate_th[4 * c:4 * c + 4])
                        nq = 2 if c < 6 else (1 if c == 6 else 0)
                        th.extend(qk_th[qi:qi + nq])
                        qi += nq
                    assert qi == len(qk_th)
                else:
                    th.extend(qk_th)
                return th

            def attn_work(g):
                th = []

                def step_f(step):
                    def f():
                        if step < NBLK:
                            emit_S(a, g, step)
                        if step >= 2:
                            emit_PV(a, g, step - 2)
                    return f
                for step in range(NBLK + 2):
                    th.append(step_f(step))
                return th

            interleave(proj_work(0), [])
            for g in range(4):
                interleave(attn_work(g), proj_work(g + 1) if g + 1 < 4 else [])

        def emit_conv_layer(l):
            bi = l // 2
            T.phase = "conv"
            for c in range(KC):
                zi = c % 2
                z = zb[zi]
                cb0 = _layer_chunk_base(l)
                slots = [wuse(cb0 + 8 + c), wuse(cb0 + 16 + c), wuse(cb0 + 24 + c), wuse(cb0 + c)]
                wc = CV_CONV + (bi * 8 + c) * 3

                def conv(j):
                    rk = [("z", zi, jj) for jj in (j - 1, j, j + 1) if 0 <= jj < NT] + [("zpad", zi)]
                    T.op("act", "activation", dict(out=a0[:, :], in_=z[:, j * TW:j * TW + TW], func=AF.Copy, scale=cv[:, wc:wc + 1]),
                         reads=rk + [("cv",)], writes=[("a0",)])
                    T.op("dve", "scalar_tensor_tensor",
                         dict(out=a1[:, :], in0=z[:, 1 + j * TW:1 + j * TW + TW], scalar=cv[:, wc + 1:wc + 2], in1=a0[:, :],
                              op0=ALU.mult, op1=ALU.add),
                         reads=rk + [("a0",), ("cv",)], writes=[("a1",)])
                    T.op("dve", "scalar_tensor_tensor",
                         dict(out=a0[:, :], in0=z[:, 2 + j * TW:2 + j * TW + TW], scalar=cv[:, wc + 2:wc + 3], in1=a1[:, :],
                              op0=ALU.mult, op1=ALU.add),
                         reads=rk + [("a1",), ("cv",)], writes=[("a0",)])
                    T.op("pool", "tensor_tensor", dict(out=ogT[:, c, ts(j)], in0=a0[:, :], in1=bsb[j % 2][:, :], op=ALU.mult),
                         reads=[("a0",), ("bs", j % 2)], writes=ogkeys(c, j))

                for j in range(NT):
                    bcg, bu, bgt, bbg = nb(), nb(), nb(), nb()
                    for si, bb in enumerate((bcg, bu, bgt, bbg)):
                        emit_proj(slots[si], j, bb, hT, hkeys(j))
                        if j == NT - 1:
                            wrel(slots[si])
                    T.op("act", "activation", dict(out=u_sb[:, :], in_=ps[bu][:, :], func=AF.Copy),
                         writes=[("ps", bu), ("u",)])
                    T.op("act", "activation", dict(out=sgb[:, :], in_=ps[bgt][:, :], func=AF.Silu),
                         writes=[("ps", bgt), ("sg",)])
                    T.op("dve", "tensor_tensor", dict(out=z[:, 1 + j * TW:1 + (j + 1) * TW], in0=ps[bcg][:, :], in1=u_sb[:, :], op=ALU.mult),
                         reads=[("u",)], writes=[("ps", bcg), ("z", zi, j)])
                    T.op("dve", "tensor_tensor", dict(out=bsb[j % 2][:, :], in0=ps[bbg][:, :], in1=sgb[:, :], op=ALU.mult),
                         reads=[("sg",)], writes=[("ps", bbg), ("bs", j % 2)])
                    if j >= 1:
                        conv(j - 1)
                conv(NT - 1)

        cur_scratch = None

        def set_scratch(kind):
            nonlocal cur_scratch
            if cur_scratch == kind:
                return
            old = A_KEYS if cur_scratch == "A" else (B_KEYS if cur_scratch == "B" else [])
            new = A_KEYS if kind == "A" else B_KEYS
            retire(old, new)
            cur_scratch = kind
            ntmp["sq"], ntmp["rs"] = (sq, rstd) if kind == "A" else (sq_b, rstd_b)
            if kind == "B":
                for i in range(2):
                    T.op("pool", "memset", dict(ap=zb[i][:, 0:1], constant=0.0), writes=[("zpad", i)])
                    T.op("pool", "memset", dict(ap=zb[i][:, ZW - 1:ZW], constant=0.0), writes=[("zpad", i)])

        def emit_all():
            nonlocal cur_scratch
            cur_scratch = None
            state.clear()
            state.update({"bank": 0, "wuse": 0, "wissued": 0, "low": 0, "live": {}, "done": set()})
            for s_ in range(nseq):
                for j in range(NT):
                    for k in range(KC):
                        T.dma("sp", dict(out=xT[:, k, ts(j)], in_=xin[s_, k * 128:(k + 1) * 128, ts(j)]),
                              writes=[("x", k, j)])
                for li, l in enumerate(layers):
                    if li == 0:
                        if cur_scratch is None:
                            set_scratch("A" if l % 2 == 0 else "B")
                        for j in range(NT):
                            emit_norm(l, j)
                    set_scratch("A" if l % 2 == 0 else "B")
                    if l % 2 == 0:
                        emit_attn_layer(l)
                    else:
                        emit_conv_layer(l)
                    nxt = layers[li + 1] if li + 1 < len(layers) else None
                    for step in range(NT + 1):
                        if step < NT:
                            emit_outproj(l, step)
                        if step >= 1:
                            j = step - 1
                            if nxt is not None:
                                emit_norm(nxt, j)
                            else:
                                for k in range(KC):
                                    T.dma("sp", dict(out=yout[s_, k * 128:(k + 1) * 128, ts(j)], in_=xT[:, k, ts(j)]),
                                          reads=[("x", k, j)])

        T.dry = True
        emit_all()
        T.dry = False
        emit_all()
        if debug:
            T.dma("sp", dict(out=dbg_h[:, :, :], in_=hT[:, :, :]), reads=[("h", k, j) for k in range(KC) for j in range(NT)])
            T.dma("sp", dict(out=dbg_og[:, :, :], in_=ogT[:, :, :]), reads=[("og", k, n) for k in range(KC) for n in range(NBLK)])
        T.finish("sp")
        T.finish("pool")
        T.finish("act")
        T.finish("dve")
        T.finish("pe")
    return nc


def _chunk(wcols):
    return np.ascontiguousarray(wcols.reshape(KC, 128, 128).transpose(1, 0, 2))


def _prep_weights(a_w_in, a_w_out, b_w_in, b_w_out):
    out = np.empty((NCH, 128, KC, 128), np.float32)
    for l in range(DEPTH):
        base = _layer_chunk_base(l)
        s = l // 2
        if l % 2 == 0:
            w = a_w_in[s]
            for c in range(8):
                out[base + c] = _chunk(w[:, 1536 + c * 128:1536 + (c + 1) * 128])
            for vc in range(2):
                out[base + 8 + vc] = _chunk(w[:, 1280 + vc * 128:1280 + (vc + 1) * 128])
            for c in range(8):
                out[base + 10 + c] = _chunk(w[:, c * 128:(c + 1) * 128])
            for g in range(4):
                kg = w[:, 1024 + g * 64:1024 + (g + 1) * 64]
                out[base + 18 + g] = _chunk(np.concatenate([kg, kg], axis=1))
            for co in range(8):
                out[base + 22 + co] = _chunk(a_w_out[s][:, co * 128:(co + 1) * 128])
        else:
            w = b_w_in[s]
            for c in range(32):
                out[base + c] = _chunk(w[:, c * 128:(c + 1) * 128])
            for co in range(8):
                out[base + 32 + co] = _chunk(b_w_out[s][:, co * 128:(co + 1) * 128])
    return out


def _prep_consts(norm_g, a_q_norm, a_k_norm, a_sink, b_conv):
    cvec = np.zeros((128, NCV), np.float32)
    p = np.arange(128)
    for l in range(DEPTH):
        for k in range(KC):
            cvec[:, CV_NORM + l * 8 + k] = norm_g[l, k * 128:(k + 1) * 128]
    for a in range(2):
        cvec[:, CV_QG + a] = a_q_norm[a][p % 64]
        cvec[:, CV_KG + a] = a_k_norm[a][p % 64]
        cvec[:, CV_QGP + a] = a_q_norm[a][(p % 64 + 32) % 64]
        cvec[:, CV_KGP + a] = a_k_norm[a][(p % 64 + 32) % 64]
        for g in range(4):
            for j in range(2):
                cvec[:, CV_SINK + (a * 4 + g) * 2 + j] = a_sink[a][4 * g + 2 * j + (p >= 64)]
    for b in range(2):
        for c in range(KC):
            for t in range(3):
                cvec[:, CV_CONV + (b * 8 + c) * 3 + t] = b_conv[b, t, c * 128:(c + 1) * 128]
    cvec[:, CV_EPS] = EPS

    cbf = np.zeros((128, NCB), np.float32)
    cbf[:, CB_ONES:CB_ONES + 128] = 1.0
    cbf[:, CB_BO:CB_BO + 128] = (p[:, None] // 64 == p[None, :] // 64)
    rtm = np.zeros((128, 128), np.float32)
    for i in range(128):
        if i % 64 < 32:
            rtm[i + 32, i] = -1.0
        else:
            rtm[i - 32, i] = 1.0
    cbf[:, CB_RT:CB_RT + 128] = rtm
    cbf[:, CB_ID:CB_ID + 128] = np.eye(128, dtype=np.float32)
    NEG = -30000.0
    cbf[:, CB_MB:CB_MB + 128] = np.where(p[:, None] <= p[None, :], 0.0, NEG)
    cbf[:, CB_MB + 128:CB_MB + 256] = np.where(p[None, :] <= p[:, None], 0.0, NEG)

    inv_freq = 10000.0 ** (-np.arange(0, 64, 2, dtype=np.float64) / 64)
    ang = np.arange(S, dtype=np.float64)[:, None] * inv_freq[None, :]
    cosT = np.cos(ang).astype(np.float32).T
    sinT = np.sin(ang).astype(np.float32).T
    rope = np.empty((128, 2, S), np.float32)
    rope[:, 0, :] = cosT[p % 32]
    rope[:, 1, :] = sinT[p % 32] * np.where(p % 64 < 32, -1.0, 1.0)[:, None]
    return cvec, cbf, rope


def _run(inputs, layers, seqs=None, ncores=NCORES, debug=False):
    x = np.asarray(inputs["x"], np.float32)
    wts = _prep_weights(np.asarray(inputs["a_w_in"], np.float32), np.asarray(inputs["a_w_out"], np.float32),
                        np.asarray(inputs["b_w_in"], np.float32), np.asarray(inputs["b_w_out"], np.float32))
    cvec, cbf, rope = _prep_consts(np.asarray(inputs["norm_g"], np.float32), np.asarray(inputs["a_q_norm"], np.float32),
                                   np.asarray(inputs["a_k_norm"], np.float32), np.asarray(inputs["a_sink"], np.float32),
                                   np.asarray(inputs["b_conv"], np.float32))
    nseq = SEQ_PER_CORE if seqs is None else seqs
    xT = np.ascontiguousarray(x.transpose(0, 2, 1))
    import time as _t
    _t0 = _t.time()
    nc = build_program(list(layers), nseq=nseq, debug=debug)
    print("[kernel] build %.1fs" % (_t.time() - _t0), flush=True)
    in_maps = []
    for c in range(ncores):
        in_maps.append({"xin": np.ascontiguousarray(xT[c * nseq:(c + 1) * nseq]), "wts": wts, "cvec": cvec, "cbf": cbf, "rope": rope})
    res = run_bass_kernel_spmd(nc, in_maps, core_ids=list(range(ncores)))
    if debug:
        return res
    outT = np.concatenate([r["yout"] for r in res.results], axis=0)
    return np.ascontiguousarray(outT.transpose(0, 2, 1)).astype(np.float32)


def kernel(x, norm_g, a_w_in, a_q_norm, a_k_norm, a_sink, a_w_out, b_w_in, b_conv, b_w_out):
    inputs = dict(x=x, norm_g=norm_g, a_w_in=a_w_in, a_q_norm=a_q_norm, a_k_norm=a_k_norm, a_sink=a_sink,
                  a_w_out=a_w_out, b_w_in=b_w_in, b_conv=b_conv, b_w_out=b_w_out)
    return _run(inputs, layers=list(range(DEPTH)))
```

```python
import contextlib
import numpy as np
import concourse.bass as bass
import concourse.mybir as mybir
from concourse.bass_utils import run_bass_kernel_spmd

F32 = mybir.dt.float32
BF16 = mybir.dt.bfloat16
ALU = mybir.AluOpType
AF = mybir.ActivationFunctionType

D = 1024
S = 2048
BATCH = 16
NCORES = 8
SEQ_PER_CORE = BATCH // NCORES
DEPTH = 4
NT = 4
TW = 512
KC = 8
NBLK = 16
EPS = 1e-6
NS = 5
NP = 4
NDS = 24
SQ_ENG, ADD_ENG, FIN_ENG = "dve", "dve", "dve"

CV_NORM = 0
CV_QG = 32
CV_KG = 34
CV_CONV = 36
CV_SINK = 84
CV_EPS = 100
CV_QGP = 101
CV_KGP = 103
NCV = 105
CB_ONES = 0
CB_BO = 128
CB_RT = 256
CB_ID = 384
CB_MB = 512
NCB = 768

A_CH = 30
B_CH = 40


def _layer_chunk_base(l):
    base = 0
    for i in range(l):
        base += A_CH if i % 2 == 0 else B_CH
    return base


NCH = _layer_chunk_base(DEPTH)


class Tok:
    __slots__ = ("sem", "val", "eng", "key")

    def __init__(self, sem, val, eng, key):
        self.sem, self.val, self.eng, self.key = sem, val, eng, key


class Trk:
    def __init__(self, nc, es):
        self.nc = nc
        self.eng = {"pe": nc.tensor, "act": nc.scalar, "dve": nc.vector,
                    "pool": nc.gpsimd, "sp": nc.sync}
        self.sem = {e: es.enter_context(nc.semaphore("s_" + e))
                    for e in ("pe", "act", "dve", "pool")}
        self.cnt = {e: 0 for e in self.sem}
        self.dsem = [es.enter_context(nc.semaphore("s_dma%d" % i)) for i in range(NDS)]
        self.dcnt = [0] * NDS
        self.dnext = {"pool": 0, "sp": 0}
        self.waited = {e: {} for e in self.eng}
        self.lw = {}
        self.rd = {}
        self.nwaits = 0
        self.phase = ""
        self.annotate = False
        self.dry = False

    def _wait(self, e, tok):
        if tok.eng == "pe" and e == "pe":
            return
        w = self.waited[e]
        if w.get(tok.key, 0) >= tok.val:
            return
        w[tok.key] = tok.val
        self.eng[e].wait_ge(tok.sem, tok.val)
        self.nwaits += 1

    def _deps(self, e, reads, writes):
        for r in reads:
            t = self.lw.get(r)
            if t is not None:
                self._wait(e, t)
        for w in writes:
            t = self.lw.get(w)
            if t is not None:
                self._wait(e, t)
            for t in self.rd.get(w, {}).values():
                self._wait(e, t)

    def _commit(self, tok, reads, writes):
        for r in reads:
            d = self.rd.setdefault(r, {})
            o = d.get(tok.key)
            if o is None or o.val < tok.val:
                d[tok.key] = tok
        for w in writes:
            self.lw[w] = tok
            self.rd[w] = {}

    def grp(self, e, instrs, reads=(), writes=()):
        if self.dry:
            return None
        self._deps(e, reads, writes)
        eng = self.eng[e]
        ins = None
        for name, kw in instrs:
            ins = getattr(eng, name)(**kw)
            if self.annotate:
                ins.annotate(self.phase)
        self.cnt[e] += 1
        ins.then_inc(self.sem[e], 1)
        tok = Tok(self.sem[e], self.cnt[e], e, e)
        self._commit(tok, reads, writes)
        return tok

    def op(self, e, name, kw, reads=(), writes=()):
        return self.grp(e, [(name, kw)], reads, writes)

    def dma(self, e, kw, reads=(), writes=()):
        if self.dry:
            return None
        half = NDS // 2
        j = self.dnext[e]
        self.dnext[e] = (j + 1) % half
        i = j + (half if e == "pool" else 0)
        key = ("d", i)
        if self.dcnt[i] > 0:
            self._wait(e, Tok(self.dsem[i], 16 * self.dcnt[i], "dma", key))
        self._deps(e, reads, writes)
        ins = self.eng[e].dma_start(**kw)
        if self.annotate:
            ins.annotate(self.phase + "/dma")
        self.dcnt[i] += 1
        ins.then_inc(self.dsem[i], 16)
        tok = Tok(self.dsem[i], 16 * self.dcnt[i], "dma", key)
        self._commit(tok, reads, writes)
        return tok

    def finish(self, e="sp"):
        for x in self.sem:
            if self.cnt[x] > 0:
                self._wait(e, Tok(self.sem[x], self.cnt[x], x, x))
        for i in range(NDS):
            if self.dcnt[i] > 0:
                self._wait(e, Tok(self.dsem[i], 16 * self.dcnt[i], "dma", ("d", i)))


def build_program(layers, nseq=SEQ_PER_CORE, debug=False, annotate=False):
    nc = bass.Bass("TRN2", target_bir_lowering=False)
    xin = nc.dram_tensor("xin", [nseq, D, S], F32, kind="ExternalInput").ap()
    wts = nc.dram_tensor("wts", [NCH, 128, KC, 128], F32, kind="ExternalInput").ap()
    cvec = nc.dram_tensor("cvec", [128, NCV], F32, kind="ExternalInput").ap()
    cbf = nc.dram_tensor("cbf", [128, NCB], F32, kind="ExternalInput").ap()
    rope = nc.dram_tensor("rope", [128, 2, S], F32, kind="ExternalInput").ap()
    yout = nc.dram_tensor("yout", [nseq, D, S], F32, kind="ExternalOutput").ap()
    if debug:
        dbg_h = nc.dram_tensor("dbg_h", [128, KC, S], BF16, kind="ExternalOutput").ap()
        dbg_og = nc.dram_tensor("dbg_og", [128, KC, S], BF16, kind="ExternalOutput").ap()

    es = contextlib.ExitStack()
    with es:
        def sb(name, shape, dt):
            return es.enter_context(nc.sbuf_tensor(name, shape, dt))

        xT = sb("xT", [128, KC, S], F32)
        hT = sb("hT", [128, KC, S], BF16)
        ogT = sb("ogT", [128, KC, S], BF16)
        ring = [sb("ring%d" % i, [128, KC, 128], BF16) for i in range(NS)]
        cs = sb("cs", [128, 2, S], BF16)
        cv = sb("cv", [128, NCV], F32)
        esink = sb("esink", [128, 16], F32)
        cb = sb("cb", [128, NCB], BF16)
        SCR_BYTES = 58 * 1024
        scr = sb("scr", [128, SCR_BYTES // 2], BF16)
        psall = es.enter_context(nc.psum_tensor("psall", [128, 8 * TW], F32))
        ps = [psall[:, i * TW:(i + 1) * TW] for i in range(8)]

        T = Trk(nc, es)
        T.annotate = annotate

        class Carve:
            def __init__(self):
                self.off = 0

            def take(self, nbytes, dt):
                nbytes = (nbytes + 31) // 32 * 32
                o = self.off
                self.off += nbytes
                assert self.off <= SCR_BYTES, (self.off, SCR_BYTES)
                v = scr[:, o // 2:(o + nbytes) // 2]
                if dt == F32:
                    v = v.bitcast(F32)
                return v

        ca = Carve()
        qr = [ca.take(2 * S * 2, BF16).rearrange("p (c t) -> p c t", c=2) for _ in range(2)]
        kr = [ca.take(S * 2, BF16) for _ in range(2)]
        Vt = ca.take(NBLK * 256 * 2, BF16)
        Pt = [ca.take(4 * 384 * 2, BF16).rearrange("p (h q) -> p h q", h=4) for _ in range(NP)]
        sq = [ca.take(TW * 2, BF16) for _ in range(2)]
        qg = [ca.take(TW * 2, BF16) for _ in range(2)]
        qrot = [ca.take(TW * 2, BF16) for _ in range(2)]
        rstd = ca.take(TW * 4, F32)
        rstd2 = [rstd, rstd]
        t1 = ca.take(TW * 4, F32)
        t2 = ca.take(TW * 4, F32)
        d1 = ca.take(256 * 4, F32)
        rdn = ca.take(256 * 4, F32)
        A_KEYS = ([("q", bq, c, j) for bq in range(2) for c in range(2) for j in range(NT)]
                  + [("k", bq, j) for bq in range(2) for j in range(NT)]
                  + [("v", i) for i in range(8)] + [("p", i) for i in range(NP)]
                  + [("sq", i) for i in range(2)] + [("qg", i) for i in range(2)]
                  + [("rstd",), ("rstd", 1), ("t1",), ("t2",), ("d1",), ("rdn",)] + [("qrot", i, qq) for i in range(2) for qq in range(4)])
        cbv = Carve()
        ZW = S + 2
        zb = [cbv.take(ZW * 4, F32) for _ in range(2)]
        u_sb = cbv.take(TW * 4, F32)
        sgb = cbv.take(TW * 4, F32)
        bsb = [cbv.take(TW * 4, F32) for _ in range(2)]
        a0 = cbv.take(TW * 4, F32)
        a1 = cbv.take(TW * 4, F32)
        sq_b = [cbv.take(TW * 2, BF16) for _ in range(2)]
        rstd_b = cbv.take(TW * 4, F32)
        B_KEYS = ([("z", i, j) for i in range(2) for j in range(NT)] + [("zpad", i) for i in range(2)]
                  + [("u",), ("sg",), ("bs", 0), ("bs", 1), ("a0",), ("a1",), ("sq", 0), ("sq", 1), ("rstd",)])
        ntmp = {"sq": sq, "rs": rstd}

        def retire(keys_old, keys_new):
            toks = {}
            for k in keys_old:
                t = T.lw.pop(k, None)
                cands = list(T.rd.pop(k, {}).values())
                if t is not None:
                    cands.append(t)
                for t in cands:
                    o = toks.get(t.key)
                    if o is None or o.val < t.val:
                        toks[t.key] = t
            for k in keys_new:
                T.lw.pop(k, None)
                T.rd[k] = dict(toks)

        state = {"bank": 0, "wuse": 0, "wissued": 0}

        def nb(stream=None):
            if stream == "qk":
                b = state.get("bank_qk", 0)
                state["bank_qk"] = (b + 1) % 4
                return b
            if stream == "hi":
                b = state.get("bank_hi", 0)
                state["bank_hi"] = (b + 1) % 6
                return 2 + b
            b = state["bank"]
            state["bank"] = (b + 1) % 8
            return b

        wplan = []

        def wprefetch(upto):
            upto = min(upto, len(wplan) - 1)
            while state["wissued"] <= upto:
                i = state["wissued"]
                slot = i % NS
                T.dma("pool", dict(out=ring[slot][:, :, :], in_=wts[wplan[i]]),
                      reads=[], writes=[("w", slot)])
                state["wissued"] += 1

        def wuse(cid):
            if T.dry:
                wplan.append(cid)
                return 0
            i = state["wuse"]
            state["wuse"] += 1
            assert wplan[i] == cid, (i, wplan[i], cid)
            assert i - NS < state["low"], ("weight ring over-subscribed", i, state["low"])
            wprefetch(i)
            state["live"][i % NS] = i
            return i % NS

        def wrel(slot):
            if T.dry:
                return
            state["done"].add(state["live"].pop(slot))
            while state["low"] in state["done"]:
                state["done"].remove(state["low"])
                state["low"] += 1
            wprefetch(state["low"] + NS - 1)

        def ts(j):
            return slice(j * TW, (j + 1) * TW)

        T.dma("sp", dict(out=cv[:, :], in_=cvec[:, :]), writes=[("cv",)])
        T.dma("pool", dict(out=cb[:, :], in_=cbf[:, :]), writes=[("cb",)])
        T.dma("pool", dict(out=cs[:, :, :], in_=rope[:, :, :]), writes=[("cs",)])
        T.op("act", "activation", dict(out=esink[:, :], in_=cv[:, CV_SINK:CV_SINK + 16], func=AF.Exp),
             reads=[("cv",)], writes=[("esink",)])
        ones = cb[:, CB_ONES:CB_ONES + 128]
        bo = cb[:, CB_BO:CB_BO + 128]
        rt = cb[:, CB_RT:CB_RT + 128]
        ident = cb[:, CB_ID:CB_ID + 128]
        mb2 = cb[:, CB_MB:CB_MB + 256].rearrange("p (a q) -> p a q", a=2)
        epsc = cv[:, CV_EPS:CV_EPS + 1]

        def ogkeys(c, j):
            return [("og", c, n) for n in range(4 * j, 4 * j + 4)]

        def emit_norm(l, j):
            T.phase = "norm"
            b = nb()
            nsq, nrs = ntmp["sq"], ntmp["rs"]
            for k in range(KC):
                T.op("act", "activation", dict(out=nsq[k % 2][:, :], in_=xT[:, k, ts(j)], func=AF.Square),
                     reads=[("x", k, j)], writes=[("sq", k % 2)])
                T.op("pe", "matmul", dict(out=ps[b][:, :], lhsT=ones, rhs=nsq[k % 2][:, :],
                                          start=(k == 0), stop=(k == KC - 1)),
                     reads=[("sq", k % 2), ("cb",)], writes=[("ps", b)])
            T.op("act", "activation", dict(out=nrs[:, :], in_=ps[b][:, :], func=AF.Ln, scale=1.0 / D, bias=epsc),
                 reads=[("cv",)], writes=[("ps", b), ("rstd",)])
            T.op("act", "activation", dict(out=nrs[:, :], in_=nrs[:, :], func=AF.Exp, scale=-0.5),
                 writes=[("rstd",)])
            for k in range(KC):
                T.op("dve", "scalar_tensor_tensor",
                     dict(out=hT[:, k, ts(j)], in0=xT[:, k, ts(j)], scalar=cv[:, CV_NORM + l * 8 + k:CV_NORM + l * 8 + k + 1],
                          in1=nrs[:, :], op0=ALU.mult, op1=ALU.mult),
                     reads=[("x", k, j), ("rstd",), ("cv",)], writes=[("h", k, j)])

        def emit_proj(slot, j, b, src, srckeys):
            T.grp("pe", [("matmul", dict(out=ps[b][:, :], lhsT=ring[slot][:, k, :], rhs=src[:, k, ts(j)],
                                         start=(k == 0), stop=(k == KC - 1))) for k in range(KC)],
                  reads=[("w", slot)] + srckeys, writes=[("ps", b)])

        def hkeys(j):
            return [("h", k, j) for k in range(KC)]

        def emit_outproj(l, j):
            T.phase = "outproj"
            srck = [key for k in range(KC) for key in ogkeys(k, j)]
            for co in range(KC):
                slot = wuse(_layer_chunk_base(l) + (22 if l % 2 == 0 else 32) + co)
                b = nb()
                emit_proj(slot, j, b, ogT, srck)
                wrel(slot)
                T.op("dve", "tensor_tensor", dict(out=xT[:, co, ts(j)], in0=ps[b][:, :], in1=xT[:, co, ts(j)], op=ALU.add),
                     reads=[], writes=[("ps", b), ("x", co, j)])

        def emit_qk_stageA(item):
            T.phase = "qkA"
            slot, j, i = item["slot"], item["j"], item["i"]
            if slot is None:
                slot = item["slotref"][0] = wuse(item["cid"]) if item["slotref"][0] is None else item["slotref"][0]
            b = 0
            item["b"] = b
            emit_proj(slot, j, b, hT, hkeys(j))
            if j == NT - 1:
                wrel(slot)
            T.op("dve", "tensor_copy", dict(out=qg[i % 2][:, :], in_=ps[b][:, :]),
                 writes=[("ps", b), ("qg", i % 2)])
            T.op(SQ_ENG, "tensor_tensor", dict(out=sq[i % 2][:, :], in0=qg[i % 2][:, :], in1=qg[i % 2][:, :], op=ALU.mult),
                 reads=[("qg", i % 2)], writes=[("sq", i % 2)])
            for qq, (dst0, src0) in enumerate(((0, 32), (32, 0), (64, 96), (96, 64))):
                T.dma("sp", dict(out=qrot[i % 2][dst0:dst0 + 32, :], in_=qg[i % 2][src0:src0 + 32, :]),
                      reads=[("qg", i % 2)], writes=[("qrot", i % 2, qq)])

        def emit_qk_stageB(item):
            T.phase = "qkB"
            j, i, b = item["j"], item["i"], item["b"]
            gcol = item["gcol"]
            b2 = 1
            T.op("pe", "matmul", dict(out=ps[b2][:, :], lhsT=bo, rhs=sq[i % 2][:, :], start=True, stop=True),
                 reads=[("sq", i % 2), ("cb",)], writes=[("ps", b2)])
            rs = rstd2[i % 2]
            rkey = ("rstd",)
            T.op("act", "activation", dict(out=rs[:, :], in_=ps[b2][:, :], func=AF.Ln, scale=1.0 / 64, bias=epsc),
                 reads=[("cv",)], writes=[("ps", b2), rkey])
            T.op("act", "activation", dict(out=rs[:, :], in_=rs[:, :], func=AF.Exp, scale=-0.5),
                 writes=[rkey])
            T.op("dve", "scalar_tensor_tensor",
                 dict(out=t1[:, :], in0=qg[i % 2][:, :], scalar=gcol, in1=cs[:, 0, ts(j)], op0=ALU.mult, op1=ALU.mult),
                 reads=[("cv",), ("cs",), ("qg", i % 2)], writes=[("t1",)])
            T.op("dve", "scalar_tensor_tensor",
                 dict(out=t2[:, :], in0=qrot[i % 2][:, :], scalar=item["gpcol"], in1=cs[:, 1, ts(j)], op0=ALU.mult, op1=ALU.mult),
                 reads=[("cv",), ("cs",)] + [("qrot", i % 2, qq) for qq in range(4)], writes=[("t2",)])
            T.op(ADD_ENG, "tensor_tensor", dict(out=t1[:, :], in0=t1[:, :], in1=t2[:, :], op=ALU.add),
                 reads=[("t2",)], writes=[("t1",)])
            T.op(FIN_ENG, "tensor_tensor", dict(out=item["dst"], in0=t1[:, :], in1=rs[:, :], op=ALU.mult),
                 reads=[("t1",), rkey], writes=[item["dkey"]])

        def emit_S(a, g, m):
            T.phase = "S"
            bq = g % 2
            slot = m % NP
            lo, hi = max(m - 1, 0), min(m + 1, NBLK - 1)
            qs, qe = lo * 128, (hi + 1) * 128
            off = (lo - (m - 1)) * 128
            w = qe - qs
            qkeys_j = sorted(set([qs // TW, (qe - 1) // TW]))
            banks = [4, 5, 6, 7]
            for hh in range(4):
                cc, par = hh // 2, hh % 2
                rows = slice(par * 64, (par + 1) * 64)
                b = banks[hh]
                T.op("pe", "matmul", dict(out=ps[b][:, off:off + w], lhsT=kr[bq][rows, m * 128:(m + 1) * 128],
                                          rhs=qr[bq][rows, cc, qs:qe], start=True, stop=False),
                     reads=[("k", bq, m // 4)] + [("q", bq, cc, jj) for jj in qkeys_j], writes=[("ps", b)])
            for hh in range(4):
                b = banks[hh]
                pieces = []
                if m >= 1:
                    pieces.append((ps[b][:, 0:128], mb2[:, 0, :]))
                if m <= NBLK - 2:
                    pieces.append((ps[b][:, 256:384], mb2[:, 1, :]))
                T.grp("pe", [("matmul", dict(out=o_, lhsT=ident, rhs=r_, start=False, stop=(ii == len(pieces) - 1)))
                             for ii, (o_, r_) in enumerate(pieces)],
                      reads=[("cb",)], writes=[("ps", b)])
            s4 = psall[:, 4 * TW:8 * TW].rearrange("p (h c) -> p h c", h=4)
            T.op("act", "activation", dict(out=Pt[slot][:, :, off:off + w], in_=s4[:, :, off:off + w],
                                           func=AF.Exp, scale=0.125),
                 writes=[("ps", 4), ("ps", 5), ("ps", 6), ("ps", 7), ("p", slot)])

        def emit_PV(a, g, n):
            T.phase = "PV"
            b = 2 + (n % 2)
            mms = [mm for mm in (n - 1, n, n + 1) if 0 <= mm < NBLK]
            instrs = []
            for kind in range(2):
                for idx, mm in enumerate(mms):
                    cbk = n - mm + 1
                    for par in range(2):
                        rows = slice(par * 64, (par + 1) * 64)
                        outv = ps[b][rows, kind * 256:(kind + 1) * 256].rearrange("p (a q) -> p a q", a=2)
                        rhs = Pt[mm % NP][:, par::2, cbk * 128:(cbk + 1) * 128]
                        lhsT = Vt[:, mm * 256 + g * 64: mm * 256 + g * 64 + 64] if kind == 0 else cb[:, CB_ONES:CB_ONES + 64]
                        instrs.append(("matmul", dict(out=outv, lhsT=lhsT, rhs=rhs, start=(idx == 0),
                                                      stop=(idx == len(mms) - 1), tile_position=(0, par * 64))))
            T.grp("pe", instrs, reads=[("p", mm % NP) for mm in mms] + [("v", mm // 2) for mm in mms] + [("cb",)],
                  writes=[("ps", b)])
            sc = (a * 4 + g) * 2
            for jj in range(2):
                T.op("act", "activation",
                     dict(out=rdn[:, jj * 128:(jj + 1) * 128], in_=ps[b][:, 256 + jj * 128:256 + (jj + 1) * 128],
                          func=AF.Ln, bias=esink[:, sc + jj:sc + jj + 1]),
                     reads=[("esink",)], writes=[("ps", b), ("rdn",)])
            T.op("act", "activation", dict(out=rdn[:, :], in_=rdn[:, :], func=AF.Exp, scale=-1.0), writes=[("rdn",)])
            T.op("dve", "tensor_tensor", dict(out=d1[:, :], in0=ps[b][:, 0:256], in1=rdn[:, :], op=ALU.mult),
                 reads=[("rdn",)], writes=[("ps", b), ("d1",)])
            ogv = ogT[:, 2 * g:2 * g + 2, n * 128:(n + 1) * 128]
            T.op("pool", "tensor_tensor", dict(out=ogv, in0=d1[:, :].rearrange("p (a q) -> p a q", a=2), in1=ogv, op=ALU.mult),
                 reads=[("d1",)], writes=[("og", 2 * g, n), ("og", 2 * g + 1, n)])

        def interleave(la, lb):
            ia = ib = 0
            while ia < len(la) or ib < len(lb):
                fa = ia / len(la) if la else 2.0
                fb = ib / len(lb) if lb else 2.0
                if ib >= len(lb) or (ia < len(la) and fa <= fb):
                    la[ia]()
                    ia += 1
                else:
                    lb[ib]()
                    ib += 1

        def emit_attn_layer(l):
            a = l // 2
            wbase = _layer_chunk_base(l)

            def proj_work(g):
                th = []
                bq = g % 2

                def gate_tile(c, j, ref):
                    def f():
                        T.phase = "gate"
                        if ref[0] is None:
                            ref[0] = wuse(wbase + c)
                        b = nb("hi")
                        emit_proj(ref[0], j, b, hT, hkeys(j))
                        if j == NT - 1:
                            wrel(ref[0])
                        T.op("act", "activation", dict(out=ogT[:, c, ts(j)], in_=ps[b][:, :], func=AF.Silu),
                             writes=[("ps", b)] + ogkeys(c, j))
                    return f
                gate_th = []
                for c in (range(KC) if g == 0 else ()):
                    ref = [None]
                    for j in range(NT):
                        gate_th.append(gate_tile(c, j, ref))
                if g == 0:
                    vref = [None, None]

                    def v_tile(tb2):
                        def f():
                            T.phase = "V"
                            if vref[0] is None:
                                vref[0] = wuse(wbase + 8)
                                vref[1] = wuse(wbase + 9)
                            b = nb()
                            instrs = []
                            for t in range(2):
                                tb = 2 * tb2 + t
                                for vc in range(2):
                                    for k in range(KC):
                                        instrs.append(("matmul", dict(out=ps[b][:, t * 256 + vc * 128:t * 256 + (vc + 1) * 128],
                                                                      lhsT=hT[:, k, tb * 128:(tb + 1) * 128], rhs=ring[vref[vc]][:, k, :],
                                                                      start=(k == 0), stop=(k == KC - 1))))
                            T.grp("pe", instrs, reads=[("w", vref[0]), ("w", vref[1])] + hkeys(tb2 // 2), writes=[("ps", b)])
                            if tb2 == 7:
                                wrel(vref[0])
                                wrel(vref[1])
                            T.op("act", "activation", dict(out=Vt[:, tb2 * 512:(tb2 + 1) * 512], in_=ps[b][:, :], func=AF.Copy),
                                 writes=[("ps", b), ("v", tb2)])
                        return f
                    for tb2 in range(8):
                        th.append(v_tile(tb2))
                items = []
                for kind, cc in (("q", 0), ("q", 1), ("k", 0)):
                    ref = [None]
                    for j in range(NT):
                        if kind == "q":
                            dst, dkey = qr[bq][:, cc, ts(j)], ("q", bq, cc, j)
                            gcol = cv[:, CV_QG + a:CV_QG + a + 1]
                            gpcol = cv[:, CV_QGP + a:CV_QGP + a + 1]
                        else:
                            dst, dkey = kr[bq][:, ts(j)], ("k", bq, j)
                            gcol = cv[:, CV_KG + a:CV_KG + a + 1]
                            gpcol = cv[:, CV_KGP + a:CV_KGP + a + 1]
                        items.append(dict(kind=kind, slot=None, slotref=ref, j=j, i=len(items), dst=dst, dkey=dkey, gcol=gcol, gpcol=gpcol,
                                          cid=wbase + (10 + 2 * g + cc if kind == "q" else 18 + g)))

                def qk_step(i):
                    def f():
                        if i < len(items):
                            emit_qk_stageA(items[i])
                        if i >= 1:
                            emit_qk_stageB(items[i - 1])
                    return f
                qk_th = [qk_step(i) for i in range(len(items) + 1)]
                if g == 0:
                    qi = 0
                    for c in range(KC):
                        th.extend(gate_th[4 * c:4 * c + 4])
                        nq = 2 if c < 6 else (1 if c == 6 else 0)
                        th.extend(qk_th[qi:qi + nq])
                        qi += nq
                    assert qi == len(qk_th)
                else:
                    th.extend(qk_th)
                return th

            def attn_work(g):
                th = []

                def step_f(step):
                    def f():
                        if step < NBLK:
                            emit_S(a, g, step)
                        if step >= 2:
                            emit_PV(a, g, step - 2)
                    return f
                for step in range(NBLK + 2):
                    th.append(step_f(step))
                return th

            interleave(proj_work(0), [])
            for g in range(4):
                interleave(attn_work(g), proj_work(g + 1) if g + 1 < 4 else [])

        def emit_conv_layer(l):
            bi = l // 2
            T.phase = "conv"
            for c in range(KC):
                zi = c % 2
                z = zb[zi]
                cb0 = _layer_chunk_base(l)
                slots = [wuse(cb0 + 8 + c), wuse(cb0 + 16 + c), wuse(cb0 + 24 + c), wuse(cb0 + c)]
                wc = CV_CONV + (bi * 8 + c) * 3

                def conv(j):
                    rk = [("z", zi, jj) for jj in (j - 1, j, j + 1) if 0 <= jj < NT] + [("zpad", zi)]
                    T.op("act", "activation", dict(out=a0[:, :], in_=z[:, j * TW:j * TW + TW], func=AF.Copy, scale=cv[:, wc:wc + 1]),
                         reads=rk + [("cv",)], writes=[("a0",)])
                    T.op("dve", "scalar_tensor_tensor",
                         dict(out=a1[:, :], in0=z[:, 1 + j * TW:1 + j * TW + TW], scalar=cv[:, wc + 1:wc + 2], in1=a0[:, :],
                              op0=ALU.mult, op1=ALU.add),
                         reads=rk + [("a0",), ("cv",)], writes=[("a1",)])
                    T.op("dve", "scalar_tensor_tensor",
                         dict(out=a0[:, :], in0=z[:, 2 + j * TW:2 + j * TW + TW], scalar=cv[:, wc + 2:wc + 3], in1=a1[:, :],
                              op0=ALU.mult, op1=ALU.add),
                         reads=rk + [("a1",), ("cv",)], writes=[("a0",)])
                    T.op("pool", "tensor_tensor", dict(out=ogT[:, c, ts(j)], in0=a0[:, :], in1=bsb[j % 2][:, :], op=ALU.mult),
                         reads=[("a0",), ("bs", j % 2)], writes=ogkeys(c, j))

                for j in range(NT):
                    bcg, bu, bgt, bbg = nb(), nb(), nb(), nb()
                    for si, bb in enumerate((bcg, bu, bgt, bbg)):
                        emit_proj(slots[si], j, bb, hT, hkeys(j))
                        if j == NT - 1:
                            wrel(slots[si])
                    T.op("act", "activation", dict(out=u_sb[:, :], in_=ps[bu][:, :], func=AF.Copy),
                         writes=[("ps", bu), ("u",)])
                    T.op("act", "activation", dict(out=sgb[:, :], in_=ps[bgt][:, :], func=AF.Silu),
                         writes=[("ps", bgt), ("sg",)])
                    T.op("dve", "tensor_tensor", dict(out=z[:, 1 + j * TW:1 + (j + 1) * TW], in0=ps[bcg][:, :], in1=u_sb[:, :], op=ALU.mult),
                         reads=[("u",)], writes=[("ps", bcg), ("z", zi, j)])
                    T.op("dve", "tensor_tensor", dict(out=bsb[j % 2][:, :], in0=ps[bbg][:, :], in1=sgb[:, :], op=ALU.mult),
                         reads=[("sg",)], writes=[("ps", bbg), ("bs", j % 2)])
                    if j >= 1:
                        conv(j - 1)
                conv(NT - 1)

        cur_scratch = None

        def set_scratch(kind):
            nonlocal cur_scratch
            if cur_scratch == kind:
                return
            old = A_KEYS if cur_scratch == "A" else (B_KEYS if cur_scratch == "B" else [])
            new = A_KEYS if kind == "A" else B_KEYS
            retire(old, new)
            cur_scratch = kind
            ntmp["sq"], ntmp["rs"] = (sq, rstd) if kind == "A" else (sq_b, rstd_b)
            if kind == "B":
                for i in range(2):
                    T.op("pool", "memset", dict(ap=zb[i][:, 0:1], constant=0.0), writes=[("zpad", i)])
                    T.op("pool", "memset", dict(ap=zb[i][:, ZW - 1:ZW], constant=0.0), writes=[("zpad", i)])

        def emit_all():
            nonlocal cur_scratch
            cur_scratch = None
            state.clear()
            state.update({"bank": 0, "wuse": 0, "wissued": 0, "low": 0, "live": {}, "done": set()})
            for s_ in range(nseq):
                for j in range(NT):
                    for k in range(KC):
                        T.dma("sp", dict(out=xT[:, k, ts(j)], in_=xin[s_, k * 128:(k + 1) * 128, ts(j)]),
                              writes=[("x", k, j)])
                for li, l in enumerate(layers):
                    if li == 0:
                        if cur_scratch is None:
                            set_scratch("A" if l % 2 == 0 else "B")
                        for j in range(NT):
                            emit_norm(l, j)
                    set_scratch("A" if l % 2 == 0 else "B")
                    if l % 2 == 0:
                        emit_attn_layer(l)
                    else:
                        emit_conv_layer(l)
                    nxt = layers[li + 1] if li + 1 < len(layers) else None
                    for step in range(NT + 1):
                        if step < NT:
                            emit_outproj(l, step)
                        if step >= 1:
                            j = step - 1
                            if nxt is not None:
                                emit_norm(nxt, j)
                            else:
                                for k in range(KC):
                                    T.dma("sp", dict(out=yout[s_, k * 128:(k + 1) * 128, ts(j)], in_=xT[:, k, ts(j)]),
                                          reads=[("x", k, j)])

        T.dry = True
        emit_all()
        T.dry = False
        emit_all()
        if debug:
            T.dma("sp", dict(out=dbg_h[:, :, :], in_=hT[:, :, :]), reads=[("h", k, j) for k in range(KC) for j in range(NT)])
            T.dma("sp", dict(out=dbg_og[:, :, :], in_=ogT[:, :, :]), reads=[("og", k, n) for k in range(KC) for n in range(NBLK)])
        T.finish("sp")
        T.finish("pool")
        T.finish("act")
        T.finish("dve")
        T.finish("pe")
    return nc


def _chunk(wcols):
    return np.ascontiguousarray(wcols.reshape(KC, 128, 128).transpose(1, 0, 2))


def _prep_weights(a_w_in, a_w_out, b_w_in, b_w_out):
    out = np.empty((NCH, 128, KC, 128), np.float32)
    for l in range(DEPTH):
        base = _layer_chunk_base(l)
        s = l // 2
        if l % 2 == 0:
            w = a_w_in[s]
            for c in range(8):
                out[base + c] = _chunk(w[:, 1536 + c * 128:1536 + (c + 1) * 128])
            for vc in range(2):
                out[base + 8 + vc] = _chunk(w[:, 1280 + vc * 128:1280 + (vc + 1) * 128])
            for c in range(8):
                out[base + 10 + c] = _chunk(w[:, c * 128:(c + 1) * 128])
            for g in range(4):
                kg = w[:, 1024 + g * 64:1024 + (g + 1) * 64]
                out[base + 18 + g] = _chunk(np.concatenate([kg, kg], axis=1))
            for co in range(8):
                out[base + 22 + co] = _chunk(a_w_out[s][:, co * 128:(co + 1) * 128])
        else:
            w = b_w_in[s]
            for c in range(32):
                out[base + c] = _chunk(w[:, c * 128:(c + 1) * 128])
            for co in range(8):
                out[base + 32 + co] = _chunk(b_w_out[s][:, co * 128:(co + 1) * 128])
    return out


def _prep_consts(norm_g, a_q_norm, a_k_norm, a_sink, b_conv):
    cvec = np.zeros((128, NCV), np.float32)
    p = np.arange(128)
    for l in range(DEPTH):
        for k in range(KC):
            cvec[:, CV_NORM + l * 8 + k] = norm_g[l, k * 128:(k + 1) * 128]
    for a in range(2):
        cvec[:, CV_QG + a] = a_q_norm[a][p % 64]
        cvec[:, CV_KG + a] = a_k_norm[a][p % 64]
        cvec[:, CV_QGP + a] = a_q_norm[a][(p % 64 + 32) % 64]
        cvec[:, CV_KGP + a] = a_k_norm[a][(p % 64 + 32) % 64]
        for g in range(4):
            for j in range(2):
                cvec[:, CV_SINK + (a * 4 + g) * 2 + j] = a_sink[a][4 * g + 2 * j + (p >= 64)]
    for b in range(2):
        for c in range(KC):
            for t in range(3):
                cvec[:, CV_CONV + (b * 8 + c) * 3 + t] = b_conv[b, t, c * 128:(c + 1) * 128]
    cvec[:, CV_EPS] = EPS

    cbf = np.zeros((128, NCB), np.float32)
    cbf[:, CB_ONES:CB_ONES + 128] = 1.0
    cbf[:, CB_BO:CB_BO + 128] = (p[:, None] // 64 == p[None, :] // 64)
    rtm = np.zeros((128, 128), np.float32)
    for i in range(128):
        if i % 64 < 32:
            rtm[i + 32, i] = -1.0
        else:
            rtm[i - 32, i] = 1.0
    cbf[:, CB_RT:CB_RT + 128] = rtm
    cbf[:, CB_ID:CB_ID + 128] = np.eye(128, dtype=np.float32)
    NEG = -30000.0
    cbf[:, CB_MB:CB_MB + 128] = np.where(p[:, None] <= p[None, :], 0.0, NEG)
    cbf[:, CB_MB + 128:CB_MB + 256] = np.where(p[None, :] <= p[:, None], 0.0, NEG)

    inv_freq = 10000.0 ** (-np.arange(0, 64, 2, dtype=np.float64) / 64)
    ang = np.arange(S, dtype=np.float64)[:, None] * inv_freq[None, :]
    cosT = np.cos(ang).astype(np.float32).T
    sinT = np.sin(ang).astype(np.float32).T
    rope = np.empty((128, 2, S), np.float32)
    rope[:, 0, :] = cosT[p % 32]
    rope[:, 1, :] = sinT[p % 32] * np.where(p % 64 < 32, -1.0, 1.0)[:, None]
    return cvec, cbf, rope


def _run(inputs, layers, seqs=None, ncores=NCORES, debug=False):
    x = np.asarray(inputs["x"], np.float32)
    wts = _prep_weights(np.asarray(inputs["a_w_in"], np.float32), np.asarray(inputs["a_w_out"], np.float32),
                        np.asarray(inputs["b_w_in"], np.float32), np.asarray(inputs["b_w_out"], np.float32))
    cvec, cbf, rope = _prep_consts(np.asarray(inputs["norm_g"], np.float32), np.asarray(inputs["a_q_norm"], np.float32),
                                   np.asarray(inputs["a_k_norm"], np.float32), np.asarray(inputs["a_sink"], np.float32),
                                   np.asarray(inputs["b_conv"], np.float32))
    nseq = SEQ_PER_CORE if seqs is None else seqs
    xT = np.ascontiguousarray(x.transpose(0, 2, 1))
    import time as _t
    _t0 = _t.time()
    nc = build_program(list(layers), nseq=nseq, debug=debug)
    print("[kernel] build %.1fs" % (_t.time() - _t0), flush=True)
    in_maps = []
    for c in range(ncores):
        in_maps.append({"xin": np.ascontiguousarray(xT[c * nseq:(c + 1) * nseq]), "wts": wts, "cvec": cvec, "cbf": cbf, "rope": rope})
    res = run_bass_kernel_spmd(nc, in_maps, core_ids=list(range(ncores)))
    if debug:
        return res
    outT = np.concatenate([r["yout"] for r in res.results], axis=0)
    return np.ascontiguousarray(outT.transpose(0, 2, 1)).astype(np.float32)


def kernel(x, norm_g, a_w_in, a_q_norm, a_k_norm, a_sink, a_w_out, b_w_in, b_conv, b_w_out):
    inputs = dict(x=x, norm_g=norm_g, a_w_in=a_w_in, a_q_norm=a_q_norm, a_k_norm=a_k_norm, a_sink=a_sink,
                  a_w_out=a_w_out, b_w_in=b_w_in, b_conv=b_conv, b_w_out=b_w_out)
    return _run(inputs, layers=list(range(DEPTH)))
```
